# Optimizing a Trainium2 kernel written in Bass

```python
import math
import jax, jax.numpy as jnp
from jax import lax
import numpy as np

D_MODEL = 2048
BATCH = 1
SEQ = 16384
DEPTH = 2

GRID_W = 64
CTX_LEN = 256
ROPE_BASE = 10000.0
EPS = 1e-6
Q_BLOCK = 128

H_A = 8
DA = 64
DV_A = 2 * DA
W_A = H_A * DV_A

H_M = 16
P_M = 64
D_INNER = H_M * P_M
N_STATE = 128
G_M = 2
D_CONV = 5
CHUNK = 128
CONV_CH = D_INNER + 2 * G_M * N_STATE

H_C = 8
D_NOPE = 128
D_ROPE = 64
D_VC = 128
Q_LORA = 512
KV_LORA = 256
W_C = H_C * D_VC
MLA_SCALE = (D_NOPE + D_ROPE) ** -0.5

D_FF = 4 * D_MODEL
N_BRANCH = 3
N_MOD = 6

IN_SIZES = (2 * H_A * DA, 2 * H_A * DA, W_A, D_INNER, CONV_CH, 2 * H_M,
            Q_LORA, KV_LORA, D_ROPE, N_BRANCH * D_MODEL)
P_IN = sum(IN_SIZES)

kernel_name = "hybrid_diff_ssd_mla_prefix_trunk"


def rmsnorm(x, g):
    xf = x.astype(jnp.float32)
    y = xf * lax.rsqrt(jnp.mean(xf * xf, axis=-1, keepdims=True) + EPS)
    return y.astype(x.dtype) * g


def modulate(s, g, shift, scale):
    return rmsnorm(s, g) * (1 + scale) + shift


def modulation(cond, w_ada, b_ada):
    m = jax.nn.silu(cond) @ w_ada + b_ada
    return jnp.split(m, N_MOD, axis=-1)


def split_in(p):
    offs, acc = [], 0
    for sz in IN_SIZES[:-1]:
        acc += sz
        offs.append(acc)
    return jnp.split(p, offs, axis=-1)


def axial_rope_tables(rows, dim, dtype):
    row = jnp.repeat(jnp.arange(rows), GRID_W).astype(jnp.float32)
    col = jnp.tile(jnp.arange(GRID_W), rows).astype(jnp.float32)
    n_freq = dim // 4
    inv = 1.0 / (ROPE_BASE ** (jnp.arange(n_freq, dtype=jnp.float32) / n_freq))
    ang = jnp.concatenate([row[:, None] * inv, col[:, None] * inv], axis=-1)
    return jnp.cos(ang).astype(dtype), jnp.sin(ang).astype(dtype)


def apply_rope(x, rope):
    cos, sin = rope
    half = x.shape[-1] // 2
    x1, x2 = x[..., :half], x[..., half:]
    c = cos[None, :, None, :]
    s = sin[None, :, None, :]
    return jnp.concatenate([x1 * c - x2 * s, x2 * c + x1 * s], axis=-1)


def softmax_attn(q, k, v, scale):
    s = jnp.einsum('bqhd,bthd->bhqt', q, k).astype(jnp.float32) * scale
    p = jax.nn.softmax(s, axis=-1).astype(v.dtype)
    return jnp.einsum('bhqt,bthe->bqhe', p, v)


def diff_attn_core(q1, q2, k1, k2, v, lam):
    scale = DA ** -0.5
    return softmax_attn(q1, k1, v, scale) - lam.astype(v.dtype) * softmax_attn(q2, k2, v, scale)


def over_query_blocks(fn, qs):
    b, s = qs[0].shape[:2]
    nb = s // Q_BLOCK
    blk = tuple(jnp.moveaxis(q.reshape((b, nb, Q_BLOCK) + q.shape[2:]), 1, 0) for q in qs)
    out = lax.map(lambda qb: fn(*qb), blk)
    return jnp.moveaxis(out, 0, 1).reshape((b, s) + out.shape[3:])


def dwconv(u, w, bias):
    ch = u.shape[-1]
    y = lax.conv_general_dilated(u, w.astype(u.dtype)[:, None, :], window_strides=(1,),
                                 padding=[(D_CONV // 2, D_CONV // 2)],
                                 dimension_numbers=('NWC', 'WIO', 'NWC'),
                                 feature_group_count=ch)
    return y + bias


def ssd_scan(x, dt, A, Bm, Cm, h0, need_y):
    b, s, h, p = x.shape
    nc = s // CHUNK
    xd = (x.astype(jnp.float32) * dt[..., None]).reshape(b, nc, CHUNK, h, p)
    Bc = Bm.astype(jnp.float32).reshape(b, nc, CHUNK, h, N_STATE)
    Cc = Cm.astype(jnp.float32).reshape(b, nc, CHUNK, h, N_STATE)
    a = jnp.moveaxis((dt * A).reshape(b, nc, CHUNK, h), 3, 1)
    a_cum = jnp.cumsum(a, axis=-1)
    decay_to_end = jnp.exp(a_cum[..., -1:] - a_cum)
    states = jnp.einsum('bclhn,bhcl,bclhp->bchpn', Bc, decay_to_end, xd)
    chunk_decay = jnp.exp(a_cum[..., -1])

    def step(hc, inp):
        dec, st = inp
        return dec[..., None, None] * hc + st, hc

    h_final, prev = lax.scan(step, h0.astype(jnp.float32),
                             (jnp.moveaxis(chunk_decay, 2, 0), jnp.moveaxis(states, 1, 0)))
    if not need_y:
        return None, h_final
    prev = jnp.moveaxis(prev, 0, 1)
    tril = jnp.tril(jnp.ones((CHUNK, CHUNK), dtype=bool))
    seg = jnp.where(tril, a_cum[..., :, None] - a_cum[..., None, :], -jnp.inf)
    cb = jnp.einsum('bclhn,bcshn->bhcls', Cc, Bc) * jnp.exp(seg)
    y_diag = jnp.einsum('bhcls,bcshp->bclhp', cb, xd)
    y_off = jnp.einsum('bclhn,bchpn,bhcl->bclhp', Cc, prev, jnp.exp(a_cum))
    return (y_diag + y_off).reshape(b, s, h, p).astype(x.dtype), h_final


def bidir_ssd(xs, Bm, Cm, dt_f, dt_b, A_f, A_b, h0_f, h0_b, need_y):
    y_f, hf = ssd_scan(xs, dt_f, A_f, Bm, Cm, h0_f, need_y)
    flip = lambda t: jnp.flip(t, axis=1)
    y_b, hb = ssd_scan(flip(xs), flip(dt_b), A_b, flip(Bm), flip(Cm), h0_b, need_y)
    y = y_f + flip(y_b) if need_y else None
    return y, hf, hb


def ssm_inputs(xbc_raw, dt_raw, conv_w, conv_b, dt_bias):
    b, s = xbc_raw.shape[:2]
    xbc = jax.nn.silu(dwconv(xbc_raw, conv_w, conv_b))
    xs = xbc[..., :D_INNER].reshape(b, s, H_M, P_M)
    grp = lambda t: jnp.repeat(t.reshape(b, s, G_M, N_STATE), H_M // G_M, axis=2)
    Bm = grp(xbc[..., D_INNER:D_INNER + G_M * N_STATE])
    Cm = grp(xbc[..., D_INNER + G_M * N_STATE:])
    dt32 = dt_raw.astype(jnp.float32)
    dtb32 = dt_bias.astype(jnp.float32)
    dt_f = jax.nn.softplus(dt32[..., :H_M] + dtb32[0])
    dt_b = jax.nn.softplus(dt32[..., H_M:] + dtb32[1])
    return xs, Bm, Cm, dt_f, dt_b


def ssm_out(y, xs, z, d_skip, norm_g, w_o):
    b, s = y.shape[:2]
    y = (y + xs * d_skip[:, None]).reshape(b, s, D_INNER) * jax.nn.silu(z)
    y = rmsnorm(y.reshape(b, s, G_M, D_INNER // G_M), norm_g.reshape(G_M, D_INNER // G_M))
    return y.reshape(b, s, D_INNER) @ w_o


def diff_heads(qd, kd, vd, rope):
    b, s = qd.shape[:2]
    q = qd.reshape(b, s, 2, H_A, DA)
    k = kd.reshape(b, s, 2, H_A, DA)
    v = vd.reshape(b, s, H_A, DV_A)
    q1, q2, k1, k2 = q[:, :, 0], q[:, :, 1], k[:, :, 0], k[:, :, 1]
    if rope is not None:
        q1, q2, k1, k2 = (apply_rope(t, rope) for t in (q1, q2, k1, k2))
    return q1, q2, k1, k2, v


def diff_out(o, subln_g, lam_init, w_o):
    b, s = o.shape[:2]
    o = rmsnorm(o, subln_g) * (1.0 - lam_init)
    return o.reshape(b, s, W_A) @ w_o


def mla_q(cq, q_norm_g, w_uq, rope):
    b, s = cq.shape[:2]
    q = (rmsnorm(cq, q_norm_g) @ w_uq).reshape(b, s, H_C, D_NOPE + D_ROPE)
    q_nope, q_rope = q[..., :D_NOPE], q[..., D_NOPE:]
    if rope is not None:
        q_rope = apply_rope(q_rope, rope)
    return jnp.concatenate([q_nope, q_rope], axis=-1)


def mla_kv(ckv, kr, kv_norm_g, w_ukv, rope):
    b, s = ckv.shape[:2]
    kv = (rmsnorm(ckv, kv_norm_g) @ w_ukv).reshape(b, s, H_C, D_NOPE + D_VC)
    k_nope, v = kv[..., :D_NOPE], kv[..., D_NOPE:]
    kr = kr[:, :, None, :]
    if rope is not None:
        kr = apply_rope(kr, rope)
    k = jnp.concatenate([k_nope, jnp.broadcast_to(kr, (b, s, H_C, D_ROPE))], axis=-1)
    return k, v


def merge_and_mlp(s, gates_raw, ys, gate1, shift2, scale2, gate2, norm2_g, w_out, w1, w2):
    g = jnp.split(jax.nn.sigmoid(gates_raw), N_BRANCH, axis=-1)
    merged = g[0] * ys[0] + g[1] * ys[1] + g[2] * ys[2]
    s = s + gate1 * (merged @ w_out)
    h = modulate(s, norm2_g, shift2, scale2)
    return s + gate2 * (jnp.square(jax.nn.relu(h @ w1)) @ w2)


def setup_inputs(seed: int = 0) -> dict:
    key = jax.random.key(seed)
    k = jax.random.split(key, 32)
    L, D = DEPTH, D_MODEL
    f32 = jnp.float32
    nrm = lambda kk, shape, scale: jax.random.normal(kk, shape, f32) * scale
    gain = lambda kk, shape: 1.0 + nrm(kk, shape, 0.02)
    dt0 = jnp.exp(jax.random.uniform(k[14], (L, 2, H_M), f32, math.log(1e-3), math.log(1e-1)))
    dt_bias = dt0 + jnp.log(-jnp.expm1(-dt0))
    a_log = jnp.log(jax.random.uniform(k[15], (L, 2, H_M), f32, 1.0, 16.0))
    return {
        "x": nrm(k[0], (BATCH, SEQ, D), 1.0),
        "c": nrm(k[1], (BATCH, D), 1.0),
        "ctx": nrm(k[2], (BATCH, CTX_LEN, D), 1.0),
        "c_ctx": nrm(k[3], (D,), 1.0),
        "norm1_g": gain(k[4], (L, D)),
        "norm2_g": gain(k[5], (L, D)),
        "w_ada": nrm(k[6], (L, D, N_MOD * D), D ** -0.5),
        "b_ada": nrm(k[7], (L, N_MOD * D), 0.02),
        "w_in": nrm(k[8], (L, D, P_IN), D ** -0.5),
        "lam_qk": nrm(k[9], (L, 4, DA), 0.1),
        "subln_g": gain(k[10], (L, DV_A)),
        "w_o_diff": nrm(k[11], (L, W_A, D), W_A ** -0.5),
        "conv_w": nrm(k[12], (L, D_CONV, CONV_CH), D_CONV ** -0.5),
        "conv_b": nrm(k[13], (L, CONV_CH), 0.02),
        "a_log": a_log,
        "dt_bias": dt_bias,
        "d_skip": gain(k[16], (L, H_M)),
        "ssm_norm_g": gain(k[17], (L, D_INNER)),
        "w_o_ssm": nrm(k[18], (L, D_INNER, D), D_INNER ** -0.5),
        "q_norm_g": gain(k[19], (L, Q_LORA)),
        "w_uq": nrm(k[20], (L, Q_LORA, H_C * (D_NOPE + D_ROPE)), Q_LORA ** -0.5),
        "kv_norm_g": gain(k[21], (L, KV_LORA)),
        "w_ukv": nrm(k[22], (L, KV_LORA, H_C * (D_NOPE + D_VC)), KV_LORA ** -0.5),
        "w_o_mla": nrm(k[23], (L, W_C, D), W_C ** -0.5),
        "w_out": nrm(k[24], (L, D, D), D ** -0.5),
        "w_mlp1": nrm(k[25], (L, D, D_FF), D ** -0.5),
        "w_mlp2": nrm(k[26], (L, D_FF, D), D_FF ** -0.5),
        "final_norm_g": gain(k[27], (D,)),
    }


def reference(x, c, ctx, c_ctx, norm1_g, norm2_g, w_ada, b_ada, w_in, lam_qk, subln_g,
              w_o_diff, conv_w, conv_b, a_log, dt_bias, d_skip, ssm_norm_g, w_o_ssm,
              q_norm_g, w_uq, kv_norm_g, w_ukv, w_o_mla, w_out, w_mlp1, w_mlp2,
              final_norm_g):
    b, n, _ = x.shape
    rows = n // GRID_W
    rope_a = axial_rope_tables(rows, DA, x.dtype)
    rope_r = axial_rope_tables(rows, D_ROPE, x.dtype)
    lat, cx = x, ctx
    for l in range(DEPTH):
        last = l == DEPTH - 1
        lam_init = 0.8 - 0.6 * math.exp(-0.3 * l)
        lq = lam_qk[l].astype(jnp.float32)
        lam = jnp.exp(jnp.sum(lq[0] * lq[1])) - jnp.exp(jnp.sum(lq[2] * lq[3])) + lam_init
        A = -jnp.exp(a_log[l].astype(jnp.float32))

        mx = [m[:, None, :] for m in modulation(c, w_ada[l], b_ada[l])]
        mc = modulation(c_ctx, w_ada[l], b_ada[l])

        pc = split_in(modulate(cx, norm1_g[l], mc[0], mc[1]) @ w_in[l])
        px = split_in(modulate(lat, norm1_g[l], mx[0], mx[1]) @ w_in[l])

        cq1, cq2, ck1, ck2, cv = diff_heads(pc[0], pc[1], pc[2], None)
        cmk, cmv = mla_kv(pc[7], pc[8], kv_norm_g[l], w_ukv[l], None)
        cxs, cB, cC, cdtf, cdtb = ssm_inputs(pc[4], pc[5], conv_w[l], conv_b[l], dt_bias[l])
        h0 = jnp.zeros((b, H_M, P_M, N_STATE), jnp.float32)
        cy, hf, hb = bidir_ssd(cxs, cB, cC, cdtf, cdtb, A[0], A[1], h0, h0, not last)

        q1, q2, k1, k2, v = diff_heads(px[0], px[1], px[2], rope_a)
        K1 = jnp.concatenate([ck1, k1], axis=1)
        K2 = jnp.concatenate([ck2, k2], axis=1)
        V = jnp.concatenate([cv, v], axis=1)
        o_a = over_query_blocks(lambda a1, a2: diff_attn_core(a1, a2, K1, K2, V, lam), (q1, q2))

        mq = mla_q(px[6], q_norm_g[l], w_uq[l], rope_r)
        mk, mv = mla_kv(px[7], px[8], kv_norm_g[l], w_ukv[l], rope_r)
        MK = jnp.concatenate([cmk, mk], axis=1)
        MV = jnp.concatenate([cmv, mv], axis=1)
        o_c = over_query_blocks(lambda qq: softmax_attn(qq, MK, MV, MLA_SCALE), (mq,))

        xs, Bm, Cm, dtf, dtb = ssm_inputs(px[4], px[5], conv_w[l], conv_b[l], dt_bias[l])
        y, _, _ = bidir_ssd(xs, Bm, Cm, dtf, dtb, A[0], A[1], hf, hb, True)

        ys_x = (diff_out(o_a, subln_g[l], lam_init, w_o_diff[l]),
                ssm_out(y, xs, px[3], d_skip[l], ssm_norm_g[l], w_o_ssm[l]),
                o_c.reshape(b, n, W_C) @ w_o_mla[l])
        new_lat = merge_and_mlp(lat, px[9], ys_x, mx[2], mx[3], mx[4], mx[5],
                                norm2_g[l], w_out[l], w_mlp1[l], w_mlp2[l])

        if not last:
            t = cx.shape[1]
            cmq = mla_q(pc[6], q_norm_g[l], w_uq[l], None)
            ys_c = (diff_out(diff_attn_core(cq1, cq2, ck1, ck2, cv, lam), subln_g[l], lam_init, w_o_diff[l]),
                    ssm_out(cy, cxs, pc[3], d_skip[l], ssm_norm_g[l], w_o_ssm[l]),
                    softmax_attn(cmq, cmk, cmv, MLA_SCALE).reshape(b, t, W_C) @ w_o_mla[l])
            cx = merge_and_mlp(cx, pc[9], ys_c, mc[2], mc[3], mc[4], mc[5],
                               norm2_g[l], w_out[l], w_mlp1[l], w_mlp2[l])
        lat = new_lat
    return rmsnorm(lat, final_norm_g)
```

```python
import contextlib
import math
import numpy as np
import ml_dtypes
import concourse.bass as bass
import concourse.mybir as mybir
from concourse.bass_utils import run_bass_kernel_spmd

F32 = mybir.dt.float32
BF16 = mybir.dt.bfloat16
AF = mybir.ActivationFunctionType
ALU = mybir.AluOpType
AX = mybir.AxisListType
NPBF = ml_dtypes.bfloat16

NCORE = 8
D = 2048
KC = 16
CTX = 256
GRID_W = 64
EPS = 1e-6
H_A, DA, DV_A, W_A = 8, 64, 128, 1024
H_M, P_M, D_INNER, N_STATE, G_M, D_CONV = 16, 64, 1024, 128, 2, 5
CONV_CH = D_INNER + 2 * G_M * N_STATE
H_C, D_NOPE, D_ROPE, D_VC, Q_LORA, KV_LORA, W_C = 8, 128, 64, 128, 512, 256, 1024
MLA_SCALE = (D_NOPE + D_ROPE) ** -0.5
D_FF = 4 * D
IN_SIZES = (2 * H_A * DA, 2 * H_A * DA, W_A, D_INNER, CONV_CH, 2 * H_M, Q_LORA, KV_LORA, D_ROPE, 3 * D)
P_IN = sum(IN_SIZES)
OFF = np.concatenate([[0], np.cumsum(IN_SIZES)]).tolist()
N_MISC = OFF[9] - OFF[3]
COLPERM = np.concatenate([np.arange(0, OFF[5]), np.arange(OFF[6], OFF[9]), np.arange(OFF[5], OFF[6])])
C_CQ, C_CKV, C_KR, C_DT = 5632, 6144, 6400, 6464
M_Z, M_XBC, M_CQ, M_CKV, M_KR, M_DT = 0, 1024, 2560, 3072, 3328, 3392


class Buf:
    __slots__ = ("name", "lw", "rd", "dsem", "dcnt", "excl")

    def __init__(self, name, excl=False):
        self.name = name
        self.excl = excl
        self.lw = None
        self.rd = []
        self.dsem = None
        self.dcnt = 0


class KB:
    ENG = ("pe", "act", "dve", "pool", "sp")

    def __init__(self):
        self.nc = bass.Bass("TRN2", target_bir_lowering=False)
        self.prog = {e: [] for e in self.ENG}
        self.cnt = {e: 0 for e in self.ENG}
        self.waited = {e: {} for e in self.ENG}
        self.semkeys = ["E" + e for e in self.ENG]
        self.final_waits = []
        self.n_instr = 0
        self._pools = {}
        self._uid = 0

    def dram(self, name, shape, dt, kind="Internal"):
        return self.nc.dram_tensor(name, list(shape), dt, kind=kind).ap()

    def sbuf(self, name, shape, dt):
        return self.nc.alloc_sbuf_tensor(name, list(shape), dt)

    def psum(self, name, shape, dt=F32):
        return self.nc.alloc_psum_tensor(name, list(shape), dt)

    def pool(self, name, shape, dt, n, space="sbuf"):
        tiles = []
        for i in range(n):
            t = (self.sbuf if space == "sbuf" else self.psum)("%s%d" % (name, i), shape, dt)
            tiles.append((t, Buf("%s%d" % (name, i), excl=(space == "psum"))))
        st = {"i": 0}

        def nxt():
            r = tiles[st["i"] % n]
            st["i"] += 1
            return r
        return nxt

    def _deps(self, eng, reads, writes):
        me = "E" + eng
        deps = []
        for b in reads:
            if b.lw is not None:
                deps.append(b.lw)
            if b.excl:
                for r in b.rd:
                    if r[0] != me:
                        deps.append(r)
        for b in writes:
            if b.lw is not None:
                deps.append(b.lw)
            for r in b.rd:
                deps.append(r)
        return deps

    def _emit_waits(self, eng, deps):
        need = {}
        for (k, v) in deps:
            if k == "Epe" and eng == "pe":
                continue
            if need.get(k, 0) < v:
                need[k] = v
        w = self.waited[eng]
        for k, v in need.items():
            if w.get(k, 0) < v:
                w[k] = v
                self.prog[eng].append(("wait", k, v))

    def op(self, eng, meth, reads=(), writes=(), signal=True, **kw):
        fn = (meth, kw)
        deps = self._deps(eng, reads, writes)
        self._emit_waits(eng, deps)
        k = "E" + eng
        if signal:
            self.cnt[eng] += 1
            v = self.cnt[eng]
        else:
            v = self.cnt[eng] + 1
        self.prog[eng].append(("op", fn, k if signal else None, 1))
        for b in reads:
            b.rd.append((k, v))
        for b in writes:
            b.lw = (k, v)
            b.rd = []
        self.n_instr += 1

    def dma(self, q, out, in_, reads=(), writes=(), sembuf=None, **kw):
        deps = self._deps("dma", reads, writes)
        self._emit_waits(q, deps)
        sb = sembuf if sembuf is not None else (list(writes) + list(reads))[0]
        if sb.dsem is None:
            sb.dsem = "D%d" % len(self.semkeys)
            self.semkeys.append(sb.dsem)
        sb.dcnt += 16
        k, v = sb.dsem, sb.dcnt
        kw = dict(kw)
        kw["out"] = out
        kw["in_"] = in_
        self.prog[q].append(("op", ("dma_start", kw), k, 16))
        for b in reads:
            b.rd.append((k, v))
        for b in writes:
            b.lw = (k, v)
            b.rd = []
        self.n_instr += 1
        return (k, v)

    def store(self, q, out, in_, src_buf):
        d = self.dma(q, out, in_, reads=[src_buf])
        self.final_waits.append(d)
        return d

    def build(self):
        nc = self.nc
        fin = {}
        for (k, v) in self.final_waits:
            fin[k] = max(fin.get(k, 0), v)
        for k, v in fin.items():
            self.prog["sp"].append(("wait", k, v))
        sems = {}
        with contextlib.ExitStack() as st:
            for k in self.semkeys:
                sems[k] = st.enter_context(nc.semaphore(k))
            block = st.enter_context(nc.Block())

            def runner(prog):
                def f(e):
                    for it in prog:
                        if it[0] == "wait":
                            e.wait_ge(sems[it[1]], it[2])
                        else:
                            m, kw = it[1]
                            a = kw.pop("_args", ())
                            ins = getattr(e, m)(*a, **kw)
                            if it[2] is not None:
                                ins.then_inc(sems[it[2]], it[3])
                return f
            block.tensor(runner(self.prog["pe"]))
            block.scalar(runner(self.prog["act"]))
            block.vector(runner(self.prog["dve"]))
            block.gpsimd(runner(self.prog["pool"]))
            block.sync(runner(self.prog["sp"]))
        return nc


def run(nc, in_maps):
    res = run_bass_kernel_spmd(nc, in_maps, core_ids=list(range(NCORE)))
    return res.results


CAST_W = ("w_in", "w_o_diff", "w_o_ssm", "w_uq", "w_ukv", "w_o_mla", "w_out", "w_mlp1", "w_mlp2")


def build_l0(L, wshapes):
    kb = KB()
    ncol = 6 * D // NCORE
    cc = kb.dram("cc", [D, 2], F32, kind="ExternalInput")
    wada = kb.dram("wada", [L, D, ncol], F32, kind="ExternalInput")
    bada = kb.dram("bada", [L, 2, ncol], F32, kind="ExternalInput")
    mods = kb.dram("mods", [L, 2, ncol], F32, kind="ExternalOutput")
    stg = kb.pool("stg", [128, 4096], F32, 3)
    cvt = kb.pool("cvt", [128, 4096], BF16, 3)
    ci = 0
    for name in CAST_W:
        rows, cols = wshapes[name]
        src = kb.dram(name, [rows, cols], F32, kind="ExternalInput")
        dst = kb.dram(name + "_bf", [rows, cols], BF16, kind="ExternalOutput")
        for r0 in range(0, rows, 128):
            pr = min(128, rows - r0)
            for c0 in range(0, cols, 4096):
                cw = min(4096, cols - c0)
                s_t, s_b = stg()
                c_t, c_b = cvt()
                kb.dma("sp", s_t[0:pr, 0:cw], src[r0:r0 + pr, c0:c0 + cw], writes=[s_b])
                eng = ("dve", "pool")[ci % 2]
                ci += 1
                kb.op(eng, "tensor_copy", out=c_t[0:pr, 0:cw], in_=s_t[0:pr, 0:cw], reads=[s_b], writes=[c_b])
                kb.store("act", dst[r0:r0 + pr, c0:c0 + cw], c_t[0:pr, 0:cw], c_b)
    cc_sb = kb.sbuf("cc_sb", [128, KC, 2], F32)
    CCB = Buf("cc")
    sc_sb = kb.sbuf("sc_sb", [128, KC, 2], F32)
    SCB = Buf("sc")
    kb.dma("sp", cc_sb[:], cc.rearrange("(j p) t -> p j t", p=128), writes=[CCB])
    kb.op("act", "activation", out=sc_sb[:], in_=cc_sb[:], func=AF.Silu, reads=[CCB], writes=[SCB])
    wst = kb.pool("wst", [128, KC, 512], F32, 2)
    psm = kb.pool("psm", [128, 512], F32, 2, space="psum")
    bsb = kb.pool("bsb", [2, 512], F32, 2)
    osb = kb.pool("osb", [2, 512], F32, 2)
    for l in range(L):
        for c0 in range(0, ncol, 512):
            w_t, w_b = wst()
            kb.dma("sp", w_t[:], wada[l, :, c0:c0 + 512].rearrange("(j p) n -> p j n", p=128), writes=[w_b])
            b_t, b_b = bsb()
            kb.dma("sp", b_t[:], bada[l, :, c0:c0 + 512], writes=[b_b])
            p_t, p_b = psm()
            for j in range(KC):
                kb.op("pe", "matmul", out=p_t[0:2, :], lhsT=sc_sb[:, j, :], rhs=w_t[:, j, :], start=(j == 0), stop=(j == KC - 1),
                      reads=[SCB, w_b], writes=[p_b], signal=(j == KC - 1))
            o_t, o_b = osb()
            kb.op("dve", "tensor_tensor", out=o_t[:], in0=p_t[0:2, :], in1=b_t[:], op=ALU.add, reads=[p_b, b_b], writes=[o_b])
            kb.store("sp", mods[l, :, c0:c0 + 512], o_t[:], o_b)
    return kb.build()


def run_l0(inp, L):
    wsl, wshapes = {}, {}
    for name in CAST_W:
        w = inp[name]
        flat = w.reshape(-1, w.shape[-1])
        rows = flat.shape[0] // NCORE
        wshapes[name] = (rows, flat.shape[1])
        wsl[name] = [np.ascontiguousarray(flat[i * rows:(i + 1) * rows]) for i in range(NCORE)]
    nc = build_l0(L, wshapes)
    ncol = 6 * D // NCORE
    cc = np.ascontiguousarray(np.stack([inp["c"][0], inp["c_ctx"]], axis=1))
    in_maps = []
    for i in range(NCORE):
        m = {"cc": cc,
             "wada": np.ascontiguousarray(inp["w_ada"][:, :, i * ncol:(i + 1) * ncol]),
             "bada": np.ascontiguousarray(np.repeat(inp["b_ada"][:, None, i * ncol:(i + 1) * ncol], 2, axis=1))}
        for name in CAST_W:
            m[name] = wsl[name][i]
        in_maps.append(m)
    res = run(nc, in_maps)
    mods = np.concatenate([r["mods"] for r in res], axis=2)
    wbf = {}
    for name in CAST_W:
        full = np.concatenate([r[name + "_bf"] for r in res], axis=0)
        wbf[name] = full.reshape(inp[name].shape)
    return mods, wbf


def groups_of(Tl, with_ctx=True):
    gs = []
    tg = min(512, Tl)
    for t0 in range(0, Tl, tg):
        gs.append((t0, tg, True))
    if with_ctx:
        gs.append((Tl, CTX, False))
    return gs


class Dev:
    def __init__(self, kb, nps=8):
        self.kb = kb
        self.ps = kb.pool("ps", [128, 512], F32, nps, space="psum")
        self.ones_d = kb.dram("ones", [128, 128], F32, kind="ExternalInput")
        self.ones = kb.sbuf("ones_sb", [128, 128], F32)
        self.ONES = Buf("ones")
        kb.dma("sp", self.ones[:], self.ones_d[:], writes=[self.ONES])
        self.acc = kb.pool("nacc", [128, 512], F32, 2)
        self.rstd = kb.pool("nrstd", [128, 512], F32, 2)
        self.ntmp = kb.pool("ntmp", [128, 512], F32, 3)
        self.epsb = kb.sbuf("epsb", [128, 1], F32)
        self.EPSB = Buf("epsb")
        kb.op("dve", "memset", _args=(self.epsb[:], EPS), writes=[self.EPSB])

    def rstd_fm(self, src, SRC, kcn, tn, dn, sq, SQ):
        kb = self.kb
        kb.op("act", "activation", out=sq[:, 0:kcn, 0:tn], in_=src[:, 0:kcn, 0:tn], func=AF.Square,
              reads=[SRC], writes=[SQ])
        a_t, a_b = self.acc()
        if kcn > 1:
            kb.op("dve", "tensor_reduce", out=a_t[:, 0:tn],
                                                   in_=sq[:, 0:kcn, 0:tn].rearrange("p j t -> p t j"),
                                                   axis=AX.X, op=ALU.add, reads=[SQ], writes=[a_b])
        else:
            kb.op("dve", "tensor_copy", out=a_t[:, 0:tn], in_=sq[:, 0, 0:tn], reads=[SQ], writes=[a_b])
        p_t, p_b = self.ps()
        kb.op("pe", "matmul", out=p_t[:, 0:tn], lhsT=self.ones[:], rhs=a_t[:, 0:tn], start=True, stop=True,
              reads=[self.ONES, a_b], writes=[p_b])
        r_t, r_b = self.rstd()
        kb.op("act", "activation", out=r_t[:, 0:tn], in_=p_t[:, 0:tn], func=AF.Sqrt,
                                            scale=1.0 / dn, bias=self.epsb[:],
              reads=[p_b, self.EPSB], writes=[r_b])
        kb.op("dve", "reciprocal", out=r_t[:, 0:tn], in_=r_t[:, 0:tn], reads=[r_b], writes=[r_b])
        return r_t, r_b

    def norm_fm(self, src, SRC, kcn, tn, dn, gs, shift, PV, dst, DST, sq, SQ):
        kb = self.kb
        r_t, r_b = self.rstd_fm(src, SRC, kcn, tn, dn, sq, SQ)
        for j in range(kcn):
            if shift is None:
                kb.op("dve", "scalar_tensor_tensor",
                    out=dst[:, j, 0:tn], in0=src[:, j, 0:tn], scalar=gs[:, j:j + 1], in1=r_t[:, 0:tn],
                    op0=ALU.mult, op1=ALU.mult, reads=[SRC, PV, r_b], writes=[DST])
            else:
                t_t, t_b = self.ntmp()
                kb.op("dve", "scalar_tensor_tensor",
                    out=t_t[:, 0:tn], in0=src[:, j, 0:tn], scalar=gs[:, j:j + 1], in1=r_t[:, 0:tn],
                    op0=ALU.mult, op1=ALU.mult, reads=[SRC, PV, r_b], writes=[t_b])
                kb.op("act", "activation",
                    out=dst[:, j, 0:tn], in_=t_t[:, 0:tn], func=AF.Identity, bias=shift[:, j:j + 1],
                    reads=[t_b, PV], writes=[DST])


def pvec(v):
    v = np.asarray(v, np.float32).reshape(-1, 128)
    return np.ascontiguousarray(v.T)


def rope_tables(pos):
    row = (pos // GRID_W).astype(np.float32)
    col = (pos % GRID_W).astype(np.float32)
    nf = 16
    inv = (1.0 / (10000.0 ** (np.arange(nf, dtype=np.float32) / nf))).astype(np.float32)
    ang = np.concatenate([row[:, None] * inv, col[:, None] * inv], axis=-1).astype(np.float32)
    c, s = np.cos(ang).astype(np.float32), np.sin(ang).astype(np.float32)
    C = np.empty((128, len(pos)), np.float32)
    S = np.empty((128, len(pos)), np.float32)
    for r in range(128):
        C[r] = c[:, r % 32]
        S[r] = (-s[:, r % 32]) if (r % 64) < 32 else s[:, r % 32]
    return C, S


def rot_perm():
    R = np.zeros((128, 128), np.float32)
    for m in range(128):
        k = m + 32 if (m % 64) < 32 else m - 32
        R[k, m] = 1.0
    return R.astype(NPBF)


NA = OFF[9]
NV_LA = 5 * KC + 4 + 2


DBG = {}


def build_la(Tl):
    kb = KB()
    dv = Dev(kb)
    T = Tl + CTX
    sT = kb.dram("sT", [D, T], F32, kind="ExternalInput")
    pv_d = kb.dram("pv", [128, NV_LA], F32, kind="ExternalInput")
    win = kb.dram("win", [D, NA], BF16, kind="ExternalInput")
    wuq_d = kb.dram("wuq", [Q_LORA, 1536], BF16, kind="ExternalInput")
    wukv_d = kb.dram("wukv", [KV_LORA, 2048], BF16, kind="ExternalInput")
    rc_d = kb.dram("ropeC", [128, Tl], F32, kind="ExternalInput")
    rs_d = kb.dram("ropeS", [128, Tl], F32, kind="ExternalInput")
    rp_d = kb.dram("rperm", [128, 128], BF16, kind="ExternalInput")
    qkT = kb.dram("qkT", [2048, T], BF16, kind="ExternalOutput")
    v_o = kb.dram("v", [T, 1024], BF16, kind="ExternalOutput")
    miscT = kb.dram("miscT", [N_MISC, T], F32, kind="ExternalOutput")
    qmT = kb.dram("qmT", [1536, T], BF16, kind="ExternalOutput")
    kmT = kb.dram("kmT", [1024, T], BF16, kind="ExternalOutput")
    vm_o = kb.dram("vm", [T, 1024], BF16, kind="ExternalOutput")
    krT = kb.dram("krT", [64, T], BF16, kind="ExternalOutput")

    pv = kb.sbuf("pv_sb", [128, NV_LA], F32)
    PV = Buf("pv")
    kb.dma("sp", pv[:], pv_d[:], writes=[PV])
    gsl = kb.sbuf("gsl", [128, KC], F32)
    gsc = kb.sbuf("gsc", [128, KC], F32)
    GS = Buf("gs")
    g1, shl, scl, shc, scc = (pv[:, i * KC:(i + 1) * KC] for i in range(5))
    qg = pv[:, 5 * KC:5 * KC + 4]
    kvg = pv[:, 5 * KC + 4:5 * KC + 6]
    kb.op("dve", "scalar_tensor_tensor", out=gsl[:], in0=scl, scalar=1.0, in1=g1, op0=ALU.add, op1=ALU.mult,
          reads=[PV], writes=[GS])
    kb.op("dve", "scalar_tensor_tensor", out=gsc[:], in0=scc, scalar=1.0, in1=g1, op0=ALU.add, op1=ALU.mult,
          reads=[PV], writes=[GS])
    ropeC = kb.sbuf("ropeC_sb", [128, Tl], F32)
    ropeS = kb.sbuf("ropeS_sb", [128, Tl], F32)
    RTC = Buf("rtc")
    RTS = Buf("rts")
    kb.dma("sp", ropeC[:], rc_d[:], writes=[RTC])
    kb.dma("sp", ropeS[:], rs_d[:], writes=[RTS])
    RT2 = Buf("rt2b")
    rperm = kb.sbuf("rperm_sb", [128, 128], BF16)
    kb.dma("sp", rperm[:], rp_d[:], writes=[RT2])
    wuq = kb.sbuf("wuq_sb", [128, 4, 1536], BF16)
    wukv = kb.sbuf("wukv_sb", [128, 2, 2048], BF16)
    WU = Buf("wu")
    WU2 = Buf("wu2")
    kb.dma("sp", wuq[:], wuq_d.rearrange("(j p) n -> p j n", p=128), writes=[WU])
    kb.dma("sp", wukv[:], wukv_d.rearrange("(j p) n -> p j n", p=128), writes=[WU2])

    sTg = kb.sbuf("sTg", [128, KC, 512], F32)
    STG = Buf("sTg")
    sq = kb.sbuf("sq", [128, KC, 512], F32)
    SQ = Buf("sq")
    hT = kb.sbuf("hT", [128, KC, 512], BF16)
    HT = Buf("hT")
    wblk = kb.pool("wblk", [128, KC, 512], BF16, 2)
    mla_in = kb.sbuf("mla_in", [128, 7, 512], F32)
    MLI = Buf("mli")
    mla_n = kb.sbuf("mla_n", [128, 7, 512], BF16)
    MLN = Buf("mln")
    stf = kb.pool("stf", [128, 512], F32, 3)
    stb = kb.pool("stb", [128, 512], BF16, 4)
    rt1 = kb.pool("rt1", [128, 512], F32, 2)
    rt2 = kb.pool("rt2", [128, 512], F32, 2)
    qb = kb.pool("qb", [128, 512], BF16, 2)

    def rope_epi(p_t, p_b, nr, t0, tn, dst_ap):
        q_t, q_b = qb()
        kb.op("act", "activation", out=q_t[0:nr, 0:tn], in_=p_t[0:nr, 0:tn], func=AF.Copy,
              reads=[p_b], writes=[q_b])
        p2, p2b = dv.ps()
        kb.op("pe", "matmul", out=p2[0:nr, 0:tn], lhsT=rperm[0:nr, 0:nr], rhs=q_t[0:nr, 0:tn], start=True, stop=True,
              reads=[RT2, q_b], writes=[p2b])
        a_t, a_b = rt1()
        kb.op("dve", "tensor_tensor", out=a_t[0:nr, 0:tn], in0=p_t[0:nr, 0:tn], in1=ropeC[0:nr, t0:t0 + tn], op=ALU.mult,
              reads=[p_b, RTC], writes=[a_b])
        b_t, b_b = rt2()
        kb.op("dve", "tensor_tensor", out=b_t[0:nr, 0:tn], in0=p2[0:nr, 0:tn], in1=ropeS[0:nr, t0:t0 + tn], op=ALU.mult,
              reads=[p2b, RTS], writes=[b_b])
        o_t, o_b = stb()
        kb.op("pool", "tensor_tensor", out=o_t[0:nr, 0:tn], in0=a_t[0:nr, 0:tn], in1=b_t[0:nr, 0:tn], op=ALU.add,
              reads=[a_b, b_b], writes=[o_b])
        kb.store("sp", dst_ap, o_t[0:nr, 0:tn], o_b)

    def copy_epi(p_t, p_b, nr, nc_, dst_ap, bf, eng="act"):
        o_t, o_b = (stb if bf else stf)()
        if eng == "act":
            kb.op("act", "activation", out=o_t[0:nr, 0:nc_], in_=p_t[0:nr, 0:nc_], func=AF.Copy,
                  reads=[p_b], writes=[o_b])
        else:
            kb.op("dve", "tensor_copy", out=o_t[0:nr, 0:nc_], in_=p_t[0:nr, 0:nc_], reads=[p_b], writes=[o_b])
        kb.store("sp", dst_ap, o_t[0:nr, 0:nc_], o_b)

    for (t0, tn, is_lat) in groups_of(Tl)[:DBG.get('groups', 99)]:
        kb.dma("sp", sTg[:, :, 0:tn], sT[:, t0:t0 + tn].rearrange("(j p) t -> p j t", p=128), writes=[STG])
        dv.norm_fm(sTg, STG, KC, tn, D, gsl if is_lat else gsc, shl if is_lat else shc, GS, hT, HT, sq, SQ)
        for c0 in range(0, NA, 512)[DBG.get('b0', 0):DBG.get('b1', 99)]:
            cw = min(512, NA - c0)
            w_t, w_b = wblk()
            kb.dma("sp", w_t[:, :, 0:cw], win[:, c0:c0 + cw].rearrange("(j p) n -> p j n", p=128), writes=[w_b])
            if 2048 <= c0 < 3072:
                for tt in range(0, tn, 128):
                    p_t, p_b = dv.ps()
                    for j in range(KC):
                        kb.op("pe", "matmul", out=p_t[:, 0:cw], lhsT=hT[:, j, tt:tt + 128], rhs=w_t[:, j, 0:cw], start=(j == 0), stop=(j == KC - 1),
                            reads=[HT, w_b], writes=[p_b], signal=(j == KC - 1))
                    copy_epi(p_t, p_b, 128, cw, v_o[t0 + tt:t0 + tt + 128, c0 - 2048:c0 - 2048 + cw], True,
                             eng=("act", "dve")[(tt // 128) % 2])
                continue
            chunks = [(cc, min(128, cw - cc)) for cc in range(0, cw, 128)]
            if c0 + cw == NA:
                chunks = chunks[:-1] + [(C_KR - c0, 64), (C_DT - c0, 32)]
            for (cc, nr) in chunks:
                col = c0 + cc
                p_t, p_b = dv.ps()
                for j in range(KC):
                    kb.op("pe", "matmul", out=p_t[0:nr, 0:tn], lhsT=w_t[:, j, cc:cc + nr], rhs=hT[:, j, 0:tn], start=(j == 0), stop=(j == KC - 1),
                        reads=[HT, w_b], writes=[p_b], signal=(j == KC - 1))
                if col < 2048:
                    if is_lat:
                        rope_epi(p_t, p_b, 128, t0, tn, qkT[col:col + 128, t0:t0 + tn])
                    else:
                        copy_epi(p_t, p_b, 128, tn, qkT[col:col + 128, t0:t0 + tn], True)
                else:
                    mrow = col - 3072
                    if C_CQ <= col < C_DT:
                        ci = (col - C_CQ) // 128
                        kb.op("dve", "tensor_copy", out=mla_in[0:nr, ci, 0:tn], in_=p_t[0:nr, 0:tn],
                              reads=[p_b], writes=[MLI])
                    copy_epi(p_t, p_b, nr, tn, miscT[mrow:mrow + nr, t0:t0 + tn], False)
        if DBG.get('nomla'):
            continue
        dv.norm_fm(mla_in[:, 0:4, :], MLI, 4, tn, Q_LORA, qg, None, PV, mla_n[:, 0:4, :], MLN, sq, SQ)
        dv.norm_fm(mla_in[:, 4:6, :], MLI, 2, tn, KV_LORA, kvg, None, PV, mla_n[:, 4:6, :], MLN, sq, SQ)
        for h in range(H_C):
            for (c_lo, nr, is_rope) in ((h * 192, 128, False), (h * 192 + 128, 64, True)):
                p_t, p_b = dv.ps()
                for j in range(4):
                    kb.op("pe", "matmul", out=p_t[0:nr, 0:tn], lhsT=wuq[:, j, c_lo:c_lo + nr], rhs=mla_n[:, j, 0:tn], start=(j == 0), stop=(j == 3),
                        reads=[MLN, WU], writes=[p_b], signal=(j == 3))
                if is_rope and is_lat:
                    rope_epi(p_t, p_b, nr, t0, tn, qmT[c_lo:c_lo + nr, t0:t0 + tn])
                else:
                    copy_epi(p_t, p_b, nr, tn, qmT[c_lo:c_lo + nr, t0:t0 + tn], True)
            p_t, p_b = dv.ps()
            for j in range(2):
                kb.op("pe", "matmul", out=p_t[:, 0:tn], lhsT=wukv[:, j, h * 256:h * 256 + 128], rhs=mla_n[:, 4 + j, 0:tn], start=(j == 0), stop=(j == 1),
                    reads=[MLN, WU2], writes=[p_b], signal=(j == 1))
            copy_epi(p_t, p_b, 128, tn, kmT[h * 128:(h + 1) * 128, t0:t0 + tn], True, eng="dve")
        for tt in range(0, tn, 128):
            for half in range(2):
                p_t, p_b = dv.ps()
                for hh in range(4):
                    h = half * 4 + hh
                    for j in range(2):
                        kb.op("pe", "matmul", out=p_t[:, hh * 128:(hh + 1) * 128], lhsT=mla_n[:, 4 + j, tt:tt + 128],
                            rhs=wukv[:, j, h * 256 + 128:h * 256 + 256], start=(j == 0), stop=(j == 1),
                            reads=[MLN, WU2], writes=[p_b], signal=(hh == 3 and j == 1))
                copy_epi(p_t, p_b, 128, 512, vm_o[t0 + tt:t0 + tt + 128, half * 512:(half + 1) * 512], True)
        krp, krb = dv.ps()
        kb.op("dve", "tensor_copy", out=krp[0:64, 0:tn], in_=mla_in[0:64, 6, 0:tn], reads=[MLI], writes=[krb])
        if is_lat:
            rope_epi(krp, krb, 64, t0, tn, krT[:, t0:t0 + tn])
        else:
            copy_epi(krp, krb, 64, tn, krT[:, t0:t0 + tn], True)
    return kb.build()


def la_consts(Tl, core):
    pos = np.arange(core * Tl, (core + 1) * Tl)
    C, S = rope_tables(pos)
    return {"ropeC": C, "ropeS": S, "rperm": rot_perm(), "ones": np.ones((128, 128), np.float32)}


def run_la(sT_lat, sT_ctx, mods_l, inp, wbf, l):
    S = sT_lat.shape[1]
    Tl = S // NCORE
    nc = build_la(Tl)
    mx = mods_l[0].reshape(6, D)
    mc = mods_l[1].reshape(6, D)
    pv = np.concatenate([pvec(inp["norm1_g"][l]), pvec(mx[0]), pvec(mx[1]), pvec(mc[0]), pvec(mc[1]),
                         pvec(inp["q_norm_g"][l]), pvec(inp["kv_norm_g"][l])], axis=1)
    win = np.ascontiguousarray(wbf["w_in"][l][:, COLPERM])
    in_maps = []
    for i in range(NCORE):
        m = {"sT": np.ascontiguousarray(np.concatenate([sT_lat[:, i * Tl:(i + 1) * Tl], sT_ctx], axis=1)),
             "pv": pv, "win": win, "wuq": wbf["w_uq"][l], "wukv": wbf["w_ukv"][l]}
        m.update(la_consts(Tl, i))
        in_maps.append(m)
    return run(nc, in_maps)


TG_LD = 256
NV_LD = 15 * KC + 8
GATE0 = OFF[9]


def build_ld(Tl, last):
    kb = KB()
    dv = Dev(kb)
    T = Tl if last else Tl + CTX
    sT = kb.dram("sT", [D, T], F32, kind="ExternalInput")
    oaT = kb.dram("oaT", [1024, T], BF16, kind="ExternalInput")
    ygT = kb.dram("ygT", [1024, T], F32, kind="ExternalInput")
    ocT = kb.dram("ocT", [1024, T], BF16, kind="ExternalInput")
    pv_d = kb.dram("pv", [128, NV_LD], F32, kind="ExternalInput")
    wg = kb.dram("wg", [D, 3 * D], BF16, kind="ExternalInput")
    wod = kb.dram("wod", [1024, D], BF16, kind="ExternalInput")
    wos = kb.dram("wos", [1024, D], BF16, kind="ExternalInput")
    wom = kb.dram("wom", [1024, D], BF16, kind="ExternalInput")
    wout = kb.dram("wout", [D, D], BF16, kind="ExternalInput")
    w1 = kb.dram("w1", [D, D_FF], BF16, kind="ExternalInput")
    w2 = kb.dram("w2", [D_FF, D], BF16, kind="ExternalInput")
    sO = kb.dram("sO", [D, T], F32, kind="ExternalOutput")

    pv = kb.sbuf("pv_sb", [128, NV_LD], F32)
    PV = Buf("pv")
    kb.dma("sp", pv[:], pv_d[:], writes=[PV])
    col = lambda i: pv[:, i * KC:(i + 1) * KC]
    ssg = pv[:, 15 * KC:15 * KC + 8]
    gs = kb.sbuf("gs_sb", [128, 4, KC], F32)
    GS = Buf("gs")
    for k, (sc_i, g_i) in enumerate(((2, 0), (4, 0), (9, 7), (11, 7))):
        kb.op("dve", "scalar_tensor_tensor", out=gs[:, k, :], in0=col(sc_i), scalar=1.0, in1=col(g_i),
              op0=ALU.add, op1=ALU.mult, reads=[PV], writes=[GS])

    tg = min(TG_LD, Tl)
    sTg = kb.sbuf("sTg", [128, KC, tg], F32)
    STG = Buf("sTg")
    hT = kb.sbuf("hT", [128, KC, tg], BF16)
    HT = Buf("hT")
    oa = kb.sbuf("oa", [128, 8, tg], BF16)
    OA = Buf("oa")
    oc = kb.sbuf("oc", [128, 8, tg], BF16)
    OC = Buf("oc")
    yg = kb.sbuf("yg", [128, 8, tg], F32)
    YG = Buf("yg")
    ysn = kb.sbuf("ysn", [128, 8, tg], BF16)
    YSN = Buf("ysn")
    mg = kb.sbuf("mg", [128, KC, tg], BF16)
    MG = Buf("mg")
    uT = kb.pool("uT", [128, KC, tg], BF16, 2)
    wt = kb.pool("wt", [128, KC, 256], BF16, 6)
    sg = kb.pool("sg", [128, tg], F32, 3)
    tt_ = kb.pool("tt", [128, tg], F32, 3)
    t01 = kb.pool("t01", [128, tg], F32, 2)
    rl = kb.pool("rl", [128, tg], F32, 3)
    fo = kb.pool("fo", [128, tg], F32, 2)

    def wload(src, r0, nkc, c0, ncol=256):
        w_t, w_b = wt()
        kb.dma("sp", w_t[:, 0:nkc, 0:ncol], src[r0:r0 + nkc * 128, c0:c0 + ncol].rearrange("(j p) n -> p j n", p=128),
               writes=[w_b])
        return w_t, w_b

    def mmgroup(p_t, p_b, w_t, w_b, nkc, cc, x, XB, tn):
        for j in range(nkc):
            kb.op("pe", "matmul", out=p_t[:, 0:tn], lhsT=w_t[:, j, cc:cc + 128], rhs=x[:, j, 0:tn],
                  start=(j == 0), stop=(j == nkc - 1), reads=[w_b, XB], writes=[p_b], signal=(j == nkc - 1))

    groups = []
    for t0 in range(0, Tl, tg):
        groups.append((t0, tg, True))
    if not last:
        for t0 in range(Tl, Tl + CTX, tg):
            groups.append((t0, min(tg, CTX), False))
    for (t0, tn, is_lat) in groups:
        li = 0 if is_lat else 1
        kb.dma("sp", sTg[:, :, 0:tn], sT[:, t0:t0 + tn].rearrange("(j p) t -> p j t", p=128), writes=[STG])
        kb.dma("sp", oa[:, :, 0:tn], oaT[:, t0:t0 + tn].rearrange("(j p) t -> p j t", p=128), writes=[OA])
        kb.dma("sp", oc[:, :, 0:tn], ocT[:, t0:t0 + tn].rearrange("(j p) t -> p j t", p=128), writes=[OC])
        kb.dma("sp", yg[:, :, 0:tn], ygT[:, t0:t0 + tn].rearrange("(j p) t -> p j t", p=128), writes=[YG])
        dv.norm_fm(sTg, STG, KC, tn, D, gs[:, li, :], col(1 + 2 * li), GS, hT, HT, hT, HT)
        for g in range(G_M):
            dv.norm_fm(yg[:, 4 * g:4 * g + 4, :], YG, 4, tn, D_INNER // G_M, ssg[:, 4 * g:4 * g + 4], None, PV,
                       ysn[:, 4 * g:4 * g + 4, :], YSN, ysn[:, 4 * g:4 * g + 4, :], YSN)
        for c0 in range(0, D, 256):
            wts = [wload(wod, 0, 8, c0), wload(wos, 0, 8, c0), wload(wom, 0, 8, c0)]
            gts = [wload(wg, 0, KC, i * D + c0) for i in range(3)]
            for cc in (0, 128):
                m = (c0 + cc) // 128
                tacc = None
                for i, (x, XB) in enumerate(((oa, OA), (ysn, YSN), (oc, OC))):
                    py, pyb = dv.ps()
                    mmgroup(py, pyb, wts[i][0], wts[i][1], 8, cc, x, XB, tn)
                    pg, pgb = dv.ps()
                    mmgroup(pg, pgb, gts[i][0], gts[i][1], KC, cc, hT, HT, tn)
                    s_t, s_b = sg()
                    kb.op("act", "activation", out=s_t[:, 0:tn], in_=pg[:, 0:tn], func=AF.Sigmoid, reads=[pgb], writes=[s_b])
                    y_t, y_b = tt_()
                    kb.op("dve", "tensor_tensor", out=y_t[:, 0:tn], in0=py[:, 0:tn], in1=s_t[:, 0:tn], op=ALU.mult,
                          reads=[pyb, s_b], writes=[y_b])
                    if i == 0:
                        tacc = (y_t, y_b)
                    elif i == 1:
                        a_t, a_b = t01()
                        kb.op("pool", "tensor_tensor", out=a_t[:, 0:tn], in0=tacc[0][:, 0:tn], in1=y_t[:, 0:tn], op=ALU.add,
                              reads=[tacc[1], y_b], writes=[a_b])
                        tacc = (a_t, a_b)
                    else:
                        kb.op("pool", "tensor_tensor", out=mg[:, m, 0:tn], in0=tacc[0][:, 0:tn], in1=y_t[:, 0:tn], op=ALU.add,
                              reads=[tacc[1], y_b], writes=[MG])
        for c0 in range(0, D, 256):
            w_t, w_b = wload(wout, 0, KC, c0)
            for cc in (0, 128):
                m = (c0 + cc) // 128
                p_t, p_b = dv.ps()
                mmgroup(p_t, p_b, w_t, w_b, KC, cc, mg, MG, tn)
                kb.op("dve", "scalar_tensor_tensor", out=sTg[:, m, 0:tn], in0=p_t[:, 0:tn], scalar=pv[:, (5 + li) * KC + m:(5 + li) * KC + m + 1],
                      in1=sTg[:, m, 0:tn], op0=ALU.mult, op1=ALU.add, reads=[p_b, PV, STG], writes=[STG])
        dv.norm_fm(sTg, STG, KC, tn, D, gs[:, 2 + li, :], col(8 + 2 * li), GS, hT, HT, hT, HT)
        for hb in range(D_FF // 2048):
            u_t, u_b = uT()
            for c0 in range(0, 2048, 256):
                w_t, w_b = wload(w1, 0, KC, hb * 2048 + c0)
                for cc in (0, 128):
                    c = (c0 + cc) // 128
                    p_t, p_b = dv.ps()
                    mmgroup(p_t, p_b, w_t, w_b, KC, cc, hT, HT, tn)
                    r_t, r_b = rl()
                    kb.op("act", "activation", out=r_t[:, 0:tn], in_=p_t[:, 0:tn], func=AF.Relu, reads=[p_b], writes=[r_b])
                    kb.op("pool", "tensor_tensor", out=u_t[:, c, 0:tn], in0=r_t[:, 0:tn], in1=r_t[:, 0:tn], op=ALU.mult,
                          reads=[r_b], writes=[u_b])
            for c0 in range(0, D, 256):
                w_t, w_b = wload(w2, hb * 2048, KC, c0)
                for cc in (0, 128):
                    m = (c0 + cc) // 128
                    p_t, p_b = dv.ps()
                    mmgroup(p_t, p_b, w_t, w_b, KC, cc, u_t, u_b, tn)
                    kb.op("dve", "scalar_tensor_tensor", out=sTg[:, m, 0:tn], in0=p_t[:, 0:tn],
                          scalar=pv[:, (12 + li) * KC + m:(12 + li) * KC + m + 1], in1=sTg[:, m, 0:tn],
                          op0=ALU.mult, op1=ALU.add, reads=[p_b, PV, STG], writes=[STG])
        if not last:
            kb.store("sp", sO[:, t0:t0 + tn].rearrange("(j p) t -> p j t", p=128), sTg[:, :, 0:tn], STG)
        else:
            r_t, r_b = dv.rstd_fm(sTg, STG, KC, tn, D, hT, HT)
            for j in range(KC):
                o_t, o_b = fo()
                kb.op("dve", "scalar_tensor_tensor", out=o_t[:, 0:tn], in0=sTg[:, j, 0:tn], scalar=pv[:, 14 * KC + j:14 * KC + j + 1],
                      in1=r_t[:, 0:tn], op0=ALU.mult, op1=ALU.mult, reads=[STG, PV, r_b], writes=[o_b])
                kb.store("sp", sO[j * 128:(j + 1) * 128, t0:t0 + tn], o_t[:, 0:tn], o_b)
    return kb.build()


def run_ld(sT_lat, sT_ctx, oaT, ygT, ocT, mods_l, inp, wbf, l, last):
    S = sT_lat.shape[1]
    Tl = S // NCORE
    nc = build_ld(Tl, last)
    mx = mods_l[0].reshape(6, D)
    mc = mods_l[1].reshape(6, D)
    cols = [inp["norm1_g"][l], mx[0], mx[1], mc[0], mc[1], mx[2], mc[2], inp["norm2_g"][l], mx[3], mx[4], mc[3], mc[4],
            mx[5], mc[5], inp["final_norm_g"]]
    pv = np.concatenate([pvec(c) for c in cols] + [pvec(inp["ssm_norm_g"][l])], axis=1)
    wg = np.ascontiguousarray(wbf["w_in"][l][:, GATE0:])
    in_maps = []
    for i in range(NCORE):
        def sl(a_lat, a_ctx):
            parts = [a_lat[:, i * Tl:(i + 1) * Tl]] + ([] if last else [a_ctx])
            return np.ascontiguousarray(np.concatenate(parts, axis=1))
        m = {"sT": sl(sT_lat, sT_ctx), "oaT": sl(oaT[:, :S], oaT[:, S:]), "ygT": sl(ygT[:, :S], ygT[:, S:]),
             "ocT": sl(ocT[:, :S], ocT[:, S:]), "pv": pv, "wg": wg, "wod": wbf["w_o_diff"][l], "wos": wbf["w_o_ssm"][l],
             "wom": wbf["w_o_mla"][l], "wout": wbf["w_out"][l], "w1": wbf["w_mlp1"][l], "w2": wbf["w_mlp2"][l],
             "ones": np.ones((128, 128), np.float32)}
        in_maps.append(m)
    res = run(nc, in_maps)
    new_lat = np.concatenate([r["sO"][:, :Tl] for r in res], axis=1)
    new_ctx = None if last else res[0]["sO"][:, Tl:]
    return new_lat, new_ctx


def build_lb(Tl, S, with_ctx, lam_init):
    kb = KB()
    dv = Dev(kb, nps=2)
    Tq = Tl + (CTX if with_ctx else 0)
    TK = S + CTX
    NKT = TK // 128
    qd = kb.dram("qd", [1024, Tq], BF16, kind="ExternalInput")
    kd = kb.dram("kd", [1024, TK], BF16, kind="ExternalInput")
    vd = kb.dram("vd", [TK, 1024], BF16, kind="ExternalInput")
    qm = kb.dram("qm", [1536, Tq], BF16, kind="ExternalInput")
    km = kb.dram("km", [1024, TK], BF16, kind="ExternalInput")
    kr = kb.dram("kr", [64, TK], BF16, kind="ExternalInput")
    vmd = kb.dram("vmd", [TK, 1024], BF16, kind="ExternalInput")
    lq_d = kb.dram("lq", [128, 256], F32, kind="ExternalInput")
    sg_d = kb.dram("subg", [128, 1], F32, kind="ExternalInput")
    oaT = kb.dram("oaT", [1024, Tq], BF16, kind="ExternalOutput")
    ocT = kb.dram("ocT", [1024, Tq], BF16, kind="ExternalOutput")

    lq = kb.sbuf("lq_sb", [128, 256], F32)
    LQ = Buf("lq")
    kb.dma("sp", lq[:], lq_d[:], writes=[LQ])
    subg = kb.sbuf("subg_sb", [128, 1], F32)
    SG = Buf("subg")
    kb.dma("sp", subg[:], sg_d[:], writes=[SG])
    lt = kb.sbuf("lt", [128, 128], F32)
    LT = Buf("lt")
    ls = kb.sbuf("ls", [128, 4], F32)
    LS = Buf("ls")
    kb.op("dve", "tensor_tensor", out=lt[:, 0:64], in0=lq[:, 0:64], in1=lq[:, 64:128], op=ALU.mult, reads=[LQ], writes=[LT])
    kb.op("dve", "tensor_tensor", out=lt[:, 64:128], in0=lq[:, 128:192], in1=lq[:, 192:256], op=ALU.mult, reads=[LQ], writes=[LT])
    kb.op("dve", "tensor_reduce", out=ls[:, 0:2], in_=lt[:].rearrange("p (a b) -> p a b", a=2), axis=AX.X, op=ALU.add,
          reads=[LT], writes=[LS])
    kb.op("act", "activation", out=ls[:, 0:2], in_=ls[:, 0:2], func=AF.Exp, reads=[LS], writes=[LS])
    kb.op("dve", "tensor_tensor", out=ls[:, 2:3], in0=ls[:, 1:2], in1=ls[:, 0:1], op=ALU.subtract, reads=[LS], writes=[LS])
    kb.op("dve", "tensor_scalar", out=ls[:, 2:3], in0=ls[:, 2:3], scalar1=-float(lam_init), scalar2=None, op0=ALU.add,
          reads=[LS], writes=[LS])
    kb.op("dve", "tensor_scalar", out=ls[:, 3:4], in0=subg[:, 0:1], scalar1=float(1.0 - lam_init), scalar2=None, op0=ALU.mult,
          reads=[SG, LS], writes=[LS])
    nlam = ls[:, 2:3]
    gsub = ls[:, 3:4]
    ones_bf = kb.sbuf("ones_bf", [128, 128], BF16)
    OB = Buf("onesbf")
    kb.op("dve", "tensor_copy", out=ones_bf[:], in_=dv.ones[:], reads=[dv.ONES], writes=[OB])

    tg = min(512, Tl)
    kA = kb.sbuf("kA", [128, TK], BF16)
    KA = Buf("kA")
    kB_ = kb.sbuf("kB", [64, TK], BF16)
    KBb = Buf("kB")
    vt = kb.sbuf("vt", [128, NKT, 128], BF16)
    PK = 13 if NKT % 13 == 0 else (10 if NKT % 10 == 0 else NKT)
    VTs = [Buf("vt%d" % i) for i in range(NKT // PK)]

    def load_v(src, h):
        for i, b in enumerate(VTs):
            kb.dma("sp", vt[:, i * PK:(i + 1) * PK, :],
                   src[i * PK * 128:(i + 1) * PK * 128, h * 128:(h + 1) * 128].rearrange("(k p) e -> p k e", p=128), writes=[b])
    qA = kb.sbuf("qA", [128, Tq], BF16)
    QA = Buf("qA")
    qB = kb.sbuf("qB", [64, Tq], BF16)
    QB = Buf("qB")
    stp = kb.pool("st", [128, 512], F32, 2, space="psum")
    pop = kb.pool("po", [128, 512], F32, 2, space="psum")
    psp = kb.pool("pss", [128, 512], F32, 2, space="psum")
    ptp = kb.pool("pt", [128, 512], BF16, 4)
    rsp = kb.pool("rs", [128, 512], F32, 2)
    o1p = kb.pool("o1", [128, 1, 512], F32, 2)
    o2p = kb.pool("o2", [128, 512], F32, 2)
    sqp = kb.pool("sqs", [128, 1, 512], F32, 2)
    obp = kb.pool("ob", [128, 512], BF16, 3)

    qgroups = [(t0, tg, True) for t0 in range(0, Tl, tg)] + ([(Tl, CTX, False)] if with_ctx else [])

    def flash(pairs, scale, t0, tn, kts):
        po, pob = pop()
        pss, pssb = psp()
        prev = None
        n = len(kts)

        def pv_mm(idx, kt, p_t, p_b):
            kb.op("pe", "matmul", out=po[:, 0:tn], lhsT=vt[:, kt, :], rhs=p_t[:, 0:tn], start=(idx == 0), stop=(idx == n - 1),
                  reads=[VTs[kt // PK], p_b], writes=[pob], signal=False)
            kb.op("pe", "matmul", out=pss[:, 0:tn], lhsT=ones_bf[:], rhs=p_t[:, 0:tn], start=(idx == 0), stop=(idx == n - 1),
                  reads=[OB, p_b], writes=[pssb, pob], signal=(idx == n - 1))
        for idx, kt in enumerate(kts):
            st, stb = stp()
            for pi, (kt_, kbuf, nr, qt_, qbuf) in enumerate(pairs):
                kb.op("pe", "matmul", out=st[:, 0:tn], lhsT=kt_[0:nr, kt * 128:(kt + 1) * 128], rhs=qt_[0:nr, t0:t0 + tn],
                      start=(pi == 0), stop=(pi == len(pairs) - 1), reads=[kbuf, qbuf], writes=[stb],
                      signal=(pi == len(pairs) - 1))
            if prev is not None:
                pv_mm(*prev)
            p_t, p_b = ptp()
            kb.op("act", "activation", out=p_t[:, 0:tn], in_=st[:, 0:tn], func=AF.Exp, scale=float(scale), reads=[stb], writes=[p_b])
            prev = (idx, kt, p_t, p_b)
        pv_mm(*prev)
        r_t, r_b = rsp()
        kb.op("dve", "reciprocal", out=r_t[:, 0:tn], in_=pss[:, 0:tn], reads=[pssb], writes=[r_b])
        return po, pob, r_t, r_b

    def qslices(is_lat):
        return list(range(NKT)) if is_lat else list(range(CTX // 128))

    for h in range(H_A):
        kb.dma("sp", kA[0:64, :], kd[h * 64:(h + 1) * 64, :], writes=[KA])
        kb.dma("sp", kB_[:, :], kd[512 + h * 64:512 + (h + 1) * 64, :], writes=[KBb])
        load_v(vd, h)
        kb.dma("sp", qA[0:64, :], qd[h * 64:(h + 1) * 64, :], writes=[QA])
        kb.dma("sp", qB[:, :], qd[512 + h * 64:512 + (h + 1) * 64, :], writes=[QB])
        for (t0, tn, is_lat) in qgroups:
            kts = qslices(is_lat)
            po, pob, r_t, r_b = flash([(kA, KA, 64, qA, QA)], DA ** -0.5, t0, tn, kts)
            o1, o1b = o1p()
            kb.op("dve", "tensor_tensor", out=o1[:, 0, 0:tn], in0=po[:, 0:tn], in1=r_t[:, 0:tn], op=ALU.mult,
                  reads=[pob, r_b], writes=[o1b])
            po, pob, r_t, r_b = flash([(kB_, KBb, 64, qB, QB)], DA ** -0.5, t0, tn, kts)
            o2, o2b = o2p()
            kb.op("dve", "tensor_tensor", out=o2[:, 0:tn], in0=po[:, 0:tn], in1=r_t[:, 0:tn], op=ALU.mult,
                  reads=[pob, r_b], writes=[o2b])
            kb.op("dve", "scalar_tensor_tensor", out=o1[:, 0, 0:tn], in0=o2[:, 0:tn], scalar=nlam, in1=o1[:, 0, 0:tn],
                  op0=ALU.mult, op1=ALU.add, reads=[o2b, LS, o1b], writes=[o1b])
            sq_t, sq_b = sqp()
            rr, rrb = dv.rstd_fm(o1, o1b, 1, tn, DV_A, sq_t, sq_b)
            ob, obb = obp()
            kb.op("dve", "scalar_tensor_tensor", out=ob[:, 0:tn], in0=o1[:, 0, 0:tn], scalar=gsub, in1=rr[:, 0:tn],
                  op0=ALU.mult, op1=ALU.mult, reads=[o1b, LS, rrb], writes=[obb])
            kb.store("sp", oaT[h * 128:(h + 1) * 128, t0:t0 + tn], ob[:, 0:tn], obb)
    kb.dma("sp", kB_[:, :], kr[:, :], writes=[KBb])
    for h in range(H_C):
        kb.dma("sp", kA[:, :], km[h * 128:(h + 1) * 128, :], writes=[KA])
        load_v(vmd, h)
        kb.dma("sp", qA[:, :], qm[h * 192:h * 192 + 128, :], writes=[QA])
        kb.dma("sp", qB[:, :], qm[h * 192 + 128:(h + 1) * 192, :], writes=[QB])
        for (t0, tn, is_lat) in qgroups:
            kts = qslices(is_lat)
            po, pob, r_t, r_b = flash([(kA, KA, 128, qA, QA), (kB_, KBb, 64, qB, QB)], MLA_SCALE, t0, tn, kts)
            ob, obb = obp()
            kb.op("dve", "tensor_tensor", out=ob[:, 0:tn], in0=po[:, 0:tn], in1=r_t[:, 0:tn], op=ALU.mult,
                  reads=[pob, r_b], writes=[obb])
            kb.store("sp", ocT[h * 128:(h + 1) * 128, t0:t0 + tn], ob[:, 0:tn], obb)
    return kb.build()


def lam_init_of(l):
    return 0.8 - 0.6 * math.exp(-0.3 * l)


def gather_fm(res, name, Tl):
    return np.ascontiguousarray(np.concatenate([res[0][name][:, Tl:]] + [r[name][:, :Tl] for r in res], axis=1))


def gather_tm(res, name, Tl):
    return np.ascontiguousarray(np.concatenate([res[0][name][Tl:]] + [r[name][:Tl] for r in res], axis=0))


def run_lb(la_res, S, inp, l, with_ctx):
    Tl = S // NCORE
    nc = build_lb(Tl, S, with_ctx, lam_init_of(l))
    kd = gather_fm(la_res, "qkT", Tl)[1024:]
    vd = gather_tm(la_res, "v", Tl)
    km = gather_fm(la_res, "kmT", Tl)
    kr = gather_fm(la_res, "krT", Tl)
    vmd = gather_tm(la_res, "vm", Tl)
    lq = np.ascontiguousarray(np.broadcast_to(inp["lam_qk"][l].reshape(1, 256), (128, 256))).astype(np.float32)
    subg = np.ascontiguousarray(inp["subln_g"][l].reshape(128, 1)).astype(np.float32)
    Tq = Tl + (CTX if with_ctx else 0)
    in_maps = []
    for i in range(NCORE):
        in_maps.append({"qd": np.ascontiguousarray(la_res[i]["qkT"][:1024, :Tq]), "kd": kd, "vd": vd,
                        "qm": np.ascontiguousarray(la_res[i]["qmT"][:, :Tq]), "km": km, "kr": kr, "vmd": vmd,
                        "lq": lq, "subg": subg, "ones": np.ones((128, 128), np.float32)})
    res = run(nc, in_maps)
    oaT = np.concatenate([r["oaT"][:, :Tl] for r in res] + ([res[0]["oaT"][:, Tl:]] if with_ctx else []), axis=1)
    ocT = np.concatenate([r["ocT"][:, :Tl] for r in res] + ([res[0]["ocT"][:, Tl:]] if with_ctx else []), axis=1)
    return oaT, ocT


def build_lc(S):
    kb = KB()
    TK = S + CTX
    NCH = TK // 128
    NQ = NCH * 4
    xr = kb.dram("xr", [3, 128, TK], F32, kind="ExternalInput")
    zr = kb.dram("zr", [128, TK], F32, kind="ExternalInput")
    dtr = kb.dram("dtr", [128, NQ], F32, kind="ExternalInput")
    dtb = kb.dram("dtb", [128, NQ], F32, kind="ExternalInput")
    alg = kb.dram("alg", [128, NQ], F32, kind="ExternalInput")
    cw_d = kb.dram("cw", [128, 3, 5], F32, kind="ExternalInput")
    cb_d = kb.dram("cb", [128, 3], F32, kind="ExternalInput")
    dsk_d = kb.dram("dsk", [128, 1], F32, kind="ExternalInput")
    tri_d = kb.dram("tri", [4, 128, 128], F32, kind="ExternalInput")
    id_d = kb.dram("ident", [128, 128], BF16, kind="ExternalInput")
    ones_d = kb.dram("ones", [128, 128], F32, kind="ExternalInput")
    ygT = kb.dram("ygT", [128, TK], F32, kind="ExternalOutput")
    yfs = kb.dram("yfs", [128, TK], F32)

    def cload(name, shape, dt, src):
        t = kb.sbuf(name, shape, dt)
        b = Buf(name)
        kb.dma("sp", t[:], src, writes=[b])
        return t, b
    tri, TRI = cload("tri_sb", [128, 4, 128], F32, tri_d.rearrange("k p n -> p k n"))
    ident, IDB = cload("id_sb", [128, 128], BF16, id_d[:])
    ones, ONES = cload("ones_sb", [128, 128], F32, ones_d[:])
    cw, CW = cload("cw_sb", [128, 3, 5], F32, cw_d[:])
    cb, CBB = cload("cb_sb", [128, 3], F32, cb_d[:])
    dsk, DSK = cload("dsk_sb", [128, 1], F32, dsk_d[:])
    dt_all, DT = cload("dt_all", [128, NQ], F32, dtr[:])
    dtb_sb, DTBB = cload("dtb_sb", [128, NQ], F32, dtb[:])
    a_all, AA = cload("a_all", [128, NQ], F32, alg[:])
    TU, TL, XF, XB = (tri[:, k, :] for k in range(4))

    ps = kb.pool("ps", [128, 512], F32, 6, space="psum")
    pst = kb.pool("pst", [128, 256], BF16, 2, space="psum")

    kb.op("dve", "tensor_tensor", out=dt_all[:], in0=dt_all[:], in1=dtb_sb[:], op=ALU.add, reads=[DT, DTBB], writes=[DT])
    kb.op("act", "activation", out=dt_all[:], in_=dt_all[:], func=AF.Exp, reads=[DT], writes=[DT])
    kb.op("act", "activation", out=dt_all[:], in_=dt_all[:], func=AF.Ln, bias=1.0, reads=[DT], writes=[DT])
    kb.op("act", "activation", out=a_all[:], in_=a_all[:], func=AF.Exp, reads=[AA], writes=[AA])
    kb.op("dve", "scalar_tensor_tensor", out=a_all[:], in0=a_all[:], scalar=-1.0, in1=dt_all[:], op0=ALU.mult, op1=ALU.mult,
          reads=[AA, DT], writes=[AA])
    cumF = kb.sbuf("cumF", [128, NQ], F32)
    cumB = kb.sbuf("cumB", [128, NQ], F32)
    dec = kb.sbuf("dec", [128, NQ], F32)
    w_all = kb.sbuf("w_all", [128, NQ], F32)
    TB = Buf("tables")
    for q0 in range(0, NQ, 512):
        qn = min(512, NQ - q0)
        pF, pFb = ps()
        kb.op("pe", "matmul", out=pF[:, 0:qn], lhsT=TU, rhs=a_all[:, q0:q0 + qn], start=True, stop=True, reads=[TRI, AA], writes=[pFb])
        pB, pBb = ps()
        kb.op("pe", "matmul", out=pB[:, 0:qn], lhsT=TL, rhs=a_all[:, q0:q0 + qn], start=True, stop=True, reads=[TRI, AA], writes=[pBb])
        pT, pTb = ps()
        kb.op("pe", "matmul", out=pT[:, 0:qn], lhsT=ones[:], rhs=a_all[:, q0:q0 + qn], start=True, stop=True, reads=[ONES, AA], writes=[pTb])
        kb.op("act", "activation", out=dec[:, q0:q0 + qn], in_=pT[:, 0:qn], func=AF.Copy, reads=[pTb], writes=[TB])
        kb.op("dve", "tensor_tensor", out=cumF[:, q0:q0 + qn], in0=dec[:, q0:q0 + qn], in1=pF[:, 0:qn], op=ALU.subtract,
              reads=[TB, pFb], writes=[TB])
        kb.op("dve", "tensor_tensor", out=cumB[:, q0:q0 + qn], in0=dec[:, q0:q0 + qn], in1=pB[:, 0:qn], op=ALU.subtract,
              reads=[TB, pBb], writes=[TB])
    kb.op("act", "activation", out=cumF[:], in_=cumF[:], func=AF.Exp, reads=[TB], writes=[TB])
    kb.op("act", "activation", out=cumB[:], in_=cumB[:], func=AF.Exp, reads=[TB], writes=[TB])
    kb.op("act", "activation", out=dec[:], in_=dec[:], func=AF.Exp, reads=[TB], writes=[TB])
    v4 = lambda t: t[:].rearrange("p (c k) -> p c k", k=4)
    kb.op("dve", "tensor_tensor", out=v4(w_all)[:, :, 0:2], in0=v4(dt_all)[:, :, 0:2], in1=v4(cumF)[:, :, 0:2], op=ALU.mult,
          reads=[DT, TB], writes=[TB])
    kb.op("dve", "tensor_tensor", out=v4(w_all)[:, :, 2:4], in0=v4(dt_all)[:, :, 2:4], in1=v4(cumB)[:, :, 2:4], op=ALU.mult,
          reads=[DT, TB], writes=[TB])

    xbc = kb.sbuf("xbc", [128, 3, TK], BF16)
    XBC = Buf("xbc")
    SEG = min(2048, S)
    raw = kb.pool("raw", [128, SEG + 4], F32, 2)
    acc = kb.pool("cacc", [128, SEG], F32, 2)
    segs = [(0, CTX, 0, CTX)] + [(CTX + a, SEG, CTX, TK) for a in range(0, S, SEG)]
    ei = 0
    for (a0, n, s0, s1) in segs:
        for k3 in range(3):
            r_t, r_b = raw()
            lo = max(a0 - 2, s0)
            hi = min(a0 + n + 2, s1)
            eng = "dve"
            kb.op("pool", "memset", _args=(r_t[:, 0:n + 4], 0.0), writes=[r_b])
            kb.dma("sp", r_t[:, lo - (a0 - 2):hi - (a0 - 2)], xr[k3, :, lo:hi], writes=[r_b])
            c_t, c_b = acc()
            kb.op(eng, "tensor_scalar", out=c_t[:, 0:n], in0=r_t[:, 2:2 + n], scalar1=cw[:, k3, 2:3], scalar2=cb[:, k3:k3 + 1],
                  op0=ALU.mult, op1=ALU.add, reads=[r_b, CW, CBB], writes=[c_b])
            for k in (0, 1, 3, 4):
                kb.op(eng, "scalar_tensor_tensor", out=c_t[:, 0:n], in0=r_t[:, k:k + n], scalar=cw[:, k3, k:k + 1], in1=c_t[:, 0:n],
                      op0=ALU.mult, op1=ALU.add, reads=[r_b, CW, c_b], writes=[c_b])
            kb.op("act", "activation", out=xbc[:, k3, a0:a0 + n], in_=c_t[:, 0:n], func=AF.Silu, reads=[c_b], writes=[XBC])

    H = kb.sbuf("H", [128, 128], F32)
    HB = Buf("H")
    Hbf = kb.sbuf("Hbf", [128, 128], BF16)
    HBF = Buf("Hbf")
    btm = kb.pool("btm", [128, 128], BF16, 2)
    cbm = kb.pool("cbm", [128, 128], F32, 2)
    xdp = kb.pool("xd", [128, 64], BF16, 4)
    xddp = kb.pool("xdd", [128, 64], BF16, 4)
    yp = kb.pool("Y", [128, 128], F32, 3)
    ep = kb.pool("E", [128, 128], F32, 3)
    erp = kb.pool("Er", [128, 128], F32, 3)
    mtp = kb.pool("MT", [128, 128], BF16, 3)
    csp = kb.pool("Cs", [128, 128], BF16, 3)
    yfo = kb.pool("yfo", [128, 128], F32, 3)
    yfi = kb.pool("yfi", [128, 128], F32, 3)
    zin = kb.pool("zin", [128, 128], F32, 3)
    ygo = kb.pool("ygo", [128, 128], F32, 3)
    YF = [Buf("yf%d" % c) for c in range(NCH)]

    def step(c, d):
        Td, Xd = (TU, XF) if d == 0 else (TL, XB)
        cs = slice(c * 128, (c + 1) * 128)
        px, pxb = pst()
        kb.op("pe", "transpose", out=px[:, 0:128], in_=xbc[:, 0, cs], identity=ident[:], reads=[XBC, IDB], writes=[pxb], signal=False)
        kb.op("pe", "transpose", out=px[:, 128:256], in_=xbc[:, 1, cs], identity=ident[:], reads=[XBC, IDB], writes=[pxb])
        b_t, b_b = btm()
        kb.op("act", "activation", out=b_t[:], in_=px[:, 128:256], func=AF.Copy, reads=[pxb], writes=[b_b])
        pc, pcb = ps()
        kb.op("pe", "matmul", out=pc[:, 0:128], lhsT=xbc[:, 1, cs], rhs=xbc[:, 2, cs], start=True, stop=True, reads=[XBC], writes=[pcb])
        m_t, m_b = cbm()
        kb.op("dve", "tensor_tensor", out=m_t[:], in0=pc[:, 0:128], in1=Td, op=ALU.mult, reads=[pcb, TRI], writes=[m_b])
        py, pyb = ps()
        pss, pssb = ps()
        for h in range(2):
            col = c * 4 + d * 2 + h
            hs = slice(h * 64, (h + 1) * 64)
            xd, xdb = xdp()
            kb.op("dve", "tensor_scalar", out=xd[:], in0=px[:, hs], scalar1=dt_all[:, col:col + 1], scalar2=None, op0=ALU.mult,
                  reads=[pxb, DT], writes=[xdb])
            xdd, xddb = xddp()
            kb.op("dve", "tensor_scalar", out=xdd[:], in0=px[:, hs], scalar1=w_all[:, col:col + 1], scalar2=None, op0=ALU.mult,
                  reads=[pxb, TB], writes=[xddb])
            y_t, y_b = yp()
            kb.op("pool", "tensor_scalar", out=y_t[:], in0=Td, scalar1=a_all[:, col:col + 1], scalar2=None, op0=ALU.mult,
                  reads=[TRI, AA], writes=[y_b])
            pg, pgb = ps()
            kb.op("pe", "matmul", out=pg[:, 0:128], lhsT=Xd, rhs=y_t[:], start=True, stop=True, reads=[TRI, y_b], writes=[pgb])
            pr, prb = ps()
            kb.op("pe", "matmul", out=pr[:, 0:128], lhsT=ones[:], rhs=y_t[:], start=True, stop=True, reads=[ONES, y_b], writes=[prb])
            e_t, e_b = ep()
            kb.op("act", "activation", out=e_t[:], in_=pg[:, 0:128], func=AF.Exp, reads=[pgb], writes=[e_b])
            er_t, er_b = erp()
            kb.op("act", "activation", out=er_t[:], in_=pr[:, 0:128], func=AF.Exp, reads=[prb], writes=[er_b])
            mt, mtb = mtp()
            kb.op("dve", "tensor_tensor", out=mt[:], in0=e_t[:], in1=m_t[:], op=ALU.mult, reads=[e_b, m_b], writes=[mtb])
            cs_t, cs_b = csp()
            kb.op("pool", "tensor_tensor", out=cs_t[:], in0=xbc[:, 2, cs], in1=er_t[:], op=ALU.mult, reads=[XBC, er_b], writes=[cs_b])
            kb.op("pe", "matmul", out=py[hs, 0:128], lhsT=xd[:], rhs=mt[:], start=True, stop=False, reads=[xdb, mtb], writes=[pyb], signal=False)
            kb.op("pe", "matmul", out=py[hs, 0:128], lhsT=Hbf[:, hs], rhs=cs_t[:], start=False, stop=True, reads=[HBF, cs_b], writes=[pyb])
            kb.op("pe", "matmul", out=pss[:, hs], lhsT=b_t[:], rhs=xdd[:], start=True, stop=True, reads=[b_b, xddb], writes=[pssb])
            kb.op("dve", "scalar_tensor_tensor", out=H[:, hs], in0=H[:, hs], scalar=dec[:, col:col + 1], in1=pss[:, hs],
                  op0=ALU.mult, op1=ALU.add, reads=[HB, TB, pssb], writes=[HB])
        kb.op("act", "activation", out=Hbf[:], in_=H[:], func=AF.Copy, reads=[HB], writes=[HBF])
        return py, pyb

    kb.op("dve", "memset", _args=(H[:], 0.0), writes=[HB])
    kb.op("dve", "memset", _args=(Hbf[:], 0.0), writes=[HBF])
    for c in range(NCH):
        py, pyb = step(c, 0)
        o_t, o_b = yfo()
        kb.op("act", "activation", out=o_t[:], in_=py[:, 0:128], func=AF.Copy, reads=[pyb], writes=[o_b])
        kb.dma("sp", yfs[:, c * 128:(c + 1) * 128], o_t[:], reads=[o_b], writes=[YF[c]], sembuf=o_b)
    kb.op("dve", "memset", _args=(H[:], 0.0), writes=[HB])
    kb.op("dve", "memset", _args=(Hbf[:], 0.0), writes=[HBF])
    order = list(range(CTX // 128 - 1, -1, -1)) + list(range(NCH - 1, CTX // 128 - 1, -1))
    for c in order:
        cs = slice(c * 128, (c + 1) * 128)
        i_t, i_b = yfi()
        kb.dma("sp", i_t[:], yfs[:, cs], reads=[YF[c]], writes=[i_b], sembuf=i_b)
        z_t, z_b = zin()
        kb.dma("sp", z_t[:], zr[:, cs], writes=[z_b])
        py, pyb = step(c, 1)
        kb.op("dve", "tensor_tensor", out=i_t[:], in0=i_t[:], in1=py[:, 0:128], op=ALU.add, reads=[i_b, pyb], writes=[i_b])
        kb.op("dve", "scalar_tensor_tensor", out=i_t[:], in0=xbc[:, 0, cs], scalar=dsk[:, 0:1], in1=i_t[:], op0=ALU.mult, op1=ALU.add,
              reads=[XBC, DSK, i_b], writes=[i_b])
        kb.op("act", "activation", out=z_t[:], in_=z_t[:], func=AF.Silu, reads=[z_b], writes=[z_b])
        g_t, g_b = ygo()
        kb.op("pool", "tensor_tensor", out=g_t[:], in0=i_t[:], in1=z_t[:], op=ALU.mult, reads=[i_b, z_b], writes=[g_b])
        kb.store("sp", ygT[:, cs], g_t[:], g_b)
    return kb.build()


def lc_consts():
    i = np.arange(128)
    TU = (i[:, None] <= i[None, :]).astype(np.float32)
    TL = (i[:, None] >= i[None, :]).astype(np.float32)
    XF = (i[None, :] < i[:, None]).astype(np.float32)
    XB = (i[:, None] < i[None, :]).astype(np.float32)
    return {"tri": np.stack([TU, TL, XF, XB]), "ident": np.eye(128, dtype=np.float32).astype(NPBF),
            "ones": np.ones((128, 128), np.float32)}


def run_lc(la_res, S, inp, l):
    Tl = S // NCORE
    TK = S + CTX
    NCH = TK // 128
    nc = build_lc(S)
    misc = gather_fm(la_res, "miscT", Tl)
    consts = lc_consts()
    in_maps = []
    for i in range(NCORE):
        g = i // 4
        ch = slice(i * 128, (i + 1) * 128)
        xrow = misc[M_XBC + i * 128:M_XBC + (i + 1) * 128]
        brow = misc[M_XBC + 1024 + g * 128:M_XBC + 1024 + (g + 1) * 128]
        crow = misc[M_XBC + 1280 + g * 128:M_XBC + 1280 + (g + 1) * 128]
        hsel = [2 * i, 2 * i + 1, 16 + 2 * i, 16 + 2 * i + 1]
        dt4 = misc[M_DT:M_DT + 32][hsel]
        dtr = np.ascontiguousarray(dt4.reshape(4, NCH, 128).transpose(2, 1, 0).reshape(128, NCH * 4))
        v4 = lambda a: np.ascontiguousarray(np.broadcast_to(np.tile(a.reshape(-1)[hsel], NCH)[None, :], (128, NCH * 4))).astype(np.float32)
        cwl = inp["conv_w"][l]
        cwp = np.stack([cwl[:, i * 128:(i + 1) * 128].T, cwl[:, 1024 + g * 128:1024 + (g + 1) * 128].T,
                        cwl[:, 1280 + g * 128:1280 + (g + 1) * 128].T], axis=1)
        cbl = inp["conv_b"][l]
        cbp = np.stack([cbl[i * 128:(i + 1) * 128], cbl[1024 + g * 128:1024 + (g + 1) * 128],
                        cbl[1280 + g * 128:1280 + (g + 1) * 128]], axis=1)
        m = {"xr": np.ascontiguousarray(np.stack([xrow, brow, crow])), "zr": np.ascontiguousarray(misc[M_Z + i * 128:M_Z + (i + 1) * 128]),
             "dtr": dtr, "dtb": v4(inp["dt_bias"][l]), "alg": v4(inp["a_log"][l]),
             "cw": np.ascontiguousarray(cwp).astype(np.float32), "cb": np.ascontiguousarray(cbp).astype(np.float32),
             "dsk": np.ascontiguousarray(np.repeat(inp["d_skip"][l][2 * i:2 * i + 2], 64).reshape(128, 1)).astype(np.float32)}
        m.update(consts)
        in_maps.append(m)
    res = run(nc, in_maps)
    yg = np.concatenate([r["ygT"] for r in res], axis=0)
    return np.ascontiguousarray(np.concatenate([yg[:, CTX:], yg[:, :CTX]], axis=1))


def kernel(**inp):
    import sys
    import time
    t00 = time.time()

    def log(msg):
        print("[kernel] %7.1fs %s" % (time.time() - t00, msg), file=sys.stderr, flush=True)
    inp = {k: np.asarray(v) for k, v in inp.items()}
    x = inp["x"]
    S = x.shape[1]
    L = inp["w_in"].shape[0]
    mods, wbf = run_l0(inp, L)
    log("L0 done")
    sT_lat = np.ascontiguousarray(x[0].T)
    sT_ctx = np.ascontiguousarray(inp["ctx"][0].T)
    for l in range(L):
        last = (l == L - 1)
        la = run_la(sT_lat, sT_ctx, mods[l], inp, wbf, l)
        log("LA%d done" % l)
        oaT, ocT = run_lb(la, S, inp, l, with_ctx=not last)
        log("LB%d done" % l)
        ygT = run_lc(la, S, inp, l)
        log("LC%d done" % l)
        del la
        sT_lat, sT_ctx = run_ld(sT_lat, sT_ctx, oaT, ygT, ocT, mods[l], inp, wbf, l, last)
        log("LD%d done" % l)
    return np.ascontiguousarray(sT_lat.T)[None].astype(np.float32)
```

```python
import contextlib
import math
import numpy as np
import ml_dtypes
import concourse.bass as bass
import concourse.mybir as mybir
from concourse.bass_utils import run_bass_kernel_spmd

F32 = mybir.dt.float32
BF16 = mybir.dt.bfloat16
AF = mybir.ActivationFunctionType
ALU = mybir.AluOpType
AX = mybir.AxisListType
NPBF = ml_dtypes.bfloat16

NCORE = 8
D = 2048
KC = 16
CTX = 256
GRID_W = 64
EPS = 1e-6
H_A, DA, DV_A, W_A = 8, 64, 128, 1024
H_M, P_M, D_INNER, N_STATE, G_M, D_CONV = 16, 64, 1024, 128, 2, 5
CONV_CH = D_INNER + 2 * G_M * N_STATE
H_C, D_NOPE, D_ROPE, D_VC, Q_LORA, KV_LORA, W_C = 8, 128, 64, 128, 512, 256, 1024
MLA_SCALE = (D_NOPE + D_ROPE) ** -0.5
D_FF = 4 * D
IN_SIZES = (2 * H_A * DA, 2 * H_A * DA, W_A, D_INNER, CONV_CH, 2 * H_M, Q_LORA, KV_LORA, D_ROPE, 3 * D)
P_IN = sum(IN_SIZES)
OFF = np.concatenate([[0], np.cumsum(IN_SIZES)]).tolist()
N_MISC = OFF[9] - OFF[3]
COLPERM = np.concatenate([np.arange(0, OFF[5]), np.arange(OFF[6], OFF[9]), np.arange(OFF[5], OFF[6])])
C_CQ, C_CKV, C_KR, C_DT = 5632, 6144, 6400, 6464
M_Z, M_XBC, M_CQ, M_CKV, M_KR, M_DT = 0, 1024, 2560, 3072, 3328, 3392


class Buf:
    __slots__ = ("name", "lw", "rd", "dsem", "dcnt", "excl")

    def __init__(self, name, excl=False):
        self.name = name
        self.excl = excl
        self.lw = None
        self.rd = []
        self.dsem = None
        self.dcnt = 0


class KB:
    ENG = ("pe", "act", "dve", "pool", "sp")

    def __init__(self):
        self.nc = bass.Bass("TRN2", target_bir_lowering=False)
        self.prog = {e: [] for e in self.ENG}
        self.cnt = {e: 0 for e in self.ENG}
        self.waited = {e: {} for e in self.ENG}
        self.semkeys = ["E" + e for e in self.ENG]
        self.final_waits = []
        self.n_instr = 0
        self._pools = {}
        self._uid = 0

    def dram(self, name, shape, dt, kind="Internal"):
        return self.nc.dram_tensor(name, list(shape), dt, kind=kind).ap()

    def sbuf(self, name, shape, dt):
        return self.nc.alloc_sbuf_tensor(name, list(shape), dt)

    def psum(self, name, shape, dt=F32):
        return self.nc.alloc_psum_tensor(name, list(shape), dt)

    def pool(self, name, shape, dt, n, space="sbuf"):
        tiles = []
        for i in range(n):
            t = (self.sbuf if space == "sbuf" else self.psum)("%s%d" % (name, i), shape, dt)
            tiles.append((t, Buf("%s%d" % (name, i), excl=(space == "psum"))))
        st = {"i": 0}

        def nxt():
            r = tiles[st["i"] % n]
            st["i"] += 1
            return r
        return nxt

    def _deps(self, eng, reads, writes):
        me = "E" + eng
        deps = []
        for b in reads:
            if b.lw is not None:
                deps.append(b.lw)
            if b.excl:
                for r in b.rd:
                    if r[0] != me:
                        deps.append(r)
        for b in writes:
            if b.lw is not None:
                deps.append(b.lw)
            for r in b.rd:
                deps.append(r)
        return deps

    def _emit_waits(self, eng, deps):
        need = {}
        for (k, v) in deps:
            if k == "Epe" and eng == "pe":
                continue
            if need.get(k, 0) < v:
                need[k] = v
        w = self.waited[eng]
        for k, v in need.items():
            if w.get(k, 0) < v:
                w[k] = v
                self.prog[eng].append(("wait", k, v))

    def op(self, eng, meth, reads=(), writes=(), signal=True, **kw):
        fn = (meth, kw)
        deps = self._deps(eng, reads, writes)
        self._emit_waits(eng, deps)
        k = "E" + eng
        if signal:
            self.cnt[eng] += 1
            v = self.cnt[eng]
        else:
            v = self.cnt[eng] + 1
        self.prog[eng].append(("op", fn, k if signal else None, 1))
        for b in reads:
            b.rd.append((k, v))
        for b in writes:
            b.lw = (k, v)
            b.rd = []
        self.n_instr += 1

    def dma(self, q, out, in_, reads=(), writes=(), sembuf=None, **kw):
        deps = self._deps("dma", reads, writes)
        self._emit_waits(q, deps)
        sb = sembuf if sembuf is not None else (list(writes) + list(reads))[0]
        if sb.dsem is None:
            sb.dsem = "D%d" % len(self.semkeys)
            self.semkeys.append(sb.dsem)
        sb.dcnt += 16
        k, v = sb.dsem, sb.dcnt
        kw = dict(kw)
        kw["out"] = out
        kw["in_"] = in_
        self.prog[q].append(("op", ("dma_start", kw), k, 16))
        for b in reads:
            b.rd.append((k, v))
        for b in writes:
            b.lw = (k, v)
            b.rd = []
        self.n_instr += 1
        return (k, v)

    def store(self, q, out, in_, src_buf):
        d = self.dma(q, out, in_, reads=[src_buf])
        self.final_waits.append(d)
        return d

    def build(self):
        nc = self.nc
        fin = {}
        for (k, v) in self.final_waits:
            fin[k] = max(fin.get(k, 0), v)
        for k, v in fin.items():
            self.prog["sp"].append(("wait", k, v))
        sems = {}
        with contextlib.ExitStack() as st:
            for k in self.semkeys:
                sems[k] = st.enter_context(nc.semaphore(k))
            block = st.enter_context(nc.Block())

            def runner(prog):
                def f(e):
                    for it in prog:
                        if it[0] == "wait":
                            e.wait_ge(sems[it[1]], it[2])
                        else:
                            m, kw = it[1]
                            a = kw.pop("_args", ())
                            ins = getattr(e, m)(*a, **kw)
                            if it[2] is not None:
                                ins.then_inc(sems[it[2]], it[3])
                return f
            block.tensor(runner(self.prog["pe"]))
            block.scalar(runner(self.prog["act"]))
            block.vector(runner(self.prog["dve"]))
            block.gpsimd(runner(self.prog["pool"]))
            block.sync(runner(self.prog["sp"]))
        return nc


def run(nc, in_maps):
    res = run_bass_kernel_spmd(nc, in_maps, core_ids=list(range(NCORE)))
    return res.results


CAST_W = ("w_in", "w_o_diff", "w_o_ssm", "w_uq", "w_ukv", "w_o_mla", "w_out", "w_mlp1", "w_mlp2")


def build_l0(L, wshapes):
    kb = KB()
    ncol = 6 * D // NCORE
    cc = kb.dram("cc", [D, 2], F32, kind="ExternalInput")
    wada = kb.dram("wada", [L, D, ncol], F32, kind="ExternalInput")
    bada = kb.dram("bada", [L, 2, ncol], F32, kind="ExternalInput")
    mods = kb.dram("mods", [L, 2, ncol], F32, kind="ExternalOutput")
    stg = kb.pool("stg", [128, 4096], F32, 3)
    cvt = kb.pool("cvt", [128, 4096], BF16, 3)
    ci = 0
    for name in CAST_W:
        rows, cols = wshapes[name]
        src = kb.dram(name, [rows, cols], F32, kind="ExternalInput")
        dst = kb.dram(name + "_bf", [rows, cols], BF16, kind="ExternalOutput")
        for r0 in range(0, rows, 128):
            pr = min(128, rows - r0)
            for c0 in range(0, cols, 4096):
                cw = min(4096, cols - c0)
                s_t, s_b = stg()
                c_t, c_b = cvt()
                kb.dma("sp", s_t[0:pr, 0:cw], src[r0:r0 + pr, c0:c0 + cw], writes=[s_b])
                eng = ("dve", "pool")[ci % 2]
                ci += 1
                kb.op(eng, "tensor_copy", out=c_t[0:pr, 0:cw], in_=s_t[0:pr, 0:cw], reads=[s_b], writes=[c_b])
                kb.store("act", dst[r0:r0 + pr, c0:c0 + cw], c_t[0:pr, 0:cw], c_b)
    cc_sb = kb.sbuf("cc_sb", [128, KC, 2], F32)
    CCB = Buf("cc")
    sc_sb = kb.sbuf("sc_sb", [128, KC, 2], F32)
    SCB = Buf("sc")
    kb.dma("sp", cc_sb[:], cc.rearrange("(j p) t -> p j t", p=128), writes=[CCB])
    kb.op("act", "activation", out=sc_sb[:], in_=cc_sb[:], func=AF.Silu, reads=[CCB], writes=[SCB])
    wst = kb.pool("wst", [128, KC, 512], F32, 2)
    psm = kb.pool("psm", [128, 512], F32, 2, space="psum")
    bsb = kb.pool("bsb", [2, 512], F32, 2)
    osb = kb.pool("osb", [2, 512], F32, 2)
    for l in range(L):
        for c0 in range(0, ncol, 512):
            w_t, w_b = wst()
            kb.dma("sp", w_t[:], wada[l, :, c0:c0 + 512].rearrange("(j p) n -> p j n", p=128), writes=[w_b])
            b_t, b_b = bsb()
            kb.dma("sp", b_t[:], bada[l, :, c0:c0 + 512], writes=[b_b])
            p_t, p_b = psm()
            for j in range(KC):
                kb.op("pe", "matmul", out=p_t[0:2, :], lhsT=sc_sb[:, j, :], rhs=w_t[:, j, :], start=(j == 0), stop=(j == KC - 1),
                      reads=[SCB, w_b], writes=[p_b], signal=(j == KC - 1))
            o_t, o_b = osb()
            kb.op("dve", "tensor_tensor", out=o_t[:], in0=p_t[0:2, :], in1=b_t[:], op=ALU.add, reads=[p_b, b_b], writes=[o_b])
            kb.store("sp", mods[l, :, c0:c0 + 512], o_t[:], o_b)
    return kb.build()


def run_l0(inp, L):
    wsl, wshapes = {}, {}
    for name in CAST_W:
        w = inp[name]
        flat = w.reshape(-1, w.shape[-1])
        rows = flat.shape[0] // NCORE
        wshapes[name] = (rows, flat.shape[1])
        wsl[name] = [np.ascontiguousarray(flat[i * rows:(i + 1) * rows]) for i in range(NCORE)]
    nc = build_l0(L, wshapes)
    ncol = 6 * D // NCORE
    cc = np.ascontiguousarray(np.stack([inp["c"][0], inp["c_ctx"]], axis=1))
    in_maps = []
    for i in range(NCORE):
        m = {"cc": cc,
             "wada": np.ascontiguousarray(inp["w_ada"][:, :, i * ncol:(i + 1) * ncol]),
             "bada": np.ascontiguousarray(np.repeat(inp["b_ada"][:, None, i * ncol:(i + 1) * ncol], 2, axis=1))}
        for name in CAST_W:
            m[name] = wsl[name][i]
        in_maps.append(m)
    res = run(nc, in_maps)
    mods = np.concatenate([r["mods"] for r in res], axis=2)
    wbf = {}
    for name in CAST_W:
        full = np.concatenate([r[name + "_bf"] for r in res], axis=0)
        wbf[name] = full.reshape(inp[name].shape)
    return mods, wbf


def groups_of(Tl, with_ctx=True):
    gs = []
    tg = min(512, Tl)
    for t0 in range(0, Tl, tg):
        gs.append((t0, tg, True))
    if with_ctx:
        gs.append((Tl, CTX, False))
    return gs


class Dev:
    def __init__(self, kb, nps=8):
        self.kb = kb
        self.ps = kb.pool("ps", [128, 512], F32, nps, space="psum")
        self.ones_d = kb.dram("ones", [128, 128], F32, kind="ExternalInput")
        self.ones = kb.sbuf("ones_sb", [128, 128], F32)
        self.ONES = Buf("ones")
        kb.dma("sp", self.ones[:], self.ones_d[:], writes=[self.ONES])
        self.acc = kb.pool("nacc", [128, 512], F32, 2)
        self.rstd = kb.pool("nrstd", [128, 512], F32, 2)
        self.ntmp = kb.pool("ntmp", [128, 512], F32, 3)
        self.epsb = kb.sbuf("epsb", [128, 1], F32)
        self.EPSB = Buf("epsb")
        kb.op("dve", "memset", _args=(self.epsb[:], EPS), writes=[self.EPSB])

    def rstd_fm(self, src, SRC, kcn, tn, dn, sq, SQ):
        kb = self.kb
        kb.op("act", "activation", out=sq[:, 0:kcn, 0:tn], in_=src[:, 0:kcn, 0:tn], func=AF.Square,
              reads=[SRC], writes=[SQ])
        a_t, a_b = self.acc()
        if kcn > 1:
            kb.op("dve", "tensor_reduce", out=a_t[:, 0:tn],
                                                   in_=sq[:, 0:kcn, 0:tn].rearrange("p j t -> p t j"),
                                                   axis=AX.X, op=ALU.add, reads=[SQ], writes=[a_b])
        else:
            kb.op("dve", "tensor_copy", out=a_t[:, 0:tn], in_=sq[:, 0, 0:tn], reads=[SQ], writes=[a_b])
        p_t, p_b = self.ps()
        kb.op("pe", "matmul", out=p_t[:, 0:tn], lhsT=self.ones[:], rhs=a_t[:, 0:tn], start=True, stop=True,
              reads=[self.ONES, a_b], writes=[p_b])
        r_t, r_b = self.rstd()
        kb.op("act", "activation", out=r_t[:, 0:tn], in_=p_t[:, 0:tn], func=AF.Sqrt,
                                            scale=1.0 / dn, bias=self.epsb[:],
              reads=[p_b, self.EPSB], writes=[r_b])
        kb.op("dve", "reciprocal", out=r_t[:, 0:tn], in_=r_t[:, 0:tn], reads=[r_b], writes=[r_b])
        return r_t, r_b

    def norm_fm(self, src, SRC, kcn, tn, dn, gs, shift, PV, dst, DST, sq, SQ):
        kb = self.kb
        r_t, r_b = self.rstd_fm(src, SRC, kcn, tn, dn, sq, SQ)
        for j in range(kcn):
            if shift is None:
                kb.op("dve", "scalar_tensor_tensor",
                    out=dst[:, j, 0:tn], in0=src[:, j, 0:tn], scalar=gs[:, j:j + 1], in1=r_t[:, 0:tn],
                    op0=ALU.mult, op1=ALU.mult, reads=[SRC, PV, r_b], writes=[DST])
            else:
                t_t, t_b = self.ntmp()
                kb.op("dve", "scalar_tensor_tensor",
                    out=t_t[:, 0:tn], in0=src[:, j, 0:tn], scalar=gs[:, j:j + 1], in1=r_t[:, 0:tn],
                    op0=ALU.mult, op1=ALU.mult, reads=[SRC, PV, r_b], writes=[t_b])
                kb.op("act", "activation",
                    out=dst[:, j, 0:tn], in_=t_t[:, 0:tn], func=AF.Identity, bias=shift[:, j:j + 1],
                    reads=[t_b, PV], writes=[DST])


def pvec(v):
    v = np.asarray(v, np.float32).reshape(-1, 128)
    return np.ascontiguousarray(v.T)


def rope_tables(pos):
    row = (pos // GRID_W).astype(np.float32)
    col = (pos % GRID_W).astype(np.float32)
    nf = 16
    inv = (1.0 / (10000.0 ** (np.arange(nf, dtype=np.float32) / nf))).astype(np.float32)
    ang = np.concatenate([row[:, None] * inv, col[:, None] * inv], axis=-1).astype(np.float32)
    c, s = np.cos(ang).astype(np.float32), np.sin(ang).astype(np.float32)
    C = np.empty((128, len(pos)), np.float32)
    S = np.empty((128, len(pos)), np.float32)
    for r in range(128):
        C[r] = c[:, r % 32]
        S[r] = (-s[:, r % 32]) if (r % 64) < 32 else s[:, r % 32]
    return C, S


def rot_perm():
    R = np.zeros((128, 128), np.float32)
    for m in range(128):
        k = m + 32 if (m % 64) < 32 else m - 32
        R[k, m] = 1.0
    return R.astype(NPBF)


NA = OFF[9]
NV_LA = 5 * KC + 4 + 2


DBG = {}


def build_la(Tl):
    kb = KB()
    dv = Dev(kb)
    T = Tl + CTX
    sT = kb.dram("sT", [D, T], F32, kind="ExternalInput")
    pv_d = kb.dram("pv", [128, NV_LA], F32, kind="ExternalInput")
    win = kb.dram("win", [D, NA], BF16, kind="ExternalInput")
    wuq_d = kb.dram("wuq", [Q_LORA, 1536], BF16, kind="ExternalInput")
    wukv_d = kb.dram("wukv", [KV_LORA, 2048], BF16, kind="ExternalInput")
    rc_d = kb.dram("ropeC", [128, Tl], F32, kind="ExternalInput")
    rs_d = kb.dram("ropeS", [128, Tl], F32, kind="ExternalInput")
    rp_d = kb.dram("rperm", [128, 128], BF16, kind="ExternalInput")
    qkT = kb.dram("qkT", [2048, T], BF16, kind="ExternalOutput")
    v_o = kb.dram("v", [T, 1024], BF16, kind="ExternalOutput")
    miscT = kb.dram("miscT", [N_MISC, T], F32, kind="ExternalOutput")
    qmT = kb.dram("qmT", [1536, T], BF16, kind="ExternalOutput")
    kmT = kb.dram("kmT", [1024, T], BF16, kind="ExternalOutput")
    vm_o = kb.dram("vm", [T, 1024], BF16, kind="ExternalOutput")
    krT = kb.dram("krT", [64, T], BF16, kind="ExternalOutput")

    pv = kb.sbuf("pv_sb", [128, NV_LA], F32)
    PV = Buf("pv")
    kb.dma("sp", pv[:], pv_d[:], writes=[PV])
    gsl = kb.sbuf("gsl", [128, KC], F32)
    gsc = kb.sbuf("gsc", [128, KC], F32)
    GS = Buf("gs")
    g1, shl, scl, shc, scc = (pv[:, i * KC:(i + 1) * KC] for i in range(5))
    qg = pv[:, 5 * KC:5 * KC + 4]
    kvg = pv[:, 5 * KC + 4:5 * KC + 6]
    kb.op("dve", "scalar_tensor_tensor", out=gsl[:], in0=scl, scalar=1.0, in1=g1, op0=ALU.add, op1=ALU.mult,
          reads=[PV], writes=[GS])
    kb.op("dve", "scalar_tensor_tensor", out=gsc[:], in0=scc, scalar=1.0, in1=g1, op0=ALU.add, op1=ALU.mult,
          reads=[PV], writes=[GS])
    ropeC = kb.sbuf("ropeC_sb", [128, Tl], F32)
    ropeS = kb.sbuf("ropeS_sb", [128, Tl], F32)
    RTC = Buf("rtc")
    RTS = Buf("rts")
    kb.dma("sp", ropeC[:], rc_d[:], writes=[RTC])
    kb.dma("sp", ropeS[:], rs_d[:], writes=[RTS])
    RT2 = Buf("rt2b")
    rperm = kb.sbuf("rperm_sb", [128, 128], BF16)
    kb.dma("sp", rperm[:], rp_d[:], writes=[RT2])
    wuq = kb.sbuf("wuq_sb", [128, 4, 1536], BF16)
    wukv = kb.sbuf("wukv_sb", [128, 2, 2048], BF16)
    WU = Buf("wu")
    WU2 = Buf("wu2")
    kb.dma("sp", wuq[:], wuq_d.rearrange("(j p) n -> p j n", p=128), writes=[WU])
    kb.dma("sp", wukv[:], wukv_d.rearrange("(j p) n -> p j n", p=128), writes=[WU2])

    sTg = kb.sbuf("sTg", [128, KC, 512], F32)
    STG = Buf("sTg")
    sq = kb.sbuf("sq", [128, KC, 512], F32)
    SQ = Buf("sq")
    hT = kb.sbuf("hT", [128, KC, 512], BF16)
    HT = Buf("hT")
    wblk = kb.pool("wblk", [128, KC, 512], BF16, 2)
    mla_in = kb.sbuf("mla_in", [128, 7, 512], F32)
    MLI = Buf("mli")
    mla_n = kb.sbuf("mla_n", [128, 7, 512], BF16)
    MLN = Buf("mln")
    stf = kb.pool("stf", [128, 512], F32, 3)
    stb = kb.pool("stb", [128, 512], BF16, 4)
    rt1 = kb.pool("rt1", [128, 512], F32, 2)
    rt2 = kb.pool("rt2", [128, 512], F32, 2)
    qb = kb.pool("qb", [128, 512], BF16, 2)

    def rope_epi(p_t, p_b, nr, t0, tn, dst_ap):
        q_t, q_b = qb()
        kb.op("act", "activation", out=q_t[0:nr, 0:tn], in_=p_t[0:nr, 0:tn], func=AF.Copy,
              reads=[p_b], writes=[q_b])
        p2, p2b = dv.ps()
        kb.op("pe", "matmul", out=p2[0:nr, 0:tn], lhsT=rperm[0:nr, 0:nr], rhs=q_t[0:nr, 0:tn], start=True, stop=True,
              reads=[RT2, q_b], writes=[p2b])
        a_t, a_b = rt1()
        kb.op("dve", "tensor_tensor", out=a_t[0:nr, 0:tn], in0=p_t[0:nr, 0:tn], in1=ropeC[0:nr, t0:t0 + tn], op=ALU.mult,
              reads=[p_b, RTC], writes=[a_b])
        b_t, b_b = rt2()
        kb.op("dve", "tensor_tensor", out=b_t[0:nr, 0:tn], in0=p2[0:nr, 0:tn], in1=ropeS[0:nr, t0:t0 + tn], op=ALU.mult,
              reads=[p2b, RTS], writes=[b_b])
        o_t, o_b = stb()
        kb.op("pool", "tensor_tensor", out=o_t[0:nr, 0:tn], in0=a_t[0:nr, 0:tn], in1=b_t[0:nr, 0:tn], op=ALU.add,
              reads=[a_b, b_b], writes=[o_b])
        kb.store("sp", dst_ap, o_t[0:nr, 0:tn], o_b)

    def copy_epi(p_t, p_b, nr, nc_, dst_ap, bf, eng="act"):
        o_t, o_b = (stb if bf else stf)()
        if eng == "act":
            kb.op("act", "activation", out=o_t[0:nr, 0:nc_], in_=p_t[0:nr, 0:nc_], func=AF.Copy,
                  reads=[p_b], writes=[o_b])
        else:
            kb.op("dve", "tensor_copy", out=o_t[0:nr, 0:nc_], in_=p_t[0:nr, 0:nc_], reads=[p_b], writes=[o_b])
        kb.store("sp", dst_ap, o_t[0:nr, 0:nc_], o_b)

    for (t0, tn, is_lat) in groups_of(Tl)[:DBG.get('groups', 99)]:
        kb.dma("sp", sTg[:, :, 0:tn], sT[:, t0:t0 + tn].rearrange("(j p) t -> p j t", p=128), writes=[STG])
        dv.norm_fm(sTg, STG, KC, tn, D, gsl if is_lat else gsc, shl if is_lat else shc, GS, hT, HT, sq, SQ)
        for c0 in range(0, NA, 512)[DBG.get('b0', 0):DBG.get('b1', 99)]:
            cw = min(512, NA - c0)
            w_t, w_b = wblk()
            kb.dma("sp", w_t[:, :, 0:cw], win[:, c0:c0 + cw].rearrange("(j p) n -> p j n", p=128), writes=[w_b])
            if 2048 <= c0 < 3072:
                for tt in range(0, tn, 128):
                    p_t, p_b = dv.ps()
                    for j in range(KC):
                        kb.op("pe", "matmul", out=p_t[:, 0:cw], lhsT=hT[:, j, tt:tt + 128], rhs=w_t[:, j, 0:cw], start=(j == 0), stop=(j == KC - 1),
                            reads=[HT, w_b], writes=[p_b], signal=(j == KC - 1))
                    copy_epi(p_t, p_b, 128, cw, v_o[t0 + tt:t0 + tt + 128, c0 - 2048:c0 - 2048 + cw], True,
                             eng=("act", "dve")[(tt // 128) % 2])
                continue
            chunks = [(cc, min(128, cw - cc)) for cc in range(0, cw, 128)]
            if c0 + cw == NA:
                chunks = chunks[:-1] + [(C_KR - c0, 64), (C_DT - c0, 32)]
            for (cc, nr) in chunks:
                col = c0 + cc
                p_t, p_b = dv.ps()
                for j in range(KC):
                    kb.op("pe", "matmul", out=p_t[0:nr, 0:tn], lhsT=w_t[:, j, cc:cc + nr], rhs=hT[:, j, 0:tn], start=(j == 0), stop=(j == KC - 1),
                        reads=[HT, w_b], writes=[p_b], signal=(j == KC - 1))
                if col < 2048:
                    if is_lat:
                        rope_epi(p_t, p_b, 128, t0, tn, qkT[col:col + 128, t0:t0 + tn])
                    else:
                        copy_epi(p_t, p_b, 128, tn, qkT[col:col + 128, t0:t0 + tn], True)
                else:
                    mrow = col - 3072
                    if C_CQ <= col < C_DT:
                        ci = (col - C_CQ) // 128
                        kb.op("dve", "tensor_copy", out=mla_in[0:nr, ci, 0:tn], in_=p_t[0:nr, 0:tn],
                              reads=[p_b], writes=[MLI])
                    copy_epi(p_t, p_b, nr, tn, miscT[mrow:mrow + nr, t0:t0 + tn], False)
        if DBG.get('nomla'):
            continue
        dv.norm_fm(mla_in[:, 0:4, :], MLI, 4, tn, Q_LORA, qg, None, PV, mla_n[:, 0:4, :], MLN, sq, SQ)
        dv.norm_fm(mla_in[:, 4:6, :], MLI, 2, tn, KV_LORA, kvg, None, PV, mla_n[:, 4:6, :], MLN, sq, SQ)
        for h in range(H_C):
            for (c_lo, nr, is_rope) in ((h * 192, 128, False), (h * 192 + 128, 64, True)):
                p_t, p_b = dv.ps()
                for j in range(4):
                    kb.op("pe", "matmul", out=p_t[0:nr, 0:tn], lhsT=wuq[:, j, c_lo:c_lo + nr], rhs=mla_n[:, j, 0:tn], start=(j == 0), stop=(j == 3),
                        reads=[MLN, WU], writes=[p_b], signal=(j == 3))
                if is_rope and is_lat:
                    rope_epi(p_t, p_b, nr, t0, tn, qmT[c_lo:c_lo + nr, t0:t0 + tn])
                else:
                    copy_epi(p_t, p_b, nr, tn, qmT[c_lo:c_lo + nr, t0:t0 + tn], True)
            p_t, p_b = dv.ps()
            for j in range(2):
                kb.op("pe", "matmul", out=p_t[:, 0:tn], lhsT=wukv[:, j, h * 256:h * 256 + 128], rhs=mla_n[:, 4 + j, 0:tn], start=(j == 0), stop=(j == 1),
                    reads=[MLN, WU2], writes=[p_b], signal=(j == 1))
            copy_epi(p_t, p_b, 128, tn, kmT[h * 128:(h + 1) * 128, t0:t0 + tn], True, eng="dve")
        for tt in range(0, tn, 128):
            for half in range(2):
                p_t, p_b = dv.ps()
                for hh in range(4):
                    h = half * 4 + hh
                    for j in range(2):
                        kb.op("pe", "matmul", out=p_t[:, hh * 128:(hh + 1) * 128], lhsT=mla_n[:, 4 + j, tt:tt + 128],
                            rhs=wukv[:, j, h * 256 + 128:h * 256 + 256], start=(j == 0), stop=(j == 1),
                            reads=[MLN, WU2], writes=[p_b], signal=(hh == 3 and j == 1))
                copy_epi(p_t, p_b, 128, 512, vm_o[t0 + tt:t0 + tt + 128, half * 512:(half + 1) * 512], True)
        krp, krb = dv.ps()
        kb.op("dve", "tensor_copy", out=krp[0:64, 0:tn], in_=mla_in[0:64, 6, 0:tn], reads=[MLI], writes=[krb])
        if is_lat:
            rope_epi(krp, krb, 64, t0, tn, krT[:, t0:t0 + tn])
        else:
            copy_epi(krp, krb, 64, tn, krT[:, t0:t0 + tn], True)
    return kb.build()


def la_consts(Tl, core):
    pos = np.arange(core * Tl, (core + 1) * Tl)
    C, S = rope_tables(pos)
    return {"ropeC": C, "ropeS": S, "rperm": rot_perm(), "ones": np.ones((128, 128), np.float32)}


def run_la(sT_lat, sT_ctx, mods_l, inp, wbf, l):
    S = sT_lat.shape[1]
    Tl = S // NCORE
    nc = build_la(Tl)
    mx = mods_l[0].reshape(6, D)
    mc = mods_l[1].reshape(6, D)
    pv = np.concatenate([pvec(inp["norm1_g"][l]), pvec(mx[0]), pvec(mx[1]), pvec(mc[0]), pvec(mc[1]),
                         pvec(inp["q_norm_g"][l]), pvec(inp["kv_norm_g"][l])], axis=1)
    win = np.ascontiguousarray(wbf["w_in"][l][:, COLPERM])
    in_maps = []
    for i in range(NCORE):
        m = {"sT": np.ascontiguousarray(np.concatenate([sT_lat[:, i * Tl:(i + 1) * Tl], sT_ctx], axis=1)),
             "pv": pv, "win": win, "wuq": wbf["w_uq"][l], "wukv": wbf["w_ukv"][l]}
        m.update(la_consts(Tl, i))
        in_maps.append(m)
    return run(nc, in_maps)


TG_LD = 256
NV_LD = 15 * KC + 8
GATE0 = OFF[9]


def build_ld(Tl, last):
    kb = KB()
    dv = Dev(kb)
    T = Tl if last else Tl + CTX
    sT = kb.dram("sT", [D, T], F32, kind="ExternalInput")
    oaT = kb.dram("oaT", [1024, T], BF16, kind="ExternalInput")
    ygT = kb.dram("ygT", [1024, T], F32, kind="ExternalInput")
    ocT = kb.dram("ocT", [1024, T], BF16, kind="ExternalInput")
    pv_d = kb.dram("pv", [128, NV_LD], F32, kind="ExternalInput")
    wg = kb.dram("wg", [D, 3 * D], BF16, kind="ExternalInput")
    wod = kb.dram("wod", [1024, D], BF16, kind="ExternalInput")
    wos = kb.dram("wos", [1024, D], BF16, kind="ExternalInput")
    wom = kb.dram("wom", [1024, D], BF16, kind="ExternalInput")
    wout = kb.dram("wout", [D, D], BF16, kind="ExternalInput")
    w1 = kb.dram("w1", [D, D_FF], BF16, kind="ExternalInput")
    w2 = kb.dram("w2", [D_FF, D], BF16, kind="ExternalInput")
    sO = kb.dram("sO", [D, T], F32, kind="ExternalOutput")

    pv = kb.sbuf("pv_sb", [128, NV_LD], F32)
    PV = Buf("pv")
    kb.dma("sp", pv[:], pv_d[:], writes=[PV])
    col = lambda i: pv[:, i * KC:(i + 1) * KC]
    ssg = pv[:, 15 * KC:15 * KC + 8]
    gs = kb.sbuf("gs_sb", [128, 4, KC], F32)
    GS = Buf("gs")
    for k, (sc_i, g_i) in enumerate(((2, 0), (4, 0), (9, 7), (11, 7))):
        kb.op("dve", "scalar_tensor_tensor", out=gs[:, k, :], in0=col(sc_i), scalar=1.0, in1=col(g_i),
              op0=ALU.add, op1=ALU.mult, reads=[PV], writes=[GS])

    tg = min(TG_LD, Tl)
    sTg = kb.sbuf("sTg", [128, KC, tg], F32)
    STG = Buf("sTg")
    hT = kb.sbuf("hT", [128, KC, tg], BF16)
    HT = Buf("hT")
    oa = kb.sbuf("oa", [128, 8, tg], BF16)
    OA = Buf("oa")
    oc = kb.sbuf("oc", [128, 8, tg], BF16)
    OC = Buf("oc")
    yg = kb.sbuf("yg", [128, 8, tg], F32)
    YG = Buf("yg")
    ysn = kb.sbuf("ysn", [128, 8, tg], BF16)
    YSN = Buf("ysn")
    mg = kb.sbuf("mg", [128, KC, tg], BF16)
    MG = Buf("mg")
    uT = kb.pool("uT", [128, KC, tg], BF16, 2)
    wt = kb.pool("wt", [128, KC, 256], BF16, 6)
    sg = kb.pool("sg", [128, tg], F32, 3)
    tt_ = kb.pool("tt", [128, tg], F32, 3)
    t01 = kb.pool("t01", [128, tg], F32, 2)
    rl = kb.pool("rl", [128, tg], F32, 3)
    fo = kb.pool("fo", [128, tg], F32, 2)

    def wload(src, r0, nkc, c0, ncol=256):
        w_t, w_b = wt()
        kb.dma("sp", w_t[:, 0:nkc, 0:ncol], src[r0:r0 + nkc * 128, c0:c0 + ncol].rearrange("(j p) n -> p j n", p=128),
               writes=[w_b])
        return w_t, w_b

    def mmgroup(p_t, p_b, w_t, w_b, nkc, cc, x, XB, tn):
        for j in range(nkc):
            kb.op("pe", "matmul", out=p_t[:, 0:tn], lhsT=w_t[:, j, cc:cc + 128], rhs=x[:, j, 0:tn],
                  start=(j == 0), stop=(j == nkc - 1), reads=[w_b, XB], writes=[p_b], signal=(j == nkc - 1))

    groups = []
    for t0 in range(0, Tl, tg):
        groups.append((t0, tg, True))
    if not last:
        for t0 in range(Tl, Tl + CTX, tg):
            groups.append((t0, min(tg, CTX), False))
    for (t0, tn, is_lat) in groups:
        li = 0 if is_lat else 1
        kb.dma("sp", sTg[:, :, 0:tn], sT[:, t0:t0 + tn].rearrange("(j p) t -> p j t", p=128), writes=[STG])
        kb.dma("sp", oa[:, :, 0:tn], oaT[:, t0:t0 + tn].rearrange("(j p) t -> p j t", p=128), writes=[OA])
        kb.dma("sp", oc[:, :, 0:tn], ocT[:, t0:t0 + tn].rearrange("(j p) t -> p j t", p=128), writes=[OC])
        kb.dma("sp", yg[:, :, 0:tn], ygT[:, t0:t0 + tn].rearrange("(j p) t -> p j t", p=128), writes=[YG])
        dv.norm_fm(sTg, STG, KC, tn, D, gs[:, li, :], col(1 + 2 * li), GS, hT, HT, hT, HT)
        for g in range(G_M):
            dv.norm_fm(yg[:, 4 * g:4 * g + 4, :], YG, 4, tn, D_INNER // G_M, ssg[:, 4 * g:4 * g + 4], None, PV,
                       ysn[:, 4 * g:4 * g + 4, :], YSN, ysn[:, 4 * g:4 * g + 4, :], YSN)
        for c0 in range(0, D, 256):
            wts = [wload(wod, 0, 8, c0), wload(wos, 0, 8, c0), wload(wom, 0, 8, c0)]
            gts = [wload(wg, 0, KC, i * D + c0) for i in range(3)]
            for cc in (0, 128):
                m = (c0 + cc) // 128
                tacc = None
                for i, (x, XB) in enumerate(((oa, OA), (ysn, YSN), (oc, OC))):
                    py, pyb = dv.ps()
                    mmgroup(py, pyb, wts[i][0], wts[i][1], 8, cc, x, XB, tn)
                    pg, pgb = dv.ps()
                    mmgroup(pg, pgb, gts[i][0], gts[i][1], KC, cc, hT, HT, tn)
                    s_t, s_b = sg()
                    kb.op("act", "activation", out=s_t[:, 0:tn], in_=pg[:, 0:tn], func=AF.Sigmoid, reads=[pgb], writes=[s_b])
                    y_t, y_b = tt_()
                    kb.op("dve", "tensor_tensor", out=y_t[:, 0:tn], in0=py[:, 0:tn], in1=s_t[:, 0:tn], op=ALU.mult,
                          reads=[pyb, s_b], writes=[y_b])
                    if i == 0:
                        tacc = (y_t, y_b)
                    elif i == 1:
                        a_t, a_b = t01()
                        kb.op("pool", "tensor_tensor", out=a_t[:, 0:tn], in0=tacc[0][:, 0:tn], in1=y_t[:, 0:tn], op=ALU.add,
                              reads=[tacc[1], y_b], writes=[a_b])
                        tacc = (a_t, a_b)
                    else:
                        kb.op("pool", "tensor_tensor", out=mg[:, m, 0:tn], in0=tacc[0][:, 0:tn], in1=y_t[:, 0:tn], op=ALU.add,
                              reads=[tacc[1], y_b], writes=[MG])
        for c0 in range(0, D, 256):
            w_t, w_b = wload(wout, 0, KC, c0)
            for cc in (0, 128):
                m = (c0 + cc) // 128
                p_t, p_b = dv.ps()
                mmgroup(p_t, p_b, w_t, w_b, KC, cc, mg, MG, tn)
                kb.op("dve", "scalar_tensor_tensor", out=sTg[:, m, 0:tn], in0=p_t[:, 0:tn], scalar=pv[:, (5 + li) * KC + m:(5 + li) * KC + m + 1],
                      in1=sTg[:, m, 0:tn], op0=ALU.mult, op1=ALU.add, reads=[p_b, PV, STG], writes=[STG])
        dv.norm_fm(sTg, STG, KC, tn, D, gs[:, 2 + li, :], col(8 + 2 * li), GS, hT, HT, hT, HT)
        for hb in range(D_FF // 2048):
            u_t, u_b = uT()
            for c0 in range(0, 2048, 256):
                w_t, w_b = wload(w1, 0, KC, hb * 2048 + c0)
                for cc in (0, 128):
                    c = (c0 + cc) // 128
                    p_t, p_b = dv.ps()
                    mmgroup(p_t, p_b, w_t, w_b, KC, cc, hT, HT, tn)
                    r_t, r_b = rl()
                    kb.op("act", "activation", out=r_t[:, 0:tn], in_=p_t[:, 0:tn], func=AF.Relu, reads=[p_b], writes=[r_b])
                    kb.op("pool", "tensor_tensor", out=u_t[:, c, 0:tn], in0=r_t[:, 0:tn], in1=r_t[:, 0:tn], op=ALU.mult,
                          reads=[r_b], writes=[u_b])
            for c0 in range(0, D, 256):
                w_t, w_b = wload(w2, hb * 2048, KC, c0)
                for cc in (0, 128):
                    m = (c0 + cc) // 128
                    p_t, p_b = dv.ps()
                    mmgroup(p_t, p_b, w_t, w_b, KC, cc, u_t, u_b, tn)
                    kb.op("dve", "scalar_tensor_tensor", out=sTg[:, m, 0:tn], in0=p_t[:, 0:tn],
                          scalar=pv[:, (12 + li) * KC + m:(12 + li) * KC + m + 1], in1=sTg[:, m, 0:tn],
                          op0=ALU.mult, op1=ALU.add, reads=[p_b, PV, STG], writes=[STG])
        if not last:
            kb.store("sp", sO[:, t0:t0 + tn].rearrange("(j p) t -> p j t", p=128), sTg[:, :, 0:tn], STG)
        else:
            r_t, r_b = dv.rstd_fm(sTg, STG, KC, tn, D, hT, HT)
            for j in range(KC):
                o_t, o_b = fo()
                kb.op("dve", "scalar_tensor_tensor", out=o_t[:, 0:tn], in0=sTg[:, j, 0:tn], scalar=pv[:, 14 * KC + j:14 * KC + j + 1],
                      in1=r_t[:, 0:tn], op0=ALU.mult, op1=ALU.mult, reads=[STG, PV, r_b], writes=[o_b])
                kb.store("sp", sO[j * 128:(j + 1) * 128, t0:t0 + tn], o_t[:, 0:tn], o_b)
    return kb.build()


def run_ld(sT_lat, sT_ctx, oaT, ygT, ocT, mods_l, inp, wbf, l, last):
    S = sT_lat.shape[1]
    Tl = S // NCORE
    nc = build_ld(Tl, last)
    mx = mods_l[0].reshape(6, D)
    mc = mods_l[1].reshape(6, D)
    cols = [inp["norm1_g"][l], mx[0], mx[1], mc[0], mc[1], mx[2], mc[2], inp["norm2_g"][l], mx[3], mx[4], mc[3], mc[4],
            mx[5], mc[5], inp["final_norm_g"]]
    pv = np.concatenate([pvec(c) for c in cols] + [pvec(inp["ssm_norm_g"][l])], axis=1)
    wg = np.ascontiguousarray(wbf["w_in"][l][:, GATE0:])
    in_maps = []
    for i in range(NCORE):
        def sl(a_lat, a_ctx):
            parts = [a_lat[:, i * Tl:(i + 1) * Tl]] + ([] if last else [a_ctx])
            return np.ascontiguousarray(np.concatenate(parts, axis=1))
        m = {"sT": sl(sT_lat, sT_ctx), "oaT": sl(oaT[:, :S], oaT[:, S:]), "ygT": sl(ygT[:, :S], ygT[:, S:]),
             "ocT": sl(ocT[:, :S], ocT[:, S:]), "pv": pv, "wg": wg, "wod": wbf["w_o_diff"][l], "wos": wbf["w_o_ssm"][l],
             "wom": wbf["w_o_mla"][l], "wout": wbf["w_out"][l], "w1": wbf["w_mlp1"][l], "w2": wbf["w_mlp2"][l],
             "ones": np.ones((128, 128), np.float32)}
        in_maps.append(m)
    res = run(nc, in_maps)
    new_lat = np.concatenate([r["sO"][:, :Tl] for r in res], axis=1)
    new_ctx = None if last else res[0]["sO"][:, Tl:]
    return new_lat, new_ctx


LB_POOL_MOD = 1000000


def build_lb(Tl, S, with_ctx, lam_init):
    kb = KB()
    dv = Dev(kb, nps=2)
    Tq = Tl + (CTX if with_ctx else 0)
    TK = S + CTX
    NKT = TK // 128
    qd = kb.dram("qd", [1024, Tq], BF16, kind="ExternalInput")
    kd = kb.dram("kd", [1024, TK], BF16, kind="ExternalInput")
    vd = kb.dram("vd", [TK, 1024], BF16, kind="ExternalInput")
    qm = kb.dram("qm", [1536, Tq], BF16, kind="ExternalInput")
    km = kb.dram("km", [1024, TK], BF16, kind="ExternalInput")
    kr = kb.dram("kr", [64, TK], BF16, kind="ExternalInput")
    vmd = kb.dram("vmd", [TK, 1024], BF16, kind="ExternalInput")
    lq_d = kb.dram("lq", [128, 256], F32, kind="ExternalInput")
    sg_d = kb.dram("subg", [128, 1], F32, kind="ExternalInput")
    oaT = kb.dram("oaT", [1024, Tq], BF16, kind="ExternalOutput")
    ocT = kb.dram("ocT", [1024, Tq], BF16, kind="ExternalOutput")

    lq = kb.sbuf("lq_sb", [128, 256], F32)
    LQ = Buf("lq")
    kb.dma("sp", lq[:], lq_d[:], writes=[LQ])
    subg = kb.sbuf("subg_sb", [128, 1], F32)
    SG = Buf("subg")
    kb.dma("sp", subg[:], sg_d[:], writes=[SG])
    lt = kb.sbuf("lt", [128, 128], F32)
    LT = Buf("lt")
    ls = kb.sbuf("ls", [128, 4], F32)
    LS = Buf("ls")
    kb.op("dve", "tensor_tensor", out=lt[:, 0:64], in0=lq[:, 0:64], in1=lq[:, 64:128], op=ALU.mult, reads=[LQ], writes=[LT])
    kb.op("dve", "tensor_tensor", out=lt[:, 64:128], in0=lq[:, 128:192], in1=lq[:, 192:256], op=ALU.mult, reads=[LQ], writes=[LT])
    kb.op("dve", "tensor_reduce", out=ls[:, 0:2], in_=lt[:].rearrange("p (a b) -> p a b", a=2), axis=AX.X, op=ALU.add,
          reads=[LT], writes=[LS])
    kb.op("act", "activation", out=ls[:, 0:2], in_=ls[:, 0:2], func=AF.Exp, reads=[LS], writes=[LS])
    kb.op("dve", "tensor_tensor", out=ls[:, 2:3], in0=ls[:, 1:2], in1=ls[:, 0:1], op=ALU.subtract, reads=[LS], writes=[LS])
    kb.op("dve", "tensor_scalar", out=ls[:, 2:3], in0=ls[:, 2:3], scalar1=-float(lam_init), scalar2=None, op0=ALU.add,
          reads=[LS], writes=[LS])
    kb.op("dve", "tensor_scalar", out=ls[:, 3:4], in0=subg[:, 0:1], scalar1=float(1.0 - lam_init), scalar2=None, op0=ALU.mult,
          reads=[SG, LS], writes=[LS])
    nlam = ls[:, 2:3]
    gsub = ls[:, 3:4]
    ones_bf = kb.sbuf("ones_bf", [128, 128], BF16)
    OB = Buf("onesbf")
    kb.op("dve", "tensor_copy", out=ones_bf[:], in_=dv.ones[:], reads=[dv.ONES], writes=[OB])

    tg = min(512, Tl)
    kA = kb.sbuf("kA", [128, TK], BF16)
    kB_ = kb.sbuf("kB", [128, TK], BF16)
    KBb = Buf("kB")
    kb.op("pool", "memset", _args=(kB_[64:128, :], 0.0), writes=[KBb])
    vt = kb.sbuf("vt", [128, NKT, 128], BF16)
    PK = 13 if NKT % 13 == 0 else (10 if NKT % 10 == 0 else NKT)
    VTs = [Buf("vt%d" % i) for i in range(NKT // PK)]

    def load_v(src, h):
        for i, b in enumerate(VTs):
            kb.dma("sp", vt[:, i * PK:(i + 1) * PK, :],
                   src[i * PK * 128:(i + 1) * PK * 128, h * 128:(h + 1) * 128].rearrange("(k p) e -> p k e", p=128), writes=[b])
    KAs = [Buf("kA%d" % i) for i in range(NKT // PK)]
    KA2 = [Buf("kA2_%d" % i) for i in range(NKT // PK)]

    def load_k(srcs):
        for i in range(NKT // PK):
            cs = slice(i * PK * 128, (i + 1) * PK * 128)
            for si, (r0, nr, src) in enumerate(srcs):
                kb.dma("sp", kA[r0:r0 + nr, cs], src[:, cs], writes=[KAs[i] if si == 0 else KA2[i]])
    qAp = kb.pool("qA", [128, Tq], BF16, 2)
    qBp = kb.pool("qB", [128, Tq], BF16, 2)
    q1p = kb.pool("q1z", [128, Tq], BF16, 2)
    q2p = kb.pool("q2z", [128, Tq], BF16, 2)
    for _ in range(2):
        t_, b_ = qBp()
        kb.op("pool", "memset", _args=(t_[64:128, :], 0.0), writes=[b_])
        t_, b_ = q1p()
        kb.op("pool", "memset", _args=(t_[64:128, :], 0.0), writes=[b_])
        t_, b_ = q2p()
        kb.op("pool", "memset", _args=(t_[0:64, :], 0.0), writes=[b_])
    stp = kb.pool("st", [128, 512], F32, 4, space="psum")
    pop = kb.pool("po", [128, 512], F32, 2, space="psum")
    ptp = kb.pool("pt", [128, 512], BF16, 8)
    accs = [(kb.pool("accD%d" % i, [128, 512], F32, 1), kb.pool("accP%d" % i, [128, 512], F32, 1)) for i in range(2)]
    rsp = kb.pool("rs", [128, 512], F32, 2)
    o1p = kb.pool("o1", [128, 1, 512], F32, 2)
    o2p = kb.pool("o2", [128, 512], F32, 2)
    sqp = kb.pool("sqs", [128, 1, 512], F32, 2)
    obp = kb.pool("ob", [128, 512], BF16, 3)

    qgroups = [(t0, tg, True) for t0 in range(0, Tl, tg)] + ([(Tl, CTX, False)] if with_ctx else [])

    def kreads(kbuf, kt):
        if not isinstance(kbuf, list):
            return [kbuf]
        if isinstance(kbuf[0], list):
            return [kk[kt // PK] for kk in kbuf]
        return [kbuf[kt // PK]]

    def flash(streams, scale, t0, tn, kts):
        ns = len(streams)
        look = 4 // ns - 1
        n = len(kts)
        pos = [pop() for _ in range(ns)]
        acc = [(accs[i][0](), accs[i][1]()) for i in range(ns)]
        usedP = [False] * ns
        pts = {}
        for i in range(n + look):
            if i < n:
                kt = kts[i]
                for si, pairs in enumerate(streams):
                    st, stb = stp()
                    for pi, (kt_, kbuf, r0, nr, qt_, qbuf) in enumerate(pairs):
                        kb.op("pe", "matmul", out=st[:, 0:tn], lhsT=kt_[r0:r0 + nr, kt * 128:(kt + 1) * 128],
                              rhs=qt_[r0:r0 + nr, t0:t0 + tn], start=(pi == 0), stop=(pi == len(pairs) - 1),
                              reads=kreads(kbuf, kt) + [qbuf], writes=[stb],
                              signal=(pi == len(pairs) - 1))
                    p_t, p_b = ptp()
                    kb.op("act", "activation", out=p_t[:, 0:tn], in_=st[:, 0:tn], func=AF.Exp, scale=float(scale),
                          reads=[stb], writes=[p_b])
                    pts[(i, si)] = (p_t, p_b)
            j = i - look
            if j >= 0:
                kt = kts[j]
                for si in range(ns):
                    p_t, p_b = pts.pop((j, si))
                    po, pob = pos[si]
                    kb.op("pe", "matmul", out=po[:, 0:tn], lhsT=vt[:, kt, :], rhs=p_t[:, 0:tn], start=(j == 0), stop=(j == n - 1),
                          reads=[VTs[kt // PK], p_b], writes=[pob], signal=(j == n - 1))
                    onpool = (j % LB_POOL_MOD == LB_POOL_MOD - 1)
                    (a_t, a_b) = acc[si][1] if onpool else acc[si][0]
                    eng = "pool" if onpool else "dve"
                    first = (j == LB_POOL_MOD - 1) if onpool else (j == 0)
                    if onpool:
                        usedP[si] = True
                    if first:
                        kb.op(eng, "tensor_copy", out=a_t[:, 0:tn], in_=p_t[:, 0:tn], reads=[p_b], writes=[a_b])
                    else:
                        kb.op(eng, "tensor_tensor", out=a_t[:, 0:tn], in0=a_t[:, 0:tn], in1=p_t[:, 0:tn], op=ALU.add,
                              reads=[a_b, p_b], writes=[a_b])
        outs = []
        for si in range(ns):
            (d_t, d_b), (g_t, g_b) = acc[si]
            if usedP[si]:
                kb.op("dve", "tensor_tensor", out=d_t[:, 0:tn], in0=d_t[:, 0:tn], in1=g_t[:, 0:tn], op=ALU.add,
                      reads=[d_b, g_b], writes=[d_b])
            pss, pssb = dv.ps()
            kb.op("pe", "matmul", out=pss[:, 0:tn], lhsT=dv.ones[:], rhs=d_t[:, 0:tn], start=True, stop=True,
                  reads=[dv.ONES, d_b], writes=[pssb])
            r_t, r_b = rsp()
            kb.op("dve", "reciprocal", out=r_t[:, 0:tn], in_=pss[:, 0:tn], reads=[pssb], writes=[r_b])
            outs.append((pos[si][0], pos[si][1], r_t, r_b))
        return outs

    def qslices(is_lat):
        return list(range(NKT)) if is_lat else list(range(CTX // 128))

    for h in range(H_A):
        load_k([(0, 64, kd[h * 64:(h + 1) * 64, :]), (64, 64, kd[512 + h * 64:512 + (h + 1) * 64, :])])
        load_v(vd, h)
        qA, QA = q1p()
        qB, QB = q2p()
        kb.dma("sp", qA[0:64, :], qd[h * 64:(h + 1) * 64, :], writes=[QA])
        kb.dma("sp", qB[64:128, :], qd[512 + h * 64:512 + (h + 1) * 64, :], writes=[QB])
        for (t0, tn, is_lat) in qgroups:
            kts = qslices(is_lat)
            (po1, po1b, r1, r1b), (po2, po2b, r2, r2b) = flash(
                [[(kA, [KAs, KA2], 0, 128, qA, QA)], [(kA, [KAs, KA2], 0, 128, qB, QB)]], DA ** -0.5, t0, tn, kts)
            o1, o1b = o1p()
            kb.op("dve", "tensor_tensor", out=o1[:, 0, 0:tn], in0=po1[:, 0:tn], in1=r1[:, 0:tn], op=ALU.mult,
                  reads=[po1b, r1b], writes=[o1b])
            o2, o2b = o2p()
            kb.op("dve", "tensor_tensor", out=o2[:, 0:tn], in0=po2[:, 0:tn], in1=r2[:, 0:tn], op=ALU.mult,
                  reads=[po2b, r2b], writes=[o2b])
            kb.op("dve", "scalar_tensor_tensor", out=o1[:, 0, 0:tn], in0=o2[:, 0:tn], scalar=nlam, in1=o1[:, 0, 0:tn],
                  op0=ALU.mult, op1=ALU.add, reads=[o2b, LS, o1b], writes=[o1b])
            sq_t, sq_b = sqp()
            rr, rrb = dv.rstd_fm(o1, o1b, 1, tn, DV_A, sq_t, sq_b)
            ob, obb = obp()
            kb.op("dve", "scalar_tensor_tensor", out=ob[:, 0:tn], in0=o1[:, 0, 0:tn], scalar=gsub, in1=rr[:, 0:tn],
                  op0=ALU.mult, op1=ALU.mult, reads=[o1b, LS, rrb], writes=[obb])
            kb.store("sp", oaT[h * 128:(h + 1) * 128, t0:t0 + tn], ob[:, 0:tn], obb)
    kb.dma("sp", kB_[0:64, :], kr[:, :], writes=[KBb])
    for h in range(H_C):
        for i_ in range(NKT // PK):
            cs_ = slice(i_ * PK * 128, (i_ + 1) * PK * 128)
            kb.dma("sp", kA[:, cs_], km[h * 128:(h + 1) * 128, cs_], writes=[KAs[i_], KA2[i_]])
        load_v(vmd, h)
        qA, QA = qAp()
        qB, QB = qBp()
        kb.dma("sp", qA[:, :], qm[h * 192:h * 192 + 128, :], writes=[QA])
        kb.dma("sp", qB[0:64, :], qm[h * 192 + 128:(h + 1) * 192, :], writes=[QB])
        for (t0, tn, is_lat) in qgroups:
            kts = qslices(is_lat)
            ((po, pob, r_t, r_b),) = flash([[(kA, KAs, 0, 128, qA, QA), (kB_, KBb, 0, 128, qB, QB)]], MLA_SCALE, t0, tn, kts)
            ob, obb = obp()
            kb.op("dve", "tensor_tensor", out=ob[:, 0:tn], in0=po[:, 0:tn], in1=r_t[:, 0:tn], op=ALU.mult,
                  reads=[pob, r_b], writes=[obb])
            kb.store("sp", ocT[h * 128:(h + 1) * 128, t0:t0 + tn], ob[:, 0:tn], obb)
    return kb.build()


def lam_init_of(l):
    return 0.8 - 0.6 * math.exp(-0.3 * l)


def gather_fm(res, name, Tl):
    return np.ascontiguousarray(np.concatenate([res[0][name][:, Tl:]] + [r[name][:, :Tl] for r in res], axis=1))


def gather_tm(res, name, Tl):
    return np.ascontiguousarray(np.concatenate([res[0][name][Tl:]] + [r[name][:Tl] for r in res], axis=0))


def run_lb(la_res, S, inp, l, with_ctx):
    Tl = S // NCORE
    nc = build_lb(Tl, S, with_ctx, lam_init_of(l))
    kd = gather_fm(la_res, "qkT", Tl)[1024:]
    vd = gather_tm(la_res, "v", Tl)
    km = gather_fm(la_res, "kmT", Tl)
    kr = gather_fm(la_res, "krT", Tl)
    vmd = gather_tm(la_res, "vm", Tl)
    lq = np.ascontiguousarray(np.broadcast_to(inp["lam_qk"][l].reshape(1, 256), (128, 256))).astype(np.float32)
    subg = np.ascontiguousarray(inp["subln_g"][l].reshape(128, 1)).astype(np.float32)
    Tq = Tl + (CTX if with_ctx else 0)
    in_maps = []
    for i in range(NCORE):
        in_maps.append({"qd": np.ascontiguousarray(la_res[i]["qkT"][:1024, :Tq]), "kd": kd, "vd": vd,
                        "qm": np.ascontiguousarray(la_res[i]["qmT"][:, :Tq]), "km": km, "kr": kr, "vmd": vmd,
                        "lq": lq, "subg": subg, "ones": np.ones((128, 128), np.float32)})
    res = run(nc, in_maps)
    oaT = np.concatenate([r["oaT"][:, :Tl] for r in res] + ([res[0]["oaT"][:, Tl:]] if with_ctx else []), axis=1)
    ocT = np.concatenate([r["ocT"][:, :Tl] for r in res] + ([res[0]["ocT"][:, Tl:]] if with_ctx else []), axis=1)
    return oaT, ocT


def build_lc(S):
    kb = KB()
    TK = S + CTX
    NCH = TK // 128
    NQ = NCH * 4
    xr = kb.dram("xr", [3, 128, TK], F32, kind="ExternalInput")
    zr = kb.dram("zr", [128, TK], F32, kind="ExternalInput")
    dtr = kb.dram("dtr", [128, NQ], F32, kind="ExternalInput")
    dtb = kb.dram("dtb", [128, NQ], F32, kind="ExternalInput")
    alg = kb.dram("alg", [128, NQ], F32, kind="ExternalInput")
    cw_d = kb.dram("cw", [128, 3, 5], F32, kind="ExternalInput")
    cb_d = kb.dram("cb", [128, 3], F32, kind="ExternalInput")
    dsk_d = kb.dram("dsk", [128, 1], F32, kind="ExternalInput")
    tri_d = kb.dram("tri", [4, 128, 128], F32, kind="ExternalInput")
    id_d = kb.dram("ident", [128, 128], BF16, kind="ExternalInput")
    ones_d = kb.dram("ones", [128, 128], F32, kind="ExternalInput")
    ygT = kb.dram("ygT", [128, TK], F32, kind="ExternalOutput")
    yfs = kb.dram("yfs", [128, TK], F32)

    def cload(name, shape, dt, src):
        t = kb.sbuf(name, shape, dt)
        b = Buf(name)
        kb.dma("sp", t[:], src, writes=[b])
        return t, b
    tri, TRI = cload("tri_sb", [128, 4, 128], F32, tri_d.rearrange("k p n -> p k n"))
    ident, IDB = cload("id_sb", [128, 128], BF16, id_d[:])
    ones, ONES = cload("ones_sb", [128, 128], F32, ones_d[:])
    cw, CW = cload("cw_sb", [128, 3, 5], F32, cw_d[:])
    cb, CBB = cload("cb_sb", [128, 3], F32, cb_d[:])
    dsk, DSK = cload("dsk_sb", [128, 1], F32, dsk_d[:])
    dt_all, DT = cload("dt_all", [128, NQ], F32, dtr[:])
    dtb_sb, DTBB = cload("dtb_sb", [128, NQ], F32, dtb[:])
    a_all, AA = cload("a_all", [128, NQ], F32, alg[:])
    TU, TL, XF, XB = (tri[:, k, :] for k in range(4))

    ps = kb.pool("ps", [128, 512], F32, 6, space="psum")
    pst = kb.pool("pst", [128, 256], BF16, 2, space="psum")

    kb.op("dve", "tensor_tensor", out=dt_all[:], in0=dt_all[:], in1=dtb_sb[:], op=ALU.add, reads=[DT, DTBB], writes=[DT])
    kb.op("act", "activation", out=dt_all[:], in_=dt_all[:], func=AF.Exp, reads=[DT], writes=[DT])
    kb.op("act", "activation", out=dt_all[:], in_=dt_all[:], func=AF.Ln, bias=1.0, reads=[DT], writes=[DT])
    kb.op("act", "activation", out=a_all[:], in_=a_all[:], func=AF.Exp, reads=[AA], writes=[AA])
    kb.op("dve", "scalar_tensor_tensor", out=a_all[:], in0=a_all[:], scalar=-1.0, in1=dt_all[:], op0=ALU.mult, op1=ALU.mult,
          reads=[AA, DT], writes=[AA])
    cumF = kb.sbuf("cumF", [128, NQ], F32)
    cumB = kb.sbuf("cumB", [128, NQ], F32)
    dec = kb.sbuf("dec", [128, NQ], F32)
    w_all = kb.sbuf("w_all", [128, NQ], F32)
    TB = Buf("tables")
    for q0 in range(0, NQ, 512):
        qn = min(512, NQ - q0)
        pF, pFb = ps()
        kb.op("pe", "matmul", out=pF[:, 0:qn], lhsT=TU, rhs=a_all[:, q0:q0 + qn], start=True, stop=True, reads=[TRI, AA], writes=[pFb])
        pB, pBb = ps()
        kb.op("pe", "matmul", out=pB[:, 0:qn], lhsT=TL, rhs=a_all[:, q0:q0 + qn], start=True, stop=True, reads=[TRI, AA], writes=[pBb])
        pT, pTb = ps()
        kb.op("pe", "matmul", out=pT[:, 0:qn], lhsT=ones[:], rhs=a_all[:, q0:q0 + qn], start=True, stop=True, reads=[ONES, AA], writes=[pTb])
        kb.op("act", "activation", out=dec[:, q0:q0 + qn], in_=pT[:, 0:qn], func=AF.Copy, reads=[pTb], writes=[TB])
        kb.op("dve", "tensor_tensor", out=cumF[:, q0:q0 + qn], in0=dec[:, q0:q0 + qn], in1=pF[:, 0:qn], op=ALU.subtract,
              reads=[TB, pFb], writes=[TB])
        kb.op("dve", "tensor_tensor", out=cumB[:, q0:q0 + qn], in0=dec[:, q0:q0 + qn], in1=pB[:, 0:qn], op=ALU.subtract,
              reads=[TB, pBb], writes=[TB])
    kb.op("act", "activation", out=cumF[:], in_=cumF[:], func=AF.Exp, reads=[TB], writes=[TB])
    kb.op("act", "activation", out=cumB[:], in_=cumB[:], func=AF.Exp, reads=[TB], writes=[TB])
    kb.op("act", "activation", out=dec[:], in_=dec[:], func=AF.Exp, reads=[TB], writes=[TB])
    v4 = lambda t: t[:].rearrange("p (c k) -> p c k", k=4)
    kb.op("dve", "tensor_tensor", out=v4(w_all)[:, :, 0:2], in0=v4(dt_all)[:, :, 0:2], in1=v4(cumF)[:, :, 0:2], op=ALU.mult,
          reads=[DT, TB], writes=[TB])
    kb.op("dve", "tensor_tensor", out=v4(w_all)[:, :, 2:4], in0=v4(dt_all)[:, :, 2:4], in1=v4(cumB)[:, :, 2:4], op=ALU.mult,
          reads=[DT, TB], writes=[TB])

    xbc = kb.sbuf("xbc", [128, 3, TK], BF16)
    XBC = Buf("xbc")
    SEG = min(2048, S)
    raw = kb.pool("raw", [128, SEG + 4], F32, 2)
    acc = kb.pool("cacc", [128, SEG], F32, 2)
    segs = [(0, CTX, 0, CTX)] + [(CTX + a, SEG, CTX, TK) for a in range(0, S, SEG)]
    ei = 0
    for (a0, n, s0, s1) in segs:
        for k3 in range(3):
            r_t, r_b = raw()
            lo = max(a0 - 2, s0)
            hi = min(a0 + n + 2, s1)
            eng = "dve"
            kb.op("pool", "memset", _args=(r_t[:, 0:n + 4], 0.0), writes=[r_b])
            kb.dma("sp", r_t[:, lo - (a0 - 2):hi - (a0 - 2)], xr[k3, :, lo:hi], writes=[r_b])
            c_t, c_b = acc()
            kb.op(eng, "tensor_scalar", out=c_t[:, 0:n], in0=r_t[:, 2:2 + n], scalar1=cw[:, k3, 2:3], scalar2=cb[:, k3:k3 + 1],
                  op0=ALU.mult, op1=ALU.add, reads=[r_b, CW, CBB], writes=[c_b])
            for k in (0, 1, 3, 4):
                kb.op(eng, "scalar_tensor_tensor", out=c_t[:, 0:n], in0=r_t[:, k:k + n], scalar=cw[:, k3, k:k + 1], in1=c_t[:, 0:n],
                      op0=ALU.mult, op1=ALU.add, reads=[r_b, CW, c_b], writes=[c_b])
            kb.op("act", "activation", out=xbc[:, k3, a0:a0 + n], in_=c_t[:, 0:n], func=AF.Silu, reads=[c_b], writes=[XBC])

    H = kb.sbuf("H", [128, 128], F32)
    HB = Buf("H")
    Hbf = kb.sbuf("Hbf", [128, 128], BF16)
    HBF = Buf("Hbf")
    btm = kb.pool("btm", [128, 128], BF16, 2)
    cbm = kb.pool("cbm", [128, 128], F32, 2)
    xdp = kb.pool("xd", [128, 64], BF16, 4)
    xddp = kb.pool("xdd", [128, 64], BF16, 4)
    yp = kb.pool("Y", [128, 128], F32, 3)
    ep = kb.pool("E", [128, 128], F32, 3)
    erp = kb.pool("Er", [128, 128], F32, 3)
    mtp = kb.pool("MT", [128, 128], BF16, 3)
    csp = kb.pool("Cs", [128, 128], BF16, 3)
    yfo = kb.pool("yfo", [128, 128], F32, 3)
    yfi = kb.pool("yfi", [128, 128], F32, 3)
    zin = kb.pool("zin", [128, 128], F32, 3)
    ygo = kb.pool("ygo", [128, 128], F32, 3)
    YF = [Buf("yf%d" % c) for c in range(NCH)]

    def step(c, d):
        Td, Xd = (TU, XF) if d == 0 else (TL, XB)
        cs = slice(c * 128, (c + 1) * 128)
        px, pxb = pst()
        kb.op("pe", "transpose", out=px[:, 0:128], in_=xbc[:, 0, cs], identity=ident[:], reads=[XBC, IDB], writes=[pxb], signal=False)
        kb.op("pe", "transpose", out=px[:, 128:256], in_=xbc[:, 1, cs], identity=ident[:], reads=[XBC, IDB], writes=[pxb])
        b_t, b_b = btm()
        kb.op("act", "activation", out=b_t[:], in_=px[:, 128:256], func=AF.Copy, reads=[pxb], writes=[b_b])
        pc, pcb = ps()
        kb.op("pe", "matmul", out=pc[:, 0:128], lhsT=xbc[:, 1, cs], rhs=xbc[:, 2, cs], start=True, stop=True, reads=[XBC], writes=[pcb])
        m_t, m_b = cbm()
        kb.op("dve", "tensor_tensor", out=m_t[:], in0=pc[:, 0:128], in1=Td, op=ALU.mult, reads=[pcb, TRI], writes=[m_b])
        py, pyb = ps()
        pss, pssb = ps()
        for h in range(2):
            col = c * 4 + d * 2 + h
            hs = slice(h * 64, (h + 1) * 64)
            xd, xdb = xdp()
            kb.op("dve", "tensor_scalar", out=xd[:], in0=px[:, hs], scalar1=dt_all[:, col:col + 1], scalar2=None, op0=ALU.mult,
                  reads=[pxb, DT], writes=[xdb])
            xdd, xddb = xddp()
            kb.op("dve", "tensor_scalar", out=xdd[:], in0=px[:, hs], scalar1=w_all[:, col:col + 1], scalar2=None, op0=ALU.mult,
                  reads=[pxb, TB], writes=[xddb])
            y_t, y_b = yp()
            kb.op("pool", "tensor_scalar", out=y_t[:], in0=Td, scalar1=a_all[:, col:col + 1], scalar2=None, op0=ALU.mult,
                  reads=[TRI, AA], writes=[y_b])
            pg, pgb = ps()
            kb.op("pe", "matmul", out=pg[:, 0:128], lhsT=Xd, rhs=y_t[:], start=True, stop=True, reads=[TRI, y_b], writes=[pgb])
            pr, prb = ps()
            kb.op("pe", "matmul", out=pr[:, 0:128], lhsT=ones[:], rhs=y_t[:], start=True, stop=True, reads=[ONES, y_b], writes=[prb])
            e_t, e_b = ep()
            kb.op("act", "activation", out=e_t[:], in_=pg[:, 0:128], func=AF.Exp, reads=[pgb], writes=[e_b])
            er_t, er_b = erp()
            kb.op("act", "activation", out=er_t[:], in_=pr[:, 0:128], func=AF.Exp, reads=[prb], writes=[er_b])
            mt, mtb = mtp()
            kb.op("dve", "tensor_tensor", out=mt[:], in0=e_t[:], in1=m_t[:], op=ALU.mult, reads=[e_b, m_b], writes=[mtb])
            cs_t, cs_b = csp()
            kb.op("pool", "tensor_tensor", out=cs_t[:], in0=xbc[:, 2, cs], in1=er_t[:], op=ALU.mult, reads=[XBC, er_b], writes=[cs_b])
            kb.op("pe", "matmul", out=py[hs, 0:128], lhsT=xd[:], rhs=mt[:], start=True, stop=False, reads=[xdb, mtb], writes=[pyb], signal=False)
            kb.op("pe", "matmul", out=py[hs, 0:128], lhsT=Hbf[:, hs], rhs=cs_t[:], start=False, stop=True, reads=[HBF, cs_b], writes=[pyb])
            kb.op("pe", "matmul", out=pss[:, hs], lhsT=b_t[:], rhs=xdd[:], start=True, stop=True, reads=[b_b, xddb], writes=[pssb])
            kb.op("dve", "scalar_tensor_tensor", out=H[:, hs], in0=H[:, hs], scalar=dec[:, col:col + 1], in1=pss[:, hs],
                  op0=ALU.mult, op1=ALU.add, reads=[HB, TB, pssb], writes=[HB])
        kb.op("act", "activation", out=Hbf[:], in_=H[:], func=AF.Copy, reads=[HB], writes=[HBF])
        return py, pyb

    kb.op("dve", "memset", _args=(H[:], 0.0), writes=[HB])
    kb.op("dve", "memset", _args=(Hbf[:], 0.0), writes=[HBF])
    for c in range(NCH):
        py, pyb = step(c, 0)
        o_t, o_b = yfo()
        kb.op("act", "activation", out=o_t[:], in_=py[:, 0:128], func=AF.Copy, reads=[pyb], writes=[o_b])
        kb.dma("sp", yfs[:, c * 128:(c + 1) * 128], o_t[:], reads=[o_b], writes=[YF[c]], sembuf=o_b)
    kb.op("dve", "memset", _args=(H[:], 0.0), writes=[HB])
    kb.op("dve", "memset", _args=(Hbf[:], 0.0), writes=[HBF])
    order = list(range(CTX // 128 - 1, -1, -1)) + list(range(NCH - 1, CTX // 128 - 1, -1))
    for c in order:
        cs = slice(c * 128, (c + 1) * 128)
        i_t, i_b = yfi()
        kb.dma("sp", i_t[:], yfs[:, cs], reads=[YF[c]], writes=[i_b], sembuf=i_b)
        z_t, z_b = zin()
        kb.dma("sp", z_t[:], zr[:, cs], writes=[z_b])
        py, pyb = step(c, 1)
        kb.op("dve", "tensor_tensor", out=i_t[:], in0=i_t[:], in1=py[:, 0:128], op=ALU.add, reads=[i_b, pyb], writes=[i_b])
        kb.op("dve", "scalar_tensor_tensor", out=i_t[:], in0=xbc[:, 0, cs], scalar=dsk[:, 0:1], in1=i_t[:], op0=ALU.mult, op1=ALU.add,
              reads=[XBC, DSK, i_b], writes=[i_b])
        kb.op("act", "activation", out=z_t[:], in_=z_t[:], func=AF.Silu, reads=[z_b], writes=[z_b])
        g_t, g_b = ygo()
        kb.op("pool", "tensor_tensor", out=g_t[:], in0=i_t[:], in1=z_t[:], op=ALU.mult, reads=[i_b, z_b], writes=[g_b])
        kb.store("sp", ygT[:, cs], g_t[:], g_b)
    return kb.build()


def lc_consts():
    i = np.arange(128)
    TU = (i[:, None] <= i[None, :]).astype(np.float32)
    TL = (i[:, None] >= i[None, :]).astype(np.float32)
    XF = (i[None, :] < i[:, None]).astype(np.float32)
    XB = (i[:, None] < i[None, :]).astype(np.float32)
    return {"tri": np.stack([TU, TL, XF, XB]), "ident": np.eye(128, dtype=np.float32).astype(NPBF),
            "ones": np.ones((128, 128), np.float32)}


def run_lc(la_res, S, inp, l):
    Tl = S // NCORE
    TK = S + CTX
    NCH = TK // 128
    nc = build_lc(S)
    misc = gather_fm(la_res, "miscT", Tl)
    consts = lc_consts()
    in_maps = []
    for i in range(NCORE):
        g = i // 4
        ch = slice(i * 128, (i + 1) * 128)
        xrow = misc[M_XBC + i * 128:M_XBC + (i + 1) * 128]
        brow = misc[M_XBC + 1024 + g * 128:M_XBC + 1024 + (g + 1) * 128]
        crow = misc[M_XBC + 1280 + g * 128:M_XBC + 1280 + (g + 1) * 128]
        hsel = [2 * i, 2 * i + 1, 16 + 2 * i, 16 + 2 * i + 1]
        dt4 = misc[M_DT:M_DT + 32][hsel]
        dtr = np.ascontiguousarray(dt4.reshape(4, NCH, 128).transpose(2, 1, 0).reshape(128, NCH * 4))
        v4 = lambda a: np.ascontiguousarray(np.broadcast_to(np.tile(a.reshape(-1)[hsel], NCH)[None, :], (128, NCH * 4))).astype(np.float32)
        cwl = inp["conv_w"][l]
        cwp = np.stack([cwl[:, i * 128:(i + 1) * 128].T, cwl[:, 1024 + g * 128:1024 + (g + 1) * 128].T,
                        cwl[:, 1280 + g * 128:1280 + (g + 1) * 128].T], axis=1)
        cbl = inp["conv_b"][l]
        cbp = np.stack([cbl[i * 128:(i + 1) * 128], cbl[1024 + g * 128:1024 + (g + 1) * 128],
                        cbl[1280 + g * 128:1280 + (g + 1) * 128]], axis=1)
        m = {"xr": np.ascontiguousarray(np.stack([xrow, brow, crow])), "zr": np.ascontiguousarray(misc[M_Z + i * 128:M_Z + (i + 1) * 128]),
             "dtr": dtr, "dtb": v4(inp["dt_bias"][l]), "alg": v4(inp["a_log"][l]),
             "cw": np.ascontiguousarray(cwp).astype(np.float32), "cb": np.ascontiguousarray(cbp).astype(np.float32),
             "dsk": np.ascontiguousarray(np.repeat(inp["d_skip"][l][2 * i:2 * i + 2], 64).reshape(128, 1)).astype(np.float32)}
        m.update(consts)
        in_maps.append(m)
    res = run(nc, in_maps)
    yg = np.concatenate([r["ygT"] for r in res], axis=0)
    return np.ascontiguousarray(np.concatenate([yg[:, CTX:], yg[:, :CTX]], axis=1))


def kernel(**inp):
    import sys
    import time
    t00 = time.time()

    def log(msg):
        print("[kernel] %7.1fs %s" % (time.time() - t00, msg), file=sys.stderr, flush=True)
    inp = {k: np.asarray(v) for k, v in inp.items()}
    x = inp["x"]
    S = x.shape[1]
    L = inp["w_in"].shape[0]
    mods, wbf = run_l0(inp, L)
    log("L0 done")
    sT_lat = np.ascontiguousarray(x[0].T)
    sT_ctx = np.ascontiguousarray(inp["ctx"][0].T)
    for l in range(L):
        last = (l == L - 1)
        la = run_la(sT_lat, sT_ctx, mods[l], inp, wbf, l)
        log("LA%d done" % l)
        oaT, ocT = run_lb(la, S, inp, l, with_ctx=not last)
        log("LB%d done" % l)
        ygT = run_lc(la, S, inp, l)
        log("LC%d done" % l)
        del la
        sT_lat, sT_ctx = run_ld(sT_lat, sT_ctx, oaT, ygT, ocT, mods[l], inp, wbf, l, last)
        log("LD%d done" % l)
    return np.ascontiguousarray(sT_lat.T)[None].astype(np.float32)
```

```python
import contextlib
import math
import numpy as np
import ml_dtypes
import concourse.bass as bass
import concourse.mybir as mybir
from concourse.bass_utils import run_bass_kernel_spmd

F32 = mybir.dt.float32
BF16 = mybir.dt.bfloat16
AF = mybir.ActivationFunctionType
ALU = mybir.AluOpType
AX = mybir.AxisListType
NPBF = ml_dtypes.bfloat16

NCORE = 8
D = 2048
KC = 16
CTX = 256
GRID_W = 64
EPS = 1e-6
H_A, DA, DV_A, W_A = 8, 64, 128, 1024
H_M, P_M, D_INNER, N_STATE, G_M, D_CONV = 16, 64, 1024, 128, 2, 5
CONV_CH = D_INNER + 2 * G_M * N_STATE
H_C, D_NOPE, D_ROPE, D_VC, Q_LORA, KV_LORA, W_C = 8, 128, 64, 128, 512, 256, 1024
MLA_SCALE = (D_NOPE + D_ROPE) ** -0.5
D_FF = 4 * D
IN_SIZES = (2 * H_A * DA, 2 * H_A * DA, W_A, D_INNER, CONV_CH, 2 * H_M, Q_LORA, KV_LORA, D_ROPE, 3 * D)
P_IN = sum(IN_SIZES)
OFF = np.concatenate([[0], np.cumsum(IN_SIZES)]).tolist()
N_MISC = OFF[9] - OFF[3]
COLPERM = np.concatenate([np.arange(0, OFF[5]), np.arange(OFF[6], OFF[9]), np.arange(OFF[5], OFF[6])])
C_CQ, C_CKV, C_KR, C_DT = 5632, 6144, 6400, 6464
M_Z, M_XBC, M_CQ, M_CKV, M_KR, M_DT = 0, 1024, 2560, 3072, 3328, 3392


class Buf:
    __slots__ = ("name", "lw", "rd", "dsem", "dcnt", "excl")

    def __init__(self, name, excl=False):
        self.name = name
        self.excl = excl
        self.lw = None
        self.rd = []
        self.dsem = None
        self.dcnt = 0


class KB:
    ENG = ("pe", "act", "dve", "pool", "sp")

    def __init__(self):
        self.nc = bass.Bass("TRN2", target_bir_lowering=False)
        self.prog = {e: [] for e in self.ENG}
        self.cnt = {e: 0 for e in self.ENG}
        self.waited = {e: {} for e in self.ENG}
        self.semkeys = ["E" + e for e in self.ENG]
        self.final_waits = []
        self.n_instr = 0
        self._pools = {}
        self._uid = 0

    def dram(self, name, shape, dt, kind="Internal"):
        return self.nc.dram_tensor(name, list(shape), dt, kind=kind).ap()

    def sbuf(self, name, shape, dt):
        return self.nc.alloc_sbuf_tensor(name, list(shape), dt)

    def psum(self, name, shape, dt=F32):
        return self.nc.alloc_psum_tensor(name, list(shape), dt)

    def pool(self, name, shape, dt, n, space="sbuf"):
        tiles = []
        for i in range(n):
            t = (self.sbuf if space == "sbuf" else self.psum)("%s%d" % (name, i), shape, dt)
            tiles.append((t, Buf("%s%d" % (name, i), excl=(space == "psum"))))
        st = {"i": 0}

        def nxt():
            r = tiles[st["i"] % n]
            st["i"] += 1
            return r
        return nxt

    def _deps(self, eng, reads, writes):
        me = "E" + eng
        deps = []
        for b in reads:
            if b.lw is not None:
                deps.append(b.lw)
            if b.excl:
                for r in b.rd:
                    if r[0] != me:
                        deps.append(r)
        for b in writes:
            if b.lw is not None:
                deps.append(b.lw)
            for r in b.rd:
                deps.append(r)
        return deps

    def _emit_waits(self, eng, deps):
        need = {}
        for (k, v) in deps:
            if k == "Epe" and eng == "pe":
                continue
            if need.get(k, 0) < v:
                need[k] = v
        w = self.waited[eng]
        for k, v in need.items():
            if w.get(k, 0) < v:
                w[k] = v
                self.prog[eng].append(("wait", k, v))

    def op(self, eng, meth, reads=(), writes=(), signal=True, **kw):
        fn = (meth, kw)
        deps = self._deps(eng, reads, writes)
        self._emit_waits(eng, deps)
        k = "E" + eng
        if signal:
            self.cnt[eng] += 1
            v = self.cnt[eng]
        else:
            v = self.cnt[eng] + 1
        self.prog[eng].append(("op", fn, k if signal else None, 1))
        for b in reads:
            b.rd.append((k, v))
        for b in writes:
            b.lw = (k, v)
            b.rd = []
        self.n_instr += 1

    def dma(self, q, out, in_, reads=(), writes=(), sembuf=None, **kw):
        deps = self._deps("dma", reads, writes)
        self._emit_waits(q, deps)
        sb = sembuf if sembuf is not None else (list(writes) + list(reads))[0]
        if sb.dsem is None:
            sb.dsem = "D%d" % len(self.semkeys)
            self.semkeys.append(sb.dsem)
        sb.dcnt += 16
        k, v = sb.dsem, sb.dcnt
        kw = dict(kw)
        kw["out"] = out
        kw["in_"] = in_
        self.prog[q].append(("op", ("dma_start", kw), k, 16))
        for b in reads:
            b.rd.append((k, v))
        for b in writes:
            b.lw = (k, v)
            b.rd = []
        self.n_instr += 1
        return (k, v)

    def store(self, q, out, in_, src_buf):
        d = self.dma(q, out, in_, reads=[src_buf])
        self.final_waits.append(d)
        return d

    def build(self):
        nc = self.nc
        fin = {}
        for (k, v) in self.final_waits:
            fin[k] = max(fin.get(k, 0), v)
        for k, v in fin.items():
            self.prog["sp"].append(("wait", k, v))
        sems = {}
        with contextlib.ExitStack() as st:
            for k in self.semkeys:
                sems[k] = st.enter_context(nc.semaphore(k))
            block = st.enter_context(nc.Block())

            def runner(prog):
                def f(e):
                    for it in prog:
                        if it[0] == "wait":
                            e.wait_ge(sems[it[1]], it[2])
                        else:
                            m, kw = it[1]
                            a = kw.pop("_args", ())
                            ins = getattr(e, m)(*a, **kw)
                            if it[2] is not None:
                                ins.then_inc(sems[it[2]], it[3])
                return f
            block.tensor(runner(self.prog["pe"]))
            block.scalar(runner(self.prog["act"]))
            block.vector(runner(self.prog["dve"]))
            block.gpsimd(runner(self.prog["pool"]))
            block.sync(runner(self.prog["sp"]))
        return nc


def run(nc, in_maps):
    res = run_bass_kernel_spmd(nc, in_maps, core_ids=list(range(NCORE)))
    return res.results


CAST_W = ("w_in", "w_o_diff", "w_o_ssm", "w_uq", "w_ukv", "w_o_mla", "w_out", "w_mlp1", "w_mlp2")


def build_l0(L, wshapes):
    kb = KB()
    ncol = 6 * D // NCORE
    cc = kb.dram("cc", [D, 2], F32, kind="ExternalInput")
    wada = kb.dram("wada", [L, D, ncol], F32, kind="ExternalInput")
    bada = kb.dram("bada", [L, 2, ncol], F32, kind="ExternalInput")
    mods = kb.dram("mods", [L, 2, ncol], F32, kind="ExternalOutput")
    stg = kb.pool("stg", [128, 4096], F32, 3)
    cvt = kb.pool("cvt", [128, 4096], BF16, 3)
    ci = 0
    for name in CAST_W:
        rows, cols = wshapes[name]
        src = kb.dram(name, [rows, cols], F32, kind="ExternalInput")
        dst = kb.dram(name + "_bf", [rows, cols], BF16, kind="ExternalOutput")
        for r0 in range(0, rows, 128):
            pr = min(128, rows - r0)
            for c0 in range(0, cols, 4096):
                cw = min(4096, cols - c0)
                s_t, s_b = stg()
                c_t, c_b = cvt()
                kb.dma("sp", s_t[0:pr, 0:cw], src[r0:r0 + pr, c0:c0 + cw], writes=[s_b])
                eng = ("dve", "pool")[ci % 2]
                ci += 1
                kb.op(eng, "tensor_copy", out=c_t[0:pr, 0:cw], in_=s_t[0:pr, 0:cw], reads=[s_b], writes=[c_b])
                kb.store("act", dst[r0:r0 + pr, c0:c0 + cw], c_t[0:pr, 0:cw], c_b)
    cc_sb = kb.sbuf("cc_sb", [128, KC, 2], F32)
    CCB = Buf("cc")
    sc_sb = kb.sbuf("sc_sb", [128, KC, 2], F32)
    SCB = Buf("sc")
    kb.dma("sp", cc_sb[:], cc.rearrange("(j p) t -> p j t", p=128), writes=[CCB])
    kb.op("act", "activation", out=sc_sb[:], in_=cc_sb[:], func=AF.Silu, reads=[CCB], writes=[SCB])
    wst = kb.pool("wst", [128, KC, 512], F32, 2)
    psm = kb.pool("psm", [128, 512], F32, 2, space="psum")
    bsb = kb.pool("bsb", [2, 512], F32, 2)
    osb = kb.pool("osb", [2, 512], F32, 2)
    for l in range(L):
        for c0 in range(0, ncol, 512):
            w_t, w_b = wst()
            kb.dma("sp", w_t[:], wada[l, :, c0:c0 + 512].rearrange("(j p) n -> p j n", p=128), writes=[w_b])
            b_t, b_b = bsb()
            kb.dma("sp", b_t[:], bada[l, :, c0:c0 + 512], writes=[b_b])
            p_t, p_b = psm()
            for j in range(KC):
                kb.op("pe", "matmul", out=p_t[0:2, :], lhsT=sc_sb[:, j, :], rhs=w_t[:, j, :], start=(j == 0), stop=(j == KC - 1),
                      reads=[SCB, w_b], writes=[p_b], signal=(j == KC - 1))
            o_t, o_b = osb()
            kb.op("dve", "tensor_tensor", out=o_t[:], in0=p_t[0:2, :], in1=b_t[:], op=ALU.add, reads=[p_b, b_b], writes=[o_b])
            kb.store("sp", mods[l, :, c0:c0 + 512], o_t[:], o_b)
    return kb.build()


def run_l0(inp, L):
    wsl, wshapes = {}, {}
    for name in CAST_W:
        w = inp[name]
        flat = w.reshape(-1, w.shape[-1])
        rows = flat.shape[0] // NCORE
        wshapes[name] = (rows, flat.shape[1])
        wsl[name] = [np.ascontiguousarray(flat[i * rows:(i + 1) * rows]) for i in range(NCORE)]
    nc = build_l0(L, wshapes)
    ncol = 6 * D // NCORE
    cc = np.ascontiguousarray(np.stack([inp["c"][0], inp["c_ctx"]], axis=1))
    in_maps = []
    for i in range(NCORE):
        m = {"cc": cc,
             "wada": np.ascontiguousarray(inp["w_ada"][:, :, i * ncol:(i + 1) * ncol]),
             "bada": np.ascontiguousarray(np.repeat(inp["b_ada"][:, None, i * ncol:(i + 1) * ncol], 2, axis=1))}
        for name in CAST_W:
            m[name] = wsl[name][i]
        in_maps.append(m)
    res = run(nc, in_maps)
    mods = np.concatenate([r["mods"] for r in res], axis=2)
    wbf = {}
    for name in CAST_W:
        full = np.concatenate([r[name + "_bf"] for r in res], axis=0)
        wbf[name] = full.reshape(inp[name].shape)
    return mods, wbf


def groups_of(Tl, with_ctx=True):
    gs = []
    tg = min(512, Tl)
    for t0 in range(0, Tl, tg):
        gs.append((t0, tg, True))
    if with_ctx:
        gs.append((Tl, CTX, False))
    return gs


class Dev:
    def __init__(self, kb, nps=8):
        self.kb = kb
        self.ps = kb.pool("ps", [128, 512], F32, nps, space="psum")
        self.ones_d = kb.dram("ones", [128, 128], F32, kind="ExternalInput")
        self.ones = kb.sbuf("ones_sb", [128, 128], F32)
        self.ONES = Buf("ones")
        kb.dma("sp", self.ones[:], self.ones_d[:], writes=[self.ONES])
        self.acc = kb.pool("nacc", [128, 512], F32, 2)
        self.rstd = kb.pool("nrstd", [128, 512], F32, 2)
        self.ntmp = kb.pool("ntmp", [128, 512], F32, 3)
        self.epsb = kb.sbuf("epsb", [128, 1], F32)
        self.EPSB = Buf("epsb")
        kb.op("dve", "memset", _args=(self.epsb[:], EPS), writes=[self.EPSB])

    def rstd_fm(self, src, SRC, kcn, tn, dn, sq, SQ):
        kb = self.kb
        kb.op("act", "activation", out=sq[:, 0:kcn, 0:tn], in_=src[:, 0:kcn, 0:tn], func=AF.Square,
              reads=[SRC], writes=[SQ])
        a_t, a_b = self.acc()
        if kcn > 1:
            kb.op("dve", "tensor_reduce", out=a_t[:, 0:tn],
                                                   in_=sq[:, 0:kcn, 0:tn].rearrange("p j t -> p t j"),
                                                   axis=AX.X, op=ALU.add, reads=[SQ], writes=[a_b])
        else:
            kb.op("dve", "tensor_copy", out=a_t[:, 0:tn], in_=sq[:, 0, 0:tn], reads=[SQ], writes=[a_b])
        p_t, p_b = self.ps()
        kb.op("pe", "matmul", out=p_t[:, 0:tn], lhsT=self.ones[:], rhs=a_t[:, 0:tn], start=True, stop=True,
              reads=[self.ONES, a_b], writes=[p_b])
        r_t, r_b = self.rstd()
        kb.op("act", "activation", out=r_t[:, 0:tn], in_=p_t[:, 0:tn], func=AF.Sqrt,
                                            scale=1.0 / dn, bias=self.epsb[:],
              reads=[p_b, self.EPSB], writes=[r_b])
        kb.op("dve", "reciprocal", out=r_t[:, 0:tn], in_=r_t[:, 0:tn], reads=[r_b], writes=[r_b])
        return r_t, r_b

    def norm_fm(self, src, SRC, kcn, tn, dn, gs, shift, PV, dst, DST, sq, SQ):
        kb = self.kb
        r_t, r_b = self.rstd_fm(src, SRC, kcn, tn, dn, sq, SQ)
        for j in range(kcn):
            if shift is None:
                kb.op("dve", "scalar_tensor_tensor",
                    out=dst[:, j, 0:tn], in0=src[:, j, 0:tn], scalar=gs[:, j:j + 1], in1=r_t[:, 0:tn],
                    op0=ALU.mult, op1=ALU.mult, reads=[SRC, PV, r_b], writes=[DST])
            else:
                t_t, t_b = self.ntmp()
                kb.op("dve", "scalar_tensor_tensor",
                    out=t_t[:, 0:tn], in0=src[:, j, 0:tn], scalar=gs[:, j:j + 1], in1=r_t[:, 0:tn],
                    op0=ALU.mult, op1=ALU.mult, reads=[SRC, PV, r_b], writes=[t_b])
                kb.op("act", "activation",
                    out=dst[:, j, 0:tn], in_=t_t[:, 0:tn], func=AF.Identity, bias=shift[:, j:j + 1],
                    reads=[t_b, PV], writes=[DST])


def pvec(v):
    v = np.asarray(v, np.float32).reshape(-1, 128)
    return np.ascontiguousarray(v.T)


def rope_tables(pos):
    row = (pos // GRID_W).astype(np.float32)
    col = (pos % GRID_W).astype(np.float32)
    nf = 16
    inv = (1.0 / (10000.0 ** (np.arange(nf, dtype=np.float32) / nf))).astype(np.float32)
    ang = np.concatenate([row[:, None] * inv, col[:, None] * inv], axis=-1).astype(np.float32)
    c, s = np.cos(ang).astype(np.float32), np.sin(ang).astype(np.float32)
    C = np.empty((128, len(pos)), np.float32)
    S = np.empty((128, len(pos)), np.float32)
    for r in range(128):
        C[r] = c[:, r % 32]
        S[r] = (-s[:, r % 32]) if (r % 64) < 32 else s[:, r % 32]
    return C, S


def rot_perm():
    R = np.zeros((128, 128), np.float32)
    for m in range(128):
        k = m + 32 if (m % 64) < 32 else m - 32
        R[k, m] = 1.0
    return R.astype(NPBF)


NA = OFF[9]
NV_LA = 5 * KC + 4 + 2


DBG = {}


def build_la(Tl):
    kb = KB()
    dv = Dev(kb)
    T = Tl + CTX
    sT = kb.dram("sT", [D, T], F32, kind="ExternalInput")
    pv_d = kb.dram("pv", [128, NV_LA], F32, kind="ExternalInput")
    win = kb.dram("win", [D, NA], BF16, kind="ExternalInput")
    wuq_d = kb.dram("wuq", [Q_LORA, 1536], BF16, kind="ExternalInput")
    wukv_d = kb.dram("wukv", [KV_LORA, 2048], BF16, kind="ExternalInput")
    rc_d = kb.dram("ropeC", [128, Tl], F32, kind="ExternalInput")
    rs_d = kb.dram("ropeS", [128, Tl], F32, kind="ExternalInput")
    rp_d = kb.dram("rperm", [128, 128], BF16, kind="ExternalInput")
    qkT = kb.dram("qkT", [2048, T], BF16, kind="ExternalOutput")
    v_o = kb.dram("v", [T, 1024], BF16, kind="ExternalOutput")
    miscT = kb.dram("miscT", [N_MISC, T], F32, kind="ExternalOutput")
    qmT = kb.dram("qmT", [1536, T], BF16, kind="ExternalOutput")
    kmT = kb.dram("kmT", [1024, T], BF16, kind="ExternalOutput")
    vm_o = kb.dram("vm", [T, 1024], BF16, kind="ExternalOutput")
    krT = kb.dram("krT", [64, T], BF16, kind="ExternalOutput")

    pv = kb.sbuf("pv_sb", [128, NV_LA], F32)
    PV = Buf("pv")
    kb.dma("sp", pv[:], pv_d[:], writes=[PV])
    gsl = kb.sbuf("gsl", [128, KC], F32)
    gsc = kb.sbuf("gsc", [128, KC], F32)
    GS = Buf("gs")
    g1, shl, scl, shc, scc = (pv[:, i * KC:(i + 1) * KC] for i in range(5))
    qg = pv[:, 5 * KC:5 * KC + 4]
    kvg = pv[:, 5 * KC + 4:5 * KC + 6]
    kb.op("dve", "scalar_tensor_tensor", out=gsl[:], in0=scl, scalar=1.0, in1=g1, op0=ALU.add, op1=ALU.mult,
          reads=[PV], writes=[GS])
    kb.op("dve", "scalar_tensor_tensor", out=gsc[:], in0=scc, scalar=1.0, in1=g1, op0=ALU.add, op1=ALU.mult,
          reads=[PV], writes=[GS])
    ropeC = kb.sbuf("ropeC_sb", [128, Tl], F32)
    ropeS = kb.sbuf("ropeS_sb", [128, Tl], F32)
    RTC = Buf("rtc")
    RTS = Buf("rts")
    kb.dma("sp", ropeC[:], rc_d[:], writes=[RTC])
    kb.dma("sp", ropeS[:], rs_d[:], writes=[RTS])
    RT2 = Buf("rt2b")
    rperm = kb.sbuf("rperm_sb", [128, 128], BF16)
    kb.dma("sp", rperm[:], rp_d[:], writes=[RT2])
    wuq = kb.sbuf("wuq_sb", [128, 4, 1536], BF16)
    wukv = kb.sbuf("wukv_sb", [128, 2, 2048], BF16)
    WU = Buf("wu")
    WU2 = Buf("wu2")
    kb.dma("sp", wuq[:], wuq_d.rearrange("(j p) n -> p j n", p=128), writes=[WU])
    kb.dma("sp", wukv[:], wukv_d.rearrange("(j p) n -> p j n", p=128), writes=[WU2])

    sTg = kb.sbuf("sTg", [128, KC, 512], F32)
    STG = Buf("sTg")
    sq = kb.sbuf("sq", [128, KC, 512], F32)
    SQ = Buf("sq")
    hT = kb.sbuf("hT", [128, KC, 512], BF16)
    HT = Buf("hT")
    wblk = kb.pool("wblk", [128, KC, 512], BF16, 2)
    mla_in = kb.sbuf("mla_in", [128, 7, 512], F32)
    MLI = Buf("mli")
    mla_n = kb.sbuf("mla_n", [128, 7, 512], BF16)
    MLN = Buf("mln")
    stf = kb.pool("stf", [128, 512], F32, 3)
    stb = kb.pool("stb", [128, 512], BF16, 4)
    rt1 = kb.pool("rt1", [128, 512], F32, 2)
    rt2 = kb.pool("rt2", [128, 512], F32, 2)
    qb = kb.pool("qb", [128, 512], BF16, 2)

    def rope_epi(p_t, p_b, nr, t0, tn, dst_ap):
        q_t, q_b = qb()
        kb.op("act", "activation", out=q_t[0:nr, 0:tn], in_=p_t[0:nr, 0:tn], func=AF.Copy,
              reads=[p_b], writes=[q_b])
        p2, p2b = dv.ps()
        kb.op("pe", "matmul", out=p2[0:nr, 0:tn], lhsT=rperm[0:nr, 0:nr], rhs=q_t[0:nr, 0:tn], start=True, stop=True,
              reads=[RT2, q_b], writes=[p2b])
        a_t, a_b = rt1()
        kb.op("dve", "tensor_tensor", out=a_t[0:nr, 0:tn], in0=p_t[0:nr, 0:tn], in1=ropeC[0:nr, t0:t0 + tn], op=ALU.mult,
              reads=[p_b, RTC], writes=[a_b])
        b_t, b_b = rt2()
        kb.op("dve", "tensor_tensor", out=b_t[0:nr, 0:tn], in0=p2[0:nr, 0:tn], in1=ropeS[0:nr, t0:t0 + tn], op=ALU.mult,
              reads=[p2b, RTS], writes=[b_b])
        o_t, o_b = stb()
        kb.op("pool", "tensor_tensor", out=o_t[0:nr, 0:tn], in0=a_t[0:nr, 0:tn], in1=b_t[0:nr, 0:tn], op=ALU.add,
              reads=[a_b, b_b], writes=[o_b])
        kb.store("sp", dst_ap, o_t[0:nr, 0:tn], o_b)

    def copy_epi(p_t, p_b, nr, nc_, dst_ap, bf, eng="act"):
        o_t, o_b = (stb if bf else stf)()
        if eng == "act":
            kb.op("act", "activation", out=o_t[0:nr, 0:nc_], in_=p_t[0:nr, 0:nc_], func=AF.Copy,
                  reads=[p_b], writes=[o_b])
        else:
            kb.op("dve", "tensor_copy", out=o_t[0:nr, 0:nc_], in_=p_t[0:nr, 0:nc_], reads=[p_b], writes=[o_b])
        kb.store("sp", dst_ap, o_t[0:nr, 0:nc_], o_b)

    for (t0, tn, is_lat) in groups_of(Tl)[:DBG.get('groups', 99)]:
        kb.dma("sp", sTg[:, :, 0:tn], sT[:, t0:t0 + tn].rearrange("(j p) t -> p j t", p=128), writes=[STG])
        dv.norm_fm(sTg, STG, KC, tn, D, gsl if is_lat else gsc, shl if is_lat else shc, GS, hT, HT, sq, SQ)
        for c0 in range(0, NA, 512)[DBG.get('b0', 0):DBG.get('b1', 99)]:
            cw = min(512, NA - c0)
            w_t, w_b = wblk()
            kb.dma("sp", w_t[:, :, 0:cw], win[:, c0:c0 + cw].rearrange("(j p) n -> p j n", p=128), writes=[w_b])
            if 2048 <= c0 < 3072:
                for tt in range(0, tn, 128):
                    p_t, p_b = dv.ps()
                    for j in range(KC):
                        kb.op("pe", "matmul", out=p_t[:, 0:cw], lhsT=hT[:, j, tt:tt + 128], rhs=w_t[:, j, 0:cw], start=(j == 0), stop=(j == KC - 1),
                            reads=[HT, w_b], writes=[p_b], signal=(j == KC - 1))
                    copy_epi(p_t, p_b, 128, cw, v_o[t0 + tt:t0 + tt + 128, c0 - 2048:c0 - 2048 + cw], True,
                             eng=("act", "dve")[(tt // 128) % 2])
                continue
            chunks = [(cc, min(128, cw - cc)) for cc in range(0, cw, 128)]
            if c0 + cw == NA:
                chunks = chunks[:-1] + [(C_KR - c0, 64), (C_DT - c0, 32)]
            for (cc, nr) in chunks:
                col = c0 + cc
                p_t, p_b = dv.ps()
                for j in range(KC):
                    kb.op("pe", "matmul", out=p_t[0:nr, 0:tn], lhsT=w_t[:, j, cc:cc + nr], rhs=hT[:, j, 0:tn], start=(j == 0), stop=(j == KC - 1),
                        reads=[HT, w_b], writes=[p_b], signal=(j == KC - 1))
                if col < 2048:
                    if is_lat:
                        rope_epi(p_t, p_b, 128, t0, tn, qkT[col:col + 128, t0:t0 + tn])
                    else:
                        copy_epi(p_t, p_b, 128, tn, qkT[col:col + 128, t0:t0 + tn], True)
                else:
                    mrow = col - 3072
                    if C_CQ <= col < C_DT:
                        ci = (col - C_CQ) // 128
                        kb.op("dve", "tensor_copy", out=mla_in[0:nr, ci, 0:tn], in_=p_t[0:nr, 0:tn],
                              reads=[p_b], writes=[MLI])
                    copy_epi(p_t, p_b, nr, tn, miscT[mrow:mrow + nr, t0:t0 + tn], False)
        if DBG.get('nomla'):
            continue
        dv.norm_fm(mla_in[:, 0:4, :], MLI, 4, tn, Q_LORA, qg, None, PV, mla_n[:, 0:4, :], MLN, sq, SQ)
        dv.norm_fm(mla_in[:, 4:6, :], MLI, 2, tn, KV_LORA, kvg, None, PV, mla_n[:, 4:6, :], MLN, sq, SQ)
        for h in range(H_C):
            for (c_lo, nr, is_rope) in ((h * 192, 128, False), (h * 192 + 128, 64, True)):
                p_t, p_b = dv.ps()
                for j in range(4):
                    kb.op("pe", "matmul", out=p_t[0:nr, 0:tn], lhsT=wuq[:, j, c_lo:c_lo + nr], rhs=mla_n[:, j, 0:tn], start=(j == 0), stop=(j == 3),
                        reads=[MLN, WU], writes=[p_b], signal=(j == 3))
                if is_rope and is_lat:
                    rope_epi(p_t, p_b, nr, t0, tn, qmT[c_lo:c_lo + nr, t0:t0 + tn])
                else:
                    copy_epi(p_t, p_b, nr, tn, qmT[c_lo:c_lo + nr, t0:t0 + tn], True)
            p_t, p_b = dv.ps()
            for j in range(2):
                kb.op("pe", "matmul", out=p_t[:, 0:tn], lhsT=wukv[:, j, h * 256:h * 256 + 128], rhs=mla_n[:, 4 + j, 0:tn], start=(j == 0), stop=(j == 1),
                    reads=[MLN, WU2], writes=[p_b], signal=(j == 1))
            copy_epi(p_t, p_b, 128, tn, kmT[h * 128:(h + 1) * 128, t0:t0 + tn], True, eng="dve")
        for tt in range(0, tn, 128):
            for half in range(2):
                p_t, p_b = dv.ps()
                for hh in range(4):
                    h = half * 4 + hh
                    for j in range(2):
                        kb.op("pe", "matmul", out=p_t[:, hh * 128:(hh + 1) * 128], lhsT=mla_n[:, 4 + j, tt:tt + 128],
                            rhs=wukv[:, j, h * 256 + 128:h * 256 + 256], start=(j == 0), stop=(j == 1),
                            reads=[MLN, WU2], writes=[p_b], signal=(hh == 3 and j == 1))
                copy_epi(p_t, p_b, 128, 512, vm_o[t0 + tt:t0 + tt + 128, half * 512:(half + 1) * 512], True)
        krp, krb = dv.ps()
        kb.op("dve", "tensor_copy", out=krp[0:64, 0:tn], in_=mla_in[0:64, 6, 0:tn], reads=[MLI], writes=[krb])
        if is_lat:
            rope_epi(krp, krb, 64, t0, tn, krT[:, t0:t0 + tn])
        else:
            copy_epi(krp, krb, 64, tn, krT[:, t0:t0 + tn], True)
    return kb.build()


def la_consts(Tl, core):
    pos = np.arange(core * Tl, (core + 1) * Tl)
    C, S = rope_tables(pos)
    return {"ropeC": C, "ropeS": S, "rperm": rot_perm(), "ones": np.ones((128, 128), np.float32)}


def run_la(sT_lat, sT_ctx, mods_l, inp, wbf, l):
    S = sT_lat.shape[1]
    Tl = S // NCORE
    nc = build_la(Tl)
    mx = mods_l[0].reshape(6, D)
    mc = mods_l[1].reshape(6, D)
    pv = np.concatenate([pvec(inp["norm1_g"][l]), pvec(mx[0]), pvec(mx[1]), pvec(mc[0]), pvec(mc[1]),
                         pvec(inp["q_norm_g"][l]), pvec(inp["kv_norm_g"][l])], axis=1)
    win = np.ascontiguousarray(wbf["w_in"][l][:, COLPERM])
    in_maps = []
    for i in range(NCORE):
        m = {"sT": np.ascontiguousarray(np.concatenate([sT_lat[:, i * Tl:(i + 1) * Tl], sT_ctx], axis=1)),
             "pv": pv, "win": win, "wuq": wbf["w_uq"][l], "wukv": wbf["w_ukv"][l]}
        m.update(la_consts(Tl, i))
        in_maps.append(m)
    return run(nc, in_maps)


TG_LD = 512
NV_LD = 15 * KC + 8
GATE0 = OFF[9]


def build_ld(Tl, last):
    kb = KB()
    dv = Dev(kb)
    T = Tl if last else Tl + CTX
    sT = kb.dram("sT", [D, T], F32, kind="ExternalInput")
    oaT = kb.dram("oaT", [1024, T], BF16, kind="ExternalInput")
    ygT = kb.dram("ygT", [1024, T], F32, kind="ExternalInput")
    ocT = kb.dram("ocT", [1024, T], BF16, kind="ExternalInput")
    pv_d = kb.dram("pv", [128, NV_LD], F32, kind="ExternalInput")
    wg = kb.dram("wg", [D, 3 * D], BF16, kind="ExternalInput")
    wod = kb.dram("wod", [1024, D], BF16, kind="ExternalInput")
    wos = kb.dram("wos", [1024, D], BF16, kind="ExternalInput")
    wom = kb.dram("wom", [1024, D], BF16, kind="ExternalInput")
    wout = kb.dram("wout", [D, D], BF16, kind="ExternalInput")
    w1 = kb.dram("w1", [D, D_FF], BF16, kind="ExternalInput")
    w2 = kb.dram("w2", [D_FF, D], BF16, kind="ExternalInput")
    sO = kb.dram("sO", [D, T], F32, kind="ExternalOutput")

    pv = kb.sbuf("pv_sb", [128, NV_LD], F32)
    PV = Buf("pv")
    kb.dma("sp", pv[:], pv_d[:], writes=[PV])
    col = lambda i: pv[:, i * KC:(i + 1) * KC]
    ssg = pv[:, 15 * KC:15 * KC + 8]
    gs = kb.sbuf("gs_sb", [128, 4, KC], F32)
    GS = Buf("gs")
    for k, (sc_i, g_i) in enumerate(((2, 0), (4, 0), (9, 7), (11, 7))):
        kb.op("dve", "scalar_tensor_tensor", out=gs[:, k, :], in0=col(sc_i), scalar=1.0, in1=col(g_i),
              op0=ALU.add, op1=ALU.mult, reads=[PV], writes=[GS])

    tg = min(TG_LD, Tl)
    sTg = kb.sbuf("sTg", [128, KC, tg], F32)
    STG = Buf("sTg")
    hT = kb.sbuf("hT", [128, KC, tg], BF16)
    HT = Buf("hT")
    oa = kb.sbuf("oa", [128, 8, tg], BF16)
    OA = Buf("oa")
    oc = kb.sbuf("oc", [128, 8, tg], BF16)
    OC = Buf("oc")
    yg = kb.sbuf("yg", [128, 8, tg], F32)
    YG = Buf("yg")
    ysn = kb.sbuf("ysn", [128, 8, tg], BF16)
    YSN = Buf("ysn")
    mg = kb.sbuf("mg", [128, KC, tg], BF16)
    MG = Buf("mg")
    uT = kb.pool("uT", [128, KC, tg], BF16, 1)
    wt = kb.pool("wt", [128, KC, 256], BF16, 6)
    sg = kb.pool("sg", [128, tg], F32, 2)
    tt_ = kb.pool("tt", [128, tg], F32, 2)
    t01 = kb.pool("t01", [128, tg], F32, 2)
    rl = kb.pool("rl", [128, tg], F32, 2)
    fo = kb.pool("fo", [128, tg], F32, 2)

    def wload(src, r0, nkc, c0, ncol=256):
        w_t, w_b = wt()
        kb.dma("sp", w_t[:, 0:nkc, 0:ncol], src[r0:r0 + nkc * 128, c0:c0 + ncol].rearrange("(j p) n -> p j n", p=128),
               writes=[w_b])
        return w_t, w_b

    def mmgroup(p_t, p_b, w_t, w_b, nkc, cc, x, XB, tn):
        for j in range(nkc):
            kb.op("pe", "matmul", out=p_t[:, 0:tn], lhsT=w_t[:, j, cc:cc + 128], rhs=x[:, j, 0:tn],
                  start=(j == 0), stop=(j == nkc - 1), reads=[w_b, XB], writes=[p_b], signal=(j == nkc - 1))

    groups = []
    for t0 in range(0, Tl, tg):
        groups.append((t0, tg, True))
    if not last:
        for t0 in range(Tl, Tl + CTX, tg):
            groups.append((t0, min(tg, CTX), False))
    for (t0, tn, is_lat) in groups:
        li = 0 if is_lat else 1
        kb.dma("sp", sTg[:, :, 0:tn], sT[:, t0:t0 + tn].rearrange("(j p) t -> p j t", p=128), writes=[STG])
        kb.dma("sp", oa[:, :, 0:tn], oaT[:, t0:t0 + tn].rearrange("(j p) t -> p j t", p=128), writes=[OA])
        kb.dma("sp", oc[:, :, 0:tn], ocT[:, t0:t0 + tn].rearrange("(j p) t -> p j t", p=128), writes=[OC])
        kb.dma("sp", yg[:, :, 0:tn], ygT[:, t0:t0 + tn].rearrange("(j p) t -> p j t", p=128), writes=[YG])
        dv.norm_fm(sTg, STG, KC, tn, D, gs[:, li, :], col(1 + 2 * li), GS, hT, HT, hT, HT)
        for g in range(G_M):
            dv.norm_fm(yg[:, 4 * g:4 * g + 4, :], YG, 4, tn, D_INNER // G_M, ssg[:, 4 * g:4 * g + 4], None, PV,
                       ysn[:, 4 * g:4 * g + 4, :], YSN, ysn[:, 4 * g:4 * g + 4, :], YSN)
        for c0 in range(0, D, 256):
            wts = [wload(wod, 0, 8, c0), wload(wos, 0, 8, c0), wload(wom, 0, 8, c0)]
            gts = [wload(wg, 0, KC, i * D + c0) for i in range(3)]
            for cc in (0, 128):
                m = (c0 + cc) // 128
                tacc = None
                for i, (x, XB) in enumerate(((oa, OA), (ysn, YSN), (oc, OC))):
                    py, pyb = dv.ps()
                    mmgroup(py, pyb, wts[i][0], wts[i][1], 8, cc, x, XB, tn)
                    pg, pgb = dv.ps()
                    mmgroup(pg, pgb, gts[i][0], gts[i][1], KC, cc, hT, HT, tn)
                    s_t, s_b = sg()
                    kb.op("act", "activation", out=s_t[:, 0:tn], in_=pg[:, 0:tn], func=AF.Sigmoid, reads=[pgb], writes=[s_b])
                    y_t, y_b = tt_()
                    kb.op("dve", "tensor_tensor", out=y_t[:, 0:tn], in0=py[:, 0:tn], in1=s_t[:, 0:tn], op=ALU.mult,
                          reads=[pyb, s_b], writes=[y_b])
                    if i == 0:
                        tacc = (y_t, y_b)
                    elif i == 1:
                        a_t, a_b = t01()
                        kb.op("pool", "tensor_tensor", out=a_t[:, 0:tn], in0=tacc[0][:, 0:tn], in1=y_t[:, 0:tn], op=ALU.add,
                              reads=[tacc[1], y_b], writes=[a_b])
                        tacc = (a_t, a_b)
                    else:
                        kb.op("pool", "tensor_tensor", out=mg[:, m, 0:tn], in0=tacc[0][:, 0:tn], in1=y_t[:, 0:tn], op=ALU.add,
                              reads=[tacc[1], y_b], writes=[MG])
        for c0 in range(0, D, 256):
            w_t, w_b = wload(wout, 0, KC, c0)
            for cc in (0, 128):
                m = (c0 + cc) // 128
                p_t, p_b = dv.ps()
                mmgroup(p_t, p_b, w_t, w_b, KC, cc, mg, MG, tn)
                kb.op("dve", "scalar_tensor_tensor", out=sTg[:, m, 0:tn], in0=p_t[:, 0:tn], scalar=pv[:, (5 + li) * KC + m:(5 + li) * KC + m + 1],
                      in1=sTg[:, m, 0:tn], op0=ALU.mult, op1=ALU.add, reads=[p_b, PV, STG], writes=[STG])
        dv.norm_fm(sTg, STG, KC, tn, D, gs[:, 2 + li, :], col(8 + 2 * li), GS, hT, HT, hT, HT)
        for hb in range(D_FF // 2048):
            u_t, u_b = uT()
            for c0 in range(0, 2048, 256):
                w_t, w_b = wload(w1, 0, KC, hb * 2048 + c0)
                for cc in (0, 128):
                    c = (c0 + cc) // 128
                    p_t, p_b = dv.ps()
                    mmgroup(p_t, p_b, w_t, w_b, KC, cc, hT, HT, tn)
                    r_t, r_b = rl()
                    kb.op("act", "activation", out=r_t[:, 0:tn], in_=p_t[:, 0:tn], func=AF.Relu, reads=[p_b], writes=[r_b])
                    kb.op("pool", "tensor_tensor", out=u_t[:, c, 0:tn], in0=r_t[:, 0:tn], in1=r_t[:, 0:tn], op=ALU.mult,
                          reads=[r_b], writes=[u_b])
            for c0 in range(0, D, 256):
                w_t, w_b = wload(w2, hb * 2048, KC, c0)
                for cc in (0, 128):
                    m = (c0 + cc) // 128
                    p_t, p_b = dv.ps()
                    mmgroup(p_t, p_b, w_t, w_b, KC, cc, u_t, u_b, tn)
                    kb.op("dve", "scalar_tensor_tensor", out=sTg[:, m, 0:tn], in0=p_t[:, 0:tn],
                          scalar=pv[:, (12 + li) * KC + m:(12 + li) * KC + m + 1], in1=sTg[:, m, 0:tn],
                          op0=ALU.mult, op1=ALU.add, reads=[p_b, PV, STG], writes=[STG])
        if not last:
            kb.store("sp", sO[:, t0:t0 + tn].rearrange("(j p) t -> p j t", p=128), sTg[:, :, 0:tn], STG)
        else:
            r_t, r_b = dv.rstd_fm(sTg, STG, KC, tn, D, hT, HT)
            for j in range(KC):
                o_t, o_b = fo()
                kb.op("dve", "scalar_tensor_tensor", out=o_t[:, 0:tn], in0=sTg[:, j, 0:tn], scalar=pv[:, 14 * KC + j:14 * KC + j + 1],
                      in1=r_t[:, 0:tn], op0=ALU.mult, op1=ALU.mult, reads=[STG, PV, r_b], writes=[o_b])
                kb.store("sp", sO[j * 128:(j + 1) * 128, t0:t0 + tn], o_t[:, 0:tn], o_b)
    return kb.build()


def run_ld(sT_lat, sT_ctx, oaT, ygT, ocT, mods_l, inp, wbf, l, last):
    S = sT_lat.shape[1]
    Tl = S // NCORE
    nc = build_ld(Tl, last)
    mx = mods_l[0].reshape(6, D)
    mc = mods_l[1].reshape(6, D)
    cols = [inp["norm1_g"][l], mx[0], mx[1], mc[0], mc[1], mx[2], mc[2], inp["norm2_g"][l], mx[3], mx[4], mc[3], mc[4],
            mx[5], mc[5], inp["final_norm_g"]]
    pv = np.concatenate([pvec(c) for c in cols] + [pvec(inp["ssm_norm_g"][l])], axis=1)
    wg = np.ascontiguousarray(wbf["w_in"][l][:, GATE0:])
    in_maps = []
    for i in range(NCORE):
        def sl(a_lat, a_ctx):
            parts = [a_lat[:, i * Tl:(i + 1) * Tl]] + ([] if last else [a_ctx])
            return np.ascontiguousarray(np.concatenate(parts, axis=1))
        m = {"sT": sl(sT_lat, sT_ctx), "oaT": sl(oaT[:, :S], oaT[:, S:]), "ygT": sl(ygT[:, :S], ygT[:, S:]),
             "ocT": sl(ocT[:, :S], ocT[:, S:]), "pv": pv, "wg": wg, "wod": wbf["w_o_diff"][l], "wos": wbf["w_o_ssm"][l],
             "wom": wbf["w_o_mla"][l], "wout": wbf["w_out"][l], "w1": wbf["w_mlp1"][l], "w2": wbf["w_mlp2"][l],
             "ones": np.ones((128, 128), np.float32)}
        in_maps.append(m)
    res = run(nc, in_maps)
    new_lat = np.concatenate([r["sO"][:, :Tl] for r in res], axis=1)
    new_ctx = None if last else res[0]["sO"][:, Tl:]
    return new_lat, new_ctx


LB_POOL_MOD = 1000000


def build_lb(Tl, S, with_ctx, lam_init):
    kb = KB()
    dv = Dev(kb, nps=2)
    Tq = Tl + (CTX if with_ctx else 0)
    TK = S + CTX
    NKT = TK // 128
    qd = kb.dram("qd", [1024, Tq], BF16, kind="ExternalInput")
    kd = kb.dram("kd", [1024, TK], BF16, kind="ExternalInput")
    vd = kb.dram("vd", [TK, 1024], BF16, kind="ExternalInput")
    qm = kb.dram("qm", [1536, Tq], BF16, kind="ExternalInput")
    km = kb.dram("km", [1024, TK], BF16, kind="ExternalInput")
    kr = kb.dram("kr", [64, TK], BF16, kind="ExternalInput")
    vmd = kb.dram("vmd", [TK, 1024], BF16, kind="ExternalInput")
    lq_d = kb.dram("lq", [128, 256], F32, kind="ExternalInput")
    sg_d = kb.dram("subg", [128, 1], F32, kind="ExternalInput")
    oaT = kb.dram("oaT", [1024, Tq], BF16, kind="ExternalOutput")
    ocT = kb.dram("ocT", [1024, Tq], BF16, kind="ExternalOutput")

    lq = kb.sbuf("lq_sb", [128, 256], F32)
    LQ = Buf("lq")
    kb.dma("sp", lq[:], lq_d[:], writes=[LQ])
    subg = kb.sbuf("subg_sb", [128, 1], F32)
    SG = Buf("subg")
    kb.dma("sp", subg[:], sg_d[:], writes=[SG])
    lt = kb.sbuf("lt", [128, 128], F32)
    LT = Buf("lt")
    ls = kb.sbuf("ls", [128, 4], F32)
    LS = Buf("ls")
    kb.op("dve", "tensor_tensor", out=lt[:, 0:64], in0=lq[:, 0:64], in1=lq[:, 64:128], op=ALU.mult, reads=[LQ], writes=[LT])
    kb.op("dve", "tensor_tensor", out=lt[:, 64:128], in0=lq[:, 128:192], in1=lq[:, 192:256], op=ALU.mult, reads=[LQ], writes=[LT])
    kb.op("dve", "tensor_reduce", out=ls[:, 0:2], in_=lt[:].rearrange("p (a b) -> p a b", a=2), axis=AX.X, op=ALU.add,
          reads=[LT], writes=[LS])
    kb.op("act", "activation", out=ls[:, 0:2], in_=ls[:, 0:2], func=AF.Exp, reads=[LS], writes=[LS])
    kb.op("dve", "tensor_tensor", out=ls[:, 2:3], in0=ls[:, 1:2], in1=ls[:, 0:1], op=ALU.subtract, reads=[LS], writes=[LS])
    kb.op("dve", "tensor_scalar", out=ls[:, 2:3], in0=ls[:, 2:3], scalar1=-float(lam_init), scalar2=None, op0=ALU.add,
          reads=[LS], writes=[LS])
    kb.op("dve", "tensor_scalar", out=ls[:, 3:4], in0=subg[:, 0:1], scalar1=float(1.0 - lam_init), scalar2=None, op0=ALU.mult,
          reads=[SG, LS], writes=[LS])
    nlam = ls[:, 2:3]
    gsub = ls[:, 3:4]
    ones_bf = kb.sbuf("ones_bf", [128, 128], BF16)
    OB = Buf("onesbf")
    kb.op("dve", "tensor_copy", out=ones_bf[:], in_=dv.ones[:], reads=[dv.ONES], writes=[OB])

    tg = min(512, Tl)
    kA = kb.sbuf("kA", [128, TK], BF16)
    kB_ = kb.sbuf("kB", [128, TK], BF16)
    KBb = Buf("kB")
    kb.op("pool", "memset", _args=(kB_[64:128, :], 0.0), writes=[KBb])
    vt = kb.sbuf("vt", [128, NKT, 128], BF16)
    PK = 13 if NKT % 13 == 0 else (10 if NKT % 10 == 0 else NKT)
    VTs = [Buf("vt%d" % i) for i in range(NKT // PK)]

    def load_v(src, h):
        for i, b in enumerate(VTs):
            kb.dma("sp", vt[:, i * PK:(i + 1) * PK, :],
                   src[i * PK * 128:(i + 1) * PK * 128, h * 128:(h + 1) * 128].rearrange("(k p) e -> p k e", p=128), writes=[b])
    KAs = [Buf("kA%d" % i) for i in range(NKT // PK)]
    KA2 = [Buf("kA2_%d" % i) for i in range(NKT // PK)]

    def load_k(srcs):
        for i in range(NKT // PK):
            cs = slice(i * PK * 128, (i + 1) * PK * 128)
            for si, (r0, nr, src) in enumerate(srcs):
                kb.dma("sp", kA[r0:r0 + nr, cs], src[:, cs], writes=[KAs[i] if si == 0 else KA2[i]])
    qAp = kb.pool("qA", [128, Tq], BF16, 2)
    qBp = kb.pool("qB", [128, Tq], BF16, 2)
    q1p = kb.pool("q1z", [128, Tq], BF16, 2)
    q2p = kb.pool("q2z", [128, Tq], BF16, 2)
    for _ in range(2):
        t_, b_ = qBp()
        kb.op("pool", "memset", _args=(t_[64:128, :], 0.0), writes=[b_])
        t_, b_ = q1p()
        kb.op("pool", "memset", _args=(t_[64:128, :], 0.0), writes=[b_])
        t_, b_ = q2p()
        kb.op("pool", "memset", _args=(t_[0:64, :], 0.0), writes=[b_])
    stp = kb.pool("st", [128, 512], F32, 4, space="psum")
    pop = kb.pool("po", [128, 512], F32, 2, space="psum")
    ptp = kb.pool("pt", [128, 512], BF16, 8)
    accs = [(kb.pool("accD%d" % i, [128, 512], F32, 1), kb.pool("accP%d" % i, [128, 512], F32, 1)) for i in range(2)]
    rsp = kb.pool("rs", [128, 512], F32, 2)
    o1p = kb.pool("o1", [128, 1, 512], F32, 2)
    o2p = kb.pool("o2", [128, 512], F32, 2)
    sqp = kb.pool("sqs", [128, 1, 512], F32, 2)
    obp = kb.pool("ob", [128, 512], BF16, 3)

    qgroups = [(t0, tg, True) for t0 in range(0, Tl, tg)] + ([(Tl, CTX, False)] if with_ctx else [])

    def kreads(kbuf, kt):
        if not isinstance(kbuf, list):
            return [kbuf]
        if isinstance(kbuf[0], list):
            return [kk[kt // PK] for kk in kbuf]
        return [kbuf[kt // PK]]

    def flash(streams, scale, t0, tn, kts):
        ns = len(streams)
        look = 4 // ns - 1
        n = len(kts)
        pos = [pop() for _ in range(ns)]
        acc = [(accs[i][0](), accs[i][1]()) for i in range(ns)]
        usedP = [False] * ns
        pts = {}
        for i in range(n + look):
            if i < n:
                kt = kts[i]
                for si, pairs in enumerate(streams):
                    st, stb = stp()
                    for pi, (kt_, kbuf, r0, nr, qt_, qbuf) in enumerate(pairs):
                        kb.op("pe", "matmul", out=st[:, 0:tn], lhsT=kt_[r0:r0 + nr, kt * 128:(kt + 1) * 128],
                              rhs=qt_[r0:r0 + nr, t0:t0 + tn], start=(pi == 0), stop=(pi == len(pairs) - 1),
                              reads=kreads(kbuf, kt) + [qbuf], writes=[stb],
                              signal=(pi == len(pairs) - 1))
                    p_t, p_b = ptp()
                    kb.op("act", "activation", out=p_t[:, 0:tn], in_=st[:, 0:tn], func=AF.Exp, scale=float(scale),
                          reads=[stb], writes=[p_b])
                    pts[(i, si)] = (p_t, p_b)
            j = i - look
            if j >= 0:
                kt = kts[j]
                for si in range(ns):
                    p_t, p_b = pts.pop((j, si))
                    po, pob = pos[si]
                    kb.op("pe", "matmul", out=po[:, 0:tn], lhsT=vt[:, kt, :], rhs=p_t[:, 0:tn], start=(j == 0), stop=(j == n - 1),
                          reads=[VTs[kt // PK], p_b], writes=[pob], signal=(j == n - 1))
                    onpool = (j % LB_POOL_MOD == LB_POOL_MOD - 1)
                    (a_t, a_b) = acc[si][1] if onpool else acc[si][0]
                    eng = "pool" if onpool else "dve"
                    first = (j == LB_POOL_MOD - 1) if onpool else (j == 0)
                    if onpool:
                        usedP[si] = True
                    if first:
                        kb.op(eng, "tensor_copy", out=a_t[:, 0:tn], in_=p_t[:, 0:tn], reads=[p_b], writes=[a_b])
                    else:
                        kb.op(eng, "tensor_tensor", out=a_t[:, 0:tn], in0=a_t[:, 0:tn], in1=p_t[:, 0:tn], op=ALU.add,
                              reads=[a_b, p_b], writes=[a_b])
        outs = []
        for si in range(ns):
            (d_t, d_b), (g_t, g_b) = acc[si]
            if usedP[si]:
                kb.op("dve", "tensor_tensor", out=d_t[:, 0:tn], in0=d_t[:, 0:tn], in1=g_t[:, 0:tn], op=ALU.add,
                      reads=[d_b, g_b], writes=[d_b])
            pss, pssb = dv.ps()
            kb.op("pe", "matmul", out=pss[:, 0:tn], lhsT=dv.ones[:], rhs=d_t[:, 0:tn], start=True, stop=True,
                  reads=[dv.ONES, d_b], writes=[pssb])
            r_t, r_b = rsp()
            kb.op("dve", "reciprocal", out=r_t[:, 0:tn], in_=pss[:, 0:tn], reads=[pssb], writes=[r_b])
            outs.append((pos[si][0], pos[si][1], r_t, r_b))
        return outs

    def qslices(is_lat):
        return list(range(NKT)) if is_lat else list(range(CTX // 128))

    for h in range(H_A):
        load_k([(0, 64, kd[h * 64:(h + 1) * 64, :]), (64, 64, kd[512 + h * 64:512 + (h + 1) * 64, :])])
        load_v(vd, h)
        qA, QA = q1p()
        qB, QB = q2p()
        kb.dma("sp", qA[0:64, :], qd[h * 64:(h + 1) * 64, :], writes=[QA])
        kb.dma("sp", qB[64:128, :], qd[512 + h * 64:512 + (h + 1) * 64, :], writes=[QB])
        for (t0, tn, is_lat) in qgroups:
            kts = qslices(is_lat)
            (po1, po1b, r1, r1b), (po2, po2b, r2, r2b) = flash(
                [[(kA, [KAs, KA2], 0, 128, qA, QA)], [(kA, [KAs, KA2], 0, 128, qB, QB)]], DA ** -0.5, t0, tn, kts)
            o1, o1b = o1p()
            kb.op("dve", "tensor_tensor", out=o1[:, 0, 0:tn], in0=po1[:, 0:tn], in1=r1[:, 0:tn], op=ALU.mult,
                  reads=[po1b, r1b], writes=[o1b])
            o2, o2b = o2p()
            kb.op("dve", "tensor_tensor", out=o2[:, 0:tn], in0=po2[:, 0:tn], in1=r2[:, 0:tn], op=ALU.mult,
                  reads=[po2b, r2b], writes=[o2b])
            kb.op("dve", "scalar_tensor_tensor", out=o1[:, 0, 0:tn], in0=o2[:, 0:tn], scalar=nlam, in1=o1[:, 0, 0:tn],
                  op0=ALU.mult, op1=ALU.add, reads=[o2b, LS, o1b], writes=[o1b])
            sq_t, sq_b = sqp()
            rr, rrb = dv.rstd_fm(o1, o1b, 1, tn, DV_A, sq_t, sq_b)
            ob, obb = obp()
            kb.op("dve", "scalar_tensor_tensor", out=ob[:, 0:tn], in0=o1[:, 0, 0:tn], scalar=gsub, in1=rr[:, 0:tn],
                  op0=ALU.mult, op1=ALU.mult, reads=[o1b, LS, rrb], writes=[obb])
            kb.store("sp", oaT[h * 128:(h + 1) * 128, t0:t0 + tn], ob[:, 0:tn], obb)
    kb.dma("sp", kB_[0:64, :], kr[:, :], writes=[KBb])
    for h in range(H_C):
        for i_ in range(NKT // PK):
            cs_ = slice(i_ * PK * 128, (i_ + 1) * PK * 128)
            kb.dma("sp", kA[:, cs_], km[h * 128:(h + 1) * 128, cs_], writes=[KAs[i_], KA2[i_]])
        load_v(vmd, h)
        qA, QA = qAp()
        qB, QB = qBp()
        kb.dma("sp", qA[:, :], qm[h * 192:h * 192 + 128, :], writes=[QA])
        kb.dma("sp", qB[0:64, :], qm[h * 192 + 128:(h + 1) * 192, :], writes=[QB])
        for (t0, tn, is_lat) in qgroups:
            kts = qslices(is_lat)
            ((po, pob, r_t, r_b),) = flash([[(kA, KAs, 0, 128, qA, QA), (kB_, KBb, 0, 128, qB, QB)]], MLA_SCALE, t0, tn, kts)
            ob, obb = obp()
            kb.op("dve", "tensor_tensor", out=ob[:, 0:tn], in0=po[:, 0:tn], in1=r_t[:, 0:tn], op=ALU.mult,
                  reads=[pob, r_b], writes=[obb])
            kb.store("sp", ocT[h * 128:(h + 1) * 128, t0:t0 + tn], ob[:, 0:tn], obb)
    return kb.build()


def lam_init_of(l):
    return 0.8 - 0.6 * math.exp(-0.3 * l)


def gather_fm(res, name, Tl):
    return np.ascontiguousarray(np.concatenate([res[0][name][:, Tl:]] + [r[name][:, :Tl] for r in res], axis=1))


def gather_tm(res, name, Tl):
    return np.ascontiguousarray(np.concatenate([res[0][name][Tl:]] + [r[name][:Tl] for r in res], axis=0))


def run_lb(la_res, S, inp, l, with_ctx):
    Tl = S // NCORE
    nc = build_lb(Tl, S, with_ctx, lam_init_of(l))
    kd = gather_fm(la_res, "qkT", Tl)[1024:]
    vd = gather_tm(la_res, "v", Tl)
    km = gather_fm(la_res, "kmT", Tl)
    kr = gather_fm(la_res, "krT", Tl)
    vmd = gather_tm(la_res, "vm", Tl)
    lq = np.ascontiguousarray(np.broadcast_to(inp["lam_qk"][l].reshape(1, 256), (128, 256))).astype(np.float32)
    subg = np.ascontiguousarray(inp["subln_g"][l].reshape(128, 1)).astype(np.float32)
    Tq = Tl + (CTX if with_ctx else 0)
    in_maps = []
    for i in range(NCORE):
        in_maps.append({"qd": np.ascontiguousarray(la_res[i]["qkT"][:1024, :Tq]), "kd": kd, "vd": vd,
                        "qm": np.ascontiguousarray(la_res[i]["qmT"][:, :Tq]), "km": km, "kr": kr, "vmd": vmd,
                        "lq": lq, "subg": subg, "ones": np.ones((128, 128), np.float32)})
    res = run(nc, in_maps)
    oaT = np.concatenate([r["oaT"][:, :Tl] for r in res] + ([res[0]["oaT"][:, Tl:]] if with_ctx else []), axis=1)
    ocT = np.concatenate([r["ocT"][:, :Tl] for r in res] + ([res[0]["ocT"][:, Tl:]] if with_ctx else []), axis=1)
    return oaT, ocT


def build_lc(S):
    kb = KB()
    TK = S + CTX
    NCH = TK // 128
    NQ = NCH * 4
    xr = kb.dram("xr", [3, 128, TK], F32, kind="ExternalInput")
    zr = kb.dram("zr", [128, TK], F32, kind="ExternalInput")
    dtr = kb.dram("dtr", [128, NQ], F32, kind="ExternalInput")
    dtb = kb.dram("dtb", [128, NQ], F32, kind="ExternalInput")
    alg = kb.dram("alg", [128, NQ], F32, kind="ExternalInput")
    cw_d = kb.dram("cw", [128, 3, 5], F32, kind="ExternalInput")
    cb_d = kb.dram("cb", [128, 3], F32, kind="ExternalInput")
    dsk_d = kb.dram("dsk", [128, 1], F32, kind="ExternalInput")
    tri_d = kb.dram("tri", [4, 128, 128], F32, kind="ExternalInput")
    id_d = kb.dram("ident", [128, 128], BF16, kind="ExternalInput")
    ones_d = kb.dram("ones", [128, 128], F32, kind="ExternalInput")
    ygT = kb.dram("ygT", [128, TK], F32, kind="ExternalOutput")
    yfs = kb.dram("yfs", [128, TK], F32)

    def cload(name, shape, dt, src):
        t = kb.sbuf(name, shape, dt)
        b = Buf(name)
        kb.dma("sp", t[:], src, writes=[b])
        return t, b
    tri, TRI = cload("tri_sb", [128, 4, 128], F32, tri_d.rearrange("k p n -> p k n"))
    ident, IDB = cload("id_sb", [128, 128], BF16, id_d[:])
    ones, ONES = cload("ones_sb", [128, 128], F32, ones_d[:])
    cw, CW = cload("cw_sb", [128, 3, 5], F32, cw_d[:])
    cb, CBB = cload("cb_sb", [128, 3], F32, cb_d[:])
    dsk, DSK = cload("dsk_sb", [128, 1], F32, dsk_d[:])
    dt_all, DT = cload("dt_all", [128, NQ], F32, dtr[:])
    dtb_sb, DTBB = cload("dtb_sb", [128, NQ], F32, dtb[:])
    a_all, AA = cload("a_all", [128, NQ], F32, alg[:])
    TU, TL, XF, XB = (tri[:, k, :] for k in range(4))

    ps = kb.pool("ps", [128, 512], F32, 6, space="psum")
    pst = kb.pool("pst", [128, 256], BF16, 2, space="psum")

    kb.op("dve", "tensor_tensor", out=dt_all[:], in0=dt_all[:], in1=dtb_sb[:], op=ALU.add, reads=[DT, DTBB], writes=[DT])
    kb.op("act", "activation", out=dt_all[:], in_=dt_all[:], func=AF.Exp, reads=[DT], writes=[DT])
    kb.op("act", "activation", out=dt_all[:], in_=dt_all[:], func=AF.Ln, bias=1.0, reads=[DT], writes=[DT])
    kb.op("act", "activation", out=a_all[:], in_=a_all[:], func=AF.Exp, reads=[AA], writes=[AA])
    kb.op("dve", "scalar_tensor_tensor", out=a_all[:], in0=a_all[:], scalar=-1.0, in1=dt_all[:], op0=ALU.mult, op1=ALU.mult,
          reads=[AA, DT], writes=[AA])
    cumF = kb.sbuf("cumF", [128, NQ], F32)
    cumB = kb.sbuf("cumB", [128, NQ], F32)
    dec = kb.sbuf("dec", [128, NQ], F32)
    w_all = kb.sbuf("w_all", [128, NQ], F32)
    TB = Buf("tables")
    for q0 in range(0, NQ, 512):
        qn = min(512, NQ - q0)
        pF, pFb = ps()
        kb.op("pe", "matmul", out=pF[:, 0:qn], lhsT=TU, rhs=a_all[:, q0:q0 + qn], start=True, stop=True, reads=[TRI, AA], writes=[pFb])
        pB, pBb = ps()
        kb.op("pe", "matmul", out=pB[:, 0:qn], lhsT=TL, rhs=a_all[:, q0:q0 + qn], start=True, stop=True, reads=[TRI, AA], writes=[pBb])
        pT, pTb = ps()
        kb.op("pe", "matmul", out=pT[:, 0:qn], lhsT=ones[:], rhs=a_all[:, q0:q0 + qn], start=True, stop=True, reads=[ONES, AA], writes=[pTb])
        kb.op("act", "activation", out=dec[:, q0:q0 + qn], in_=pT[:, 0:qn], func=AF.Copy, reads=[pTb], writes=[TB])
        kb.op("dve", "tensor_tensor", out=cumF[:, q0:q0 + qn], in0=dec[:, q0:q0 + qn], in1=pF[:, 0:qn], op=ALU.subtract,
              reads=[TB, pFb], writes=[TB])
        kb.op("dve", "tensor_tensor", out=cumB[:, q0:q0 + qn], in0=dec[:, q0:q0 + qn], in1=pB[:, 0:qn], op=ALU.subtract,
              reads=[TB, pBb], writes=[TB])
    kb.op("act", "activation", out=cumF[:], in_=cumF[:], func=AF.Exp, reads=[TB], writes=[TB])
    kb.op("act", "activation", out=cumB[:], in_=cumB[:], func=AF.Exp, reads=[TB], writes=[TB])
    kb.op("act", "activation", out=dec[:], in_=dec[:], func=AF.Exp, reads=[TB], writes=[TB])
    v4 = lambda t: t[:].rearrange("p (c k) -> p c k", k=4)
    kb.op("dve", "tensor_tensor", out=v4(w_all)[:, :, 0:2], in0=v4(dt_all)[:, :, 0:2], in1=v4(cumF)[:, :, 0:2], op=ALU.mult,
          reads=[DT, TB], writes=[TB])
    kb.op("dve", "tensor_tensor", out=v4(w_all)[:, :, 2:4], in0=v4(dt_all)[:, :, 2:4], in1=v4(cumB)[:, :, 2:4], op=ALU.mult,
          reads=[DT, TB], writes=[TB])

    xbc = kb.sbuf("xbc", [128, 3, TK], BF16)
    XBC = Buf("xbc")
    SEG = min(2048, S)
    raw = kb.pool("raw", [128, SEG + 4], F32, 2)
    acc = kb.pool("cacc", [128, SEG], F32, 2)
    segs = [(0, CTX, 0, CTX)] + [(CTX + a, SEG, CTX, TK) for a in range(0, S, SEG)]
    ei = 0
    for (a0, n, s0, s1) in segs:
        for k3 in range(3):
            r_t, r_b = raw()
            lo = max(a0 - 2, s0)
            hi = min(a0 + n + 2, s1)
            eng = "dve"
            kb.op("pool", "memset", _args=(r_t[:, 0:n + 4], 0.0), writes=[r_b])
            kb.dma("sp", r_t[:, lo - (a0 - 2):hi - (a0 - 2)], xr[k3, :, lo:hi], writes=[r_b])
            c_t, c_b = acc()
            kb.op(eng, "tensor_scalar", out=c_t[:, 0:n], in0=r_t[:, 2:2 + n], scalar1=cw[:, k3, 2:3], scalar2=cb[:, k3:k3 + 1],
                  op0=ALU.mult, op1=ALU.add, reads=[r_b, CW, CBB], writes=[c_b])
            for k in (0, 1, 3, 4):
                kb.op(eng, "scalar_tensor_tensor", out=c_t[:, 0:n], in0=r_t[:, k:k + n], scalar=cw[:, k3, k:k + 1], in1=c_t[:, 0:n],
                      op0=ALU.mult, op1=ALU.add, reads=[r_b, CW, c_b], writes=[c_b])
            kb.op("act", "activation", out=xbc[:, k3, a0:a0 + n], in_=c_t[:, 0:n], func=AF.Silu, reads=[c_b], writes=[XBC])

    H = kb.sbuf("H", [128, 128], F32)
    HB = Buf("H")
    Hbf = kb.sbuf("Hbf", [128, 128], BF16)
    HBF = Buf("Hbf")
    btm = kb.pool("btm", [128, 128], BF16, 2)
    cbm = kb.pool("cbm", [128, 128], F32, 2)
    xdp = kb.pool("xd", [128, 64], BF16, 4)
    xddp = kb.pool("xdd", [128, 64], BF16, 4)
    yp = kb.pool("Y", [128, 128], F32, 3)
    ep = kb.pool("E", [128, 128], F32, 3)
    erp = kb.pool("Er", [128, 128], F32, 3)
    mtp = kb.pool("MT", [128, 128], BF16, 3)
    csp = kb.pool("Cs", [128, 128], BF16, 3)
    yfo = kb.pool("yfo", [128, 128], F32, 3)
    yfi = kb.pool("yfi", [128, 128], F32, 3)
    zin = kb.pool("zin", [128, 128], F32, 3)
    ygo = kb.pool("ygo", [128, 128], F32, 3)
    YF = [Buf("yf%d" % c) for c in range(NCH)]

    def step(c, d, H, HB, Hbf, HBF):
        Td, Xd = (TU, XF) if d == 0 else (TL, XB)
        cs = slice(c * 128, (c + 1) * 128)
        px, pxb = pst()
        kb.op("pe", "transpose", out=px[:, 0:128], in_=xbc[:, 0, cs], identity=ident[:], reads=[XBC, IDB], writes=[pxb], signal=False)
        kb.op("pe", "transpose", out=px[:, 128:256], in_=xbc[:, 1, cs], identity=ident[:], reads=[XBC, IDB], writes=[pxb])
        b_t, b_b = btm()
        kb.op("act", "activation", out=b_t[:], in_=px[:, 128:256], func=AF.Copy, reads=[pxb], writes=[b_b])
        pc, pcb = ps()
        kb.op("pe", "matmul", out=pc[:, 0:128], lhsT=xbc[:, 1, cs], rhs=xbc[:, 2, cs], start=True, stop=True, reads=[XBC], writes=[pcb])
        m_t, m_b = cbm()
        kb.op("dve", "tensor_tensor", out=m_t[:], in0=pc[:, 0:128], in1=Td, op=ALU.mult, reads=[pcb, TRI], writes=[m_b])
        py, pyb = ps()
        pss, pssb = ps()
        cols = [c * 4 + d * 2 + h for h in range(2)]
        hss = [slice(h * 64, (h + 1) * 64) for h in range(2)]
        Ys, PGs, PRs, XD, XDD, Es, ERs, MTs, CSs = [], [], [], [], [], [], [], [], []
        for h in range(2):
            y_t, y_b = yp()
            kb.op("dve", "tensor_scalar", out=y_t[:], in0=Td, scalar1=a_all[:, cols[h]:cols[h] + 1], scalar2=None, op0=ALU.mult,
                  reads=[TRI, AA], writes=[y_b])
            Ys.append((y_t, y_b))
        for h in range(2):
            y_t, y_b = Ys[h]
            pg, pgb = ps()
            kb.op("pe", "matmul", out=pg[:, 0:128], lhsT=Xd, rhs=y_t[:], start=True, stop=True, reads=[TRI, y_b], writes=[pgb])
            pr, prb = ps()
            kb.op("pe", "matmul", out=pr[:, 0:128], lhsT=ones[:], rhs=y_t[:], start=True, stop=True, reads=[ONES, y_b], writes=[prb])
            PGs.append((pg, pgb))
            PRs.append((pr, prb))
        for h in range(2):
            xd, xdb = xdp()
            kb.op("dve", "tensor_scalar", out=xd[:], in0=px[:, hss[h]], scalar1=dt_all[:, cols[h]:cols[h] + 1], scalar2=None, op0=ALU.mult,
                  reads=[pxb, DT], writes=[xdb])
            xdd, xddb = xddp()
            kb.op("dve", "tensor_scalar", out=xdd[:], in0=px[:, hss[h]], scalar1=w_all[:, cols[h]:cols[h] + 1], scalar2=None, op0=ALU.mult,
                  reads=[pxb, TB], writes=[xddb])
            XD.append((xd, xdb))
            XDD.append((xdd, xddb))
        for h in range(2):
            e_t, e_b = ep()
            kb.op("act", "activation", out=e_t[:], in_=PGs[h][0][:, 0:128], func=AF.Exp, reads=[PGs[h][1]], writes=[e_b])
            er_t, er_b = erp()
            kb.op("act", "activation", out=er_t[:], in_=PRs[h][0][:, 0:128], func=AF.Exp, reads=[PRs[h][1]], writes=[er_b])
            Es.append((e_t, e_b))
            ERs.append((er_t, er_b))
        for h in range(2):
            mt, mtb = mtp()
            kb.op("dve", "tensor_tensor", out=mt[:], in0=Es[h][0][:], in1=m_t[:], op=ALU.mult, reads=[Es[h][1], m_b], writes=[mtb])
            cs_t, cs_b = csp()
            kb.op("pool", "tensor_tensor", out=cs_t[:], in0=xbc[:, 2, cs], in1=ERs[h][0][:], op=ALU.mult, reads=[XBC, ERs[h][1]], writes=[cs_b])
            MTs.append((mt, mtb))
            CSs.append((cs_t, cs_b))
        for h in range(2):
            hs = hss[h]
            kb.op("pe", "matmul", out=pss[:, hs], lhsT=b_t[:], rhs=XDD[h][0][:], start=True, stop=True, reads=[b_b, XDD[h][1]], writes=[pssb],
                  signal=(h == 1))
        for h in range(2):
            hs = hss[h]
            kb.op("pe", "matmul", out=py[hs, 0:128], lhsT=XD[h][0][:], rhs=MTs[h][0][:], start=True, stop=False, reads=[XD[h][1], MTs[h][1]],
                  writes=[pyb], signal=False)
            kb.op("pe", "matmul", out=py[hs, 0:128], lhsT=Hbf[:, hs], rhs=CSs[h][0][:], start=False, stop=True, reads=[HBF, CSs[h][1]], writes=[pyb])
        for h in range(2):
            hs = hss[h]
            kb.op("dve", "scalar_tensor_tensor", out=H[:, hs], in0=H[:, hs], scalar=dec[:, cols[h]:cols[h] + 1], in1=pss[:, hs],
                  op0=ALU.mult, op1=ALU.add, reads=[HB, TB, pssb], writes=[HB])
        kb.op("act", "activation", out=Hbf[:], in_=H[:], func=AF.Copy, reads=[HB], writes=[HBF])
        return py, pyb

    H2 = kb.sbuf("H2", [128, 128], F32)
    HB2 = Buf("H2")
    Hbf2 = kb.sbuf("Hbf2", [128, 128], BF16)
    HBF2 = Buf("Hbf2")
    ybs = kb.dram("ybs", [128, TK], F32)
    YB = [Buf("yb%d" % c) for c in range(NCH)]
    for (t_, b_) in ((H, HB), (Hbf, HBF), (H2, HB2), (Hbf2, HBF2)):
        kb.op("dve", "memset", _args=(t_[:], 0.0), writes=[b_])
    order_b = list(range(CTX // 128 - 1, -1, -1)) + list(range(NCH - 1, CTX // 128 - 1, -1))
    for i in range(NCH):
        c = i
        py, pyb = step(c, 0, H, HB, Hbf, HBF)
        o_t, o_b = yfo()
        kb.op("act", "activation", out=o_t[:], in_=py[:, 0:128], func=AF.Copy, reads=[pyb], writes=[o_b])
        kb.dma("sp", yfs[:, c * 128:(c + 1) * 128], o_t[:], reads=[o_b], writes=[YF[c]], sembuf=o_b)
        c = order_b[i]
        py, pyb = step(c, 1, H2, HB2, Hbf2, HBF2)
        o_t, o_b = yfi()
        kb.op("act", "activation", out=o_t[:], in_=py[:, 0:128], func=AF.Copy, reads=[pyb], writes=[o_b])
        kb.dma("sp", ybs[:, c * 128:(c + 1) * 128], o_t[:], reads=[o_b], writes=[YB[c]], sembuf=o_b)
    ea = kb.pool("ea", [128, 512], F32, 2)
    eb = kb.pool("eb", [128, 512], F32, 2)
    ez = kb.pool("ez", [128, 512], F32, 2)
    eo = kb.pool("eo", [128, 512], F32, 2)
    for c0 in range(0, NCH, 4):
        nb = min(4, NCH - c0)
        w = nb * 128
        cs = slice(c0 * 128, c0 * 128 + w)
        a_t, a_b = ea()
        kb.dma("sp", a_t[:, 0:w], yfs[:, cs], reads=[YF[c] for c in range(c0, c0 + nb)], writes=[a_b], sembuf=a_b)
        b_t, b_b = eb()
        kb.dma("sp", b_t[:, 0:w], ybs[:, cs], reads=[YB[c] for c in range(c0, c0 + nb)], writes=[b_b], sembuf=b_b)
        z_t, z_b = ez()
        kb.dma("sp", z_t[:, 0:w], zr[:, cs], writes=[z_b])
        kb.op("dve", "tensor_tensor", out=a_t[:, 0:w], in0=a_t[:, 0:w], in1=b_t[:, 0:w], op=ALU.add, reads=[a_b, b_b], writes=[a_b])
        kb.op("dve", "scalar_tensor_tensor", out=a_t[:, 0:w], in0=xbc[:, 0, cs], scalar=dsk[:, 0:1], in1=a_t[:, 0:w], op0=ALU.mult, op1=ALU.add,
              reads=[XBC, DSK, a_b], writes=[a_b])
        kb.op("act", "activation", out=z_t[:, 0:w], in_=z_t[:, 0:w], func=AF.Silu, reads=[z_b], writes=[z_b])
        g_t, g_b = eo()
        kb.op("pool", "tensor_tensor", out=g_t[:, 0:w], in0=a_t[:, 0:w], in1=z_t[:, 0:w], op=ALU.mult, reads=[a_b, z_b], writes=[g_b])
        kb.store("sp", ygT[:, cs], g_t[:, 0:w], g_b)
    return kb.build()


def lc_consts():
    i = np.arange(128)
    TU = (i[:, None] <= i[None, :]).astype(np.float32)
    TL = (i[:, None] >= i[None, :]).astype(np.float32)
    XF = (i[None, :] < i[:, None]).astype(np.float32)
    XB = (i[:, None] < i[None, :]).astype(np.float32)
    return {"tri": np.stack([TU, TL, XF, XB]), "ident": np.eye(128, dtype=np.float32).astype(NPBF),
            "ones": np.ones((128, 128), np.float32)}


def run_lc(la_res, S, inp, l):
    Tl = S // NCORE
    TK = S + CTX
    NCH = TK // 128
    nc = build_lc(S)
    misc = gather_fm(la_res, "miscT", Tl)
    consts = lc_consts()
    in_maps = []
    for i in range(NCORE):
        g = i // 4
        ch = slice(i * 128, (i + 1) * 128)
        xrow = misc[M_XBC + i * 128:M_XBC + (i + 1) * 128]
        brow = misc[M_XBC + 1024 + g * 128:M_XBC + 1024 + (g + 1) * 128]
        crow = misc[M_XBC + 1280 + g * 128:M_XBC + 1280 + (g + 1) * 128]
        hsel = [2 * i, 2 * i + 1, 16 + 2 * i, 16 + 2 * i + 1]
        dt4 = misc[M_DT:M_DT + 32][hsel]
        dtr = np.ascontiguousarray(dt4.reshape(4, NCH, 128).transpose(2, 1, 0).reshape(128, NCH * 4))
        v4 = lambda a: np.ascontiguousarray(np.broadcast_to(np.tile(a.reshape(-1)[hsel], NCH)[None, :], (128, NCH * 4))).astype(np.float32)
        cwl = inp["conv_w"][l]
        cwp = np.stack([cwl[:, i * 128:(i + 1) * 128].T, cwl[:, 1024 + g * 128:1024 + (g + 1) * 128].T,
                        cwl[:, 1280 + g * 128:1280 + (g + 1) * 128].T], axis=1)
        cbl = inp["conv_b"][l]
        cbp = np.stack([cbl[i * 128:(i + 1) * 128], cbl[1024 + g * 128:1024 + (g + 1) * 128],
                        cbl[1280 + g * 128:1280 + (g + 1) * 128]], axis=1)
        m = {"xr": np.ascontiguousarray(np.stack([xrow, brow, crow])), "zr": np.ascontiguousarray(misc[M_Z + i * 128:M_Z + (i + 1) * 128]),
             "dtr": dtr, "dtb": v4(inp["dt_bias"][l]), "alg": v4(inp["a_log"][l]),
             "cw": np.ascontiguousarray(cwp).astype(np.float32), "cb": np.ascontiguousarray(cbp).astype(np.float32),
             "dsk": np.ascontiguousarray(np.repeat(inp["d_skip"][l][2 * i:2 * i + 2], 64).reshape(128, 1)).astype(np.float32)}
        m.update(consts)
        in_maps.append(m)
    res = run(nc, in_maps)
    yg = np.concatenate([r["ygT"] for r in res], axis=0)
    return np.ascontiguousarray(np.concatenate([yg[:, CTX:], yg[:, :CTX]], axis=1))


def kernel(**inp):
    import sys
    import time
    t00 = time.time()

    def log(msg):
        print("[kernel] %7.1fs %s" % (time.time() - t00, msg), file=sys.stderr, flush=True)
    inp = {k: np.asarray(v) for k, v in inp.items()}
    x = inp["x"]
    S = x.shape[1]
    L = inp["w_in"].shape[0]
    mods, wbf = run_l0(inp, L)
    log("L0 done")
    sT_lat = np.ascontiguousarray(x[0].T)
    sT_ctx = np.ascontiguousarray(inp["ctx"][0].T)
    for l in range(L):
        last = (l == L - 1)
        la = run_la(sT_lat, sT_ctx, mods[l], inp, wbf, l)
        log("LA%d done" % l)
        oaT, ocT = run_lb(la, S, inp, l, with_ctx=not last)
        log("LB%d done" % l)
        ygT = run_lc(la, S, inp, l)
        log("LC%d done" % l)
        del la
        sT_lat, sT_ctx = run_ld(sT_lat, sT_ctx, oaT, ygT, ocT, mods[l], inp, wbf, l, last)
        log("LD%d done" % l)
    return np.ascontiguousarray(sT_lat.T)[None].astype(np.float32)
```

```python
import contextlib
import math
import numpy as np
import ml_dtypes
import concourse.bass as bass
import concourse.mybir as mybir
from concourse.bass_utils import run_bass_kernel_spmd

F32 = mybir.dt.float32
BF16 = mybir.dt.bfloat16
AF = mybir.ActivationFunctionType
ALU = mybir.AluOpType
AX = mybir.AxisListType
NPBF = ml_dtypes.bfloat16

NCORE = 8
D = 2048
KC = 16
CTX = 256
GRID_W = 64
EPS = 1e-6
H_A, DA, DV_A, W_A = 8, 64, 128, 1024
H_M, P_M, D_INNER, N_STATE, G_M, D_CONV = 16, 64, 1024, 128, 2, 5
CONV_CH = D_INNER + 2 * G_M * N_STATE
H_C, D_NOPE, D_ROPE, D_VC, Q_LORA, KV_LORA, W_C = 8, 128, 64, 128, 512, 256, 1024
MLA_SCALE = (D_NOPE + D_ROPE) ** -0.5
D_FF = 4 * D
IN_SIZES = (2 * H_A * DA, 2 * H_A * DA, W_A, D_INNER, CONV_CH, 2 * H_M, Q_LORA, KV_LORA, D_ROPE, 3 * D)
P_IN = sum(IN_SIZES)
OFF = np.concatenate([[0], np.cumsum(IN_SIZES)]).tolist()
N_MISC = OFF[9] - OFF[3]
COLPERM = np.concatenate([np.arange(0, OFF[5]), np.arange(OFF[6], OFF[9]), np.arange(OFF[5], OFF[6])])
C_CQ, C_CKV, C_KR, C_DT = 5632, 6144, 6400, 6464
M_Z, M_XBC, M_CQ, M_CKV, M_KR, M_DT = 0, 1024, 2560, 3072, 3328, 3392


class Buf:
    __slots__ = ("name", "lw", "rd", "dsem", "dcnt", "excl")

    def __init__(self, name, excl=False):
        self.name = name
        self.excl = excl
        self.lw = None
        self.rd = []
        self.dsem = None
        self.dcnt = 0


class KB:
    ENG = ("pe", "act", "dve", "pool", "sp")

    def __init__(self):
        self.nc = bass.Bass("TRN2", target_bir_lowering=False)
        self.prog = {e: [] for e in self.ENG}
        self.cnt = {e: 0 for e in self.ENG}
        self.waited = {e: {} for e in self.ENG}
        self.semkeys = ["E" + e for e in self.ENG]
        self.final_waits = []
        self.n_instr = 0
        self._pools = {}
        self._uid = 0

    def dram(self, name, shape, dt, kind="Internal"):
        return self.nc.dram_tensor(name, list(shape), dt, kind=kind).ap()

    def sbuf(self, name, shape, dt):
        return self.nc.alloc_sbuf_tensor(name, list(shape), dt)

    def psum(self, name, shape, dt=F32):
        return self.nc.alloc_psum_tensor(name, list(shape), dt)

    def pool(self, name, shape, dt, n, space="sbuf"):
        tiles = []
        for i in range(n):
            t = (self.sbuf if space == "sbuf" else self.psum)("%s%d" % (name, i), shape, dt)
            tiles.append((t, Buf("%s%d" % (name, i), excl=(space == "psum"))))
        st = {"i": 0}

        def nxt():
            r = tiles[st["i"] % n]
            st["i"] += 1
            return r
        return nxt

    def _deps(self, eng, reads, writes):
        me = "E" + eng
        deps = []
        for b in reads:
            if b.lw is not None:
                deps.append(b.lw)
            if b.excl:
                for r in b.rd:
                    if r[0] != me:
                        deps.append(r)
        for b in writes:
            if b.lw is not None:
                deps.append(b.lw)
            for r in b.rd:
                deps.append(r)
        return deps

    def _emit_waits(self, eng, deps):
        need = {}
        for (k, v) in deps:
            if k == "Epe" and eng == "pe":
                continue
            if need.get(k, 0) < v:
                need[k] = v
        w = self.waited[eng]
        for k, v in need.items():
            if w.get(k, 0) < v:
                w[k] = v
                self.prog[eng].append(("wait", k, v))

    def op(self, eng, meth, reads=(), writes=(), signal=True, **kw):
        fn = (meth, kw)
        deps = self._deps(eng, reads, writes)
        self._emit_waits(eng, deps)
        k = "E" + eng
        if signal:
            self.cnt[eng] += 1
            v = self.cnt[eng]
        else:
            v = self.cnt[eng] + 1
        self.prog[eng].append(("op", fn, k if signal else None, 1))
        for b in reads:
            b.rd.append((k, v))
        for b in writes:
            b.lw = (k, v)
            b.rd = []
        self.n_instr += 1

    def dma(self, q, out, in_, reads=(), writes=(), sembuf=None, **kw):
        deps = self._deps("dma", reads, writes)
        self._emit_waits(q, deps)
        sb = sembuf if sembuf is not None else (list(writes) + list(reads))[0]
        if sb.dsem is None:
            sb.dsem = "D%d" % len(self.semkeys)
            self.semkeys.append(sb.dsem)
        sb.dcnt += 16
        k, v = sb.dsem, sb.dcnt
        kw = dict(kw)
        kw["out"] = out
        kw["in_"] = in_
        self.prog[q].append(("op", ("dma_start", kw), k, 16))
        for b in reads:
            b.rd.append((k, v))
        for b in writes:
            b.lw = (k, v)
            b.rd = []
        self.n_instr += 1
        return (k, v)

    def store(self, q, out, in_, src_buf):
        d = self.dma(q, out, in_, reads=[src_buf])
        self.final_waits.append(d)
        return d

    def build(self):
        nc = self.nc
        fin = {}
        for (k, v) in self.final_waits:
            fin[k] = max(fin.get(k, 0), v)
        for k, v in fin.items():
            self.prog["sp"].append(("wait", k, v))
        sems = {}
        with contextlib.ExitStack() as st:
            for k in self.semkeys:
                sems[k] = st.enter_context(nc.semaphore(k))
            block = st.enter_context(nc.Block())

            def runner(prog):
                def f(e):
                    for it in prog:
                        if it[0] == "wait":
                            e.wait_ge(sems[it[1]], it[2])
                        else:
                            m, kw = it[1]
                            a = kw.pop("_args", ())
                            ins = getattr(e, m)(*a, **kw)
                            if it[2] is not None:
                                ins.then_inc(sems[it[2]], it[3])
                return f
            block.tensor(runner(self.prog["pe"]))
            block.scalar(runner(self.prog["act"]))
            block.vector(runner(self.prog["dve"]))
            block.gpsimd(runner(self.prog["pool"]))
            block.sync(runner(self.prog["sp"]))
        return nc


def run(nc, in_maps):
    res = run_bass_kernel_spmd(nc, in_maps, core_ids=list(range(NCORE)))
    return res.results


CAST_W = ("w_in", "w_o_diff", "w_o_ssm", "w_uq", "w_ukv", "w_o_mla", "w_out", "w_mlp1", "w_mlp2")


def build_l0(L, wshapes):
    kb = KB()
    ncol = 6 * D // NCORE
    cc = kb.dram("cc", [D, 2], F32, kind="ExternalInput")
    wada = kb.dram("wada", [L, D, ncol], F32, kind="ExternalInput")
    bada = kb.dram("bada", [L, 2, ncol], F32, kind="ExternalInput")
    mods = kb.dram("mods", [L, 2, ncol], F32, kind="ExternalOutput")
    stg = kb.pool("stg", [128, 4096], F32, 3)
    cvt = kb.pool("cvt", [128, 4096], BF16, 3)
    ci = 0
    for name in CAST_W:
        rows, cols = wshapes[name]
        src = kb.dram(name, [rows, cols], F32, kind="ExternalInput")
        dst = kb.dram(name + "_bf", [rows, cols], BF16, kind="ExternalOutput")
        for r0 in range(0, rows, 128):
            pr = min(128, rows - r0)
            for c0 in range(0, cols, 4096):
                cw = min(4096, cols - c0)
                s_t, s_b = stg()
                c_t, c_b = cvt()
                kb.dma("sp", s_t[0:pr, 0:cw], src[r0:r0 + pr, c0:c0 + cw], writes=[s_b])
                eng = ("dve", "pool")[ci % 2]
                ci += 1
                kb.op(eng, "tensor_copy", out=c_t[0:pr, 0:cw], in_=s_t[0:pr, 0:cw], reads=[s_b], writes=[c_b])
                kb.store("act", dst[r0:r0 + pr, c0:c0 + cw], c_t[0:pr, 0:cw], c_b)
    cc_sb = kb.sbuf("cc_sb", [128, KC, 2], F32)
    CCB = Buf("cc")
    sc_sb = kb.sbuf("sc_sb", [128, KC, 2], F32)
    SCB = Buf("sc")
    kb.dma("sp", cc_sb[:], cc.rearrange("(j p) t -> p j t", p=128), writes=[CCB])
    kb.op("act", "activation", out=sc_sb[:], in_=cc_sb[:], func=AF.Silu, reads=[CCB], writes=[SCB])
    wst = kb.pool("wst", [128, KC, 512], F32, 2)
    psm = kb.pool("psm", [128, 512], F32, 2, space="psum")
    bsb = kb.pool("bsb", [2, 512], F32, 2)
    osb = kb.pool("osb", [2, 512], F32, 2)
    for l in range(L):
        for c0 in range(0, ncol, 512):
            w_t, w_b = wst()
            kb.dma("sp", w_t[:], wada[l, :, c0:c0 + 512].rearrange("(j p) n -> p j n", p=128), writes=[w_b])
            b_t, b_b = bsb()
            kb.dma("sp", b_t[:], bada[l, :, c0:c0 + 512], writes=[b_b])
            p_t, p_b = psm()
            for j in range(KC):
                kb.op("pe", "matmul", out=p_t[0:2, :], lhsT=sc_sb[:, j, :], rhs=w_t[:, j, :], start=(j == 0), stop=(j == KC - 1),
                      reads=[SCB, w_b], writes=[p_b], signal=(j == KC - 1))
            o_t, o_b = osb()
            kb.op("dve", "tensor_tensor", out=o_t[:], in0=p_t[0:2, :], in1=b_t[:], op=ALU.add, reads=[p_b, b_b], writes=[o_b])
            kb.store("sp", mods[l, :, c0:c0 + 512], o_t[:], o_b)
    return kb.build()


def run_l0(inp, L):
    wsl, wshapes = {}, {}
    for name in CAST_W:
        w = inp[name]
        flat = w.reshape(-1, w.shape[-1])
        rows = flat.shape[0] // NCORE
        wshapes[name] = (rows, flat.shape[1])
        wsl[name] = [np.ascontiguousarray(flat[i * rows:(i + 1) * rows]) for i in range(NCORE)]
    nc = build_l0(L, wshapes)
    ncol = 6 * D // NCORE
    cc = np.ascontiguousarray(np.stack([inp["c"][0], inp["c_ctx"]], axis=1))
    in_maps = []
    for i in range(NCORE):
        m = {"cc": cc,
             "wada": np.ascontiguousarray(inp["w_ada"][:, :, i * ncol:(i + 1) * ncol]),
             "bada": np.ascontiguousarray(np.repeat(inp["b_ada"][:, None, i * ncol:(i + 1) * ncol], 2, axis=1))}
        for name in CAST_W:
            m[name] = wsl[name][i]
        in_maps.append(m)
    res = run(nc, in_maps)
    mods = np.concatenate([r["mods"] for r in res], axis=2)
    wbf = {}
    for name in CAST_W:
        full = np.concatenate([r[name + "_bf"] for r in res], axis=0)
        wbf[name] = full.reshape(inp[name].shape)
    return mods, wbf


def groups_of(Tl, with_ctx=True):
    gs = []
    tg = min(512, Tl)
    for t0 in range(0, Tl, tg):
        gs.append((t0, tg, True))
    if with_ctx:
        gs.append((Tl, CTX, False))
    return gs


class Dev:
    def __init__(self, kb, nps=8):
        self.kb = kb
        self.ps = kb.pool("ps", [128, 512], F32, nps, space="psum")
        self.ones_d = kb.dram("ones", [128, 128], F32, kind="ExternalInput")
        self.ones = kb.sbuf("ones_sb", [128, 128], F32)
        self.ONES = Buf("ones")
        kb.dma("sp", self.ones[:], self.ones_d[:], writes=[self.ONES])
        self.acc = kb.pool("nacc", [128, 512], F32, 2)
        self.rstd = kb.pool("nrstd", [128, 512], F32, 2)
        self.ntmp = kb.pool("ntmp", [128, 512], F32, 3)
        self.epsb = kb.sbuf("epsb", [128, 1], F32)
        self.EPSB = Buf("epsb")
        kb.op("dve", "memset", _args=(self.epsb[:], EPS), writes=[self.EPSB])

    def rstd_fm(self, src, SRC, kcn, tn, dn, sq, SQ):
        kb = self.kb
        kb.op("act", "activation", out=sq[:, 0:kcn, 0:tn], in_=src[:, 0:kcn, 0:tn], func=AF.Square,
              reads=[SRC], writes=[SQ])
        a_t, a_b = self.acc()
        if kcn > 1:
            kb.op("dve", "tensor_reduce", out=a_t[:, 0:tn],
                                                   in_=sq[:, 0:kcn, 0:tn].rearrange("p j t -> p t j"),
                                                   axis=AX.X, op=ALU.add, reads=[SQ], writes=[a_b])
        else:
            kb.op("dve", "tensor_copy", out=a_t[:, 0:tn], in_=sq[:, 0, 0:tn], reads=[SQ], writes=[a_b])
        p_t, p_b = self.ps()
        kb.op("pe", "matmul", out=p_t[:, 0:tn], lhsT=self.ones[:], rhs=a_t[:, 0:tn], start=True, stop=True,
              reads=[self.ONES, a_b], writes=[p_b])
        r_t, r_b = self.rstd()
        kb.op("act", "activation", out=r_t[:, 0:tn], in_=p_t[:, 0:tn], func=AF.Sqrt,
                                            scale=1.0 / dn, bias=self.epsb[:],
              reads=[p_b, self.EPSB], writes=[r_b])
        kb.op("dve", "reciprocal", out=r_t[:, 0:tn], in_=r_t[:, 0:tn], reads=[r_b], writes=[r_b])
        return r_t, r_b

    def norm_fm(self, src, SRC, kcn, tn, dn, gs, shift, PV, dst, DST, sq, SQ):
        kb = self.kb
        r_t, r_b = self.rstd_fm(src, SRC, kcn, tn, dn, sq, SQ)
        for j in range(kcn):
            if shift is None:
                kb.op("dve", "scalar_tensor_tensor",
                    out=dst[:, j, 0:tn], in0=src[:, j, 0:tn], scalar=gs[:, j:j + 1], in1=r_t[:, 0:tn],
                    op0=ALU.mult, op1=ALU.mult, reads=[SRC, PV, r_b], writes=[DST])
            else:
                t_t, t_b = self.ntmp()
                kb.op("dve", "scalar_tensor_tensor",
                    out=t_t[:, 0:tn], in0=src[:, j, 0:tn], scalar=gs[:, j:j + 1], in1=r_t[:, 0:tn],
                    op0=ALU.mult, op1=ALU.mult, reads=[SRC, PV, r_b], writes=[t_b])
                kb.op("act", "activation",
                    out=dst[:, j, 0:tn], in_=t_t[:, 0:tn], func=AF.Identity, bias=shift[:, j:j + 1],
                    reads=[t_b, PV], writes=[DST])


def pvec(v):
    v = np.asarray(v, np.float32).reshape(-1, 128)
    return np.ascontiguousarray(v.T)


def rope_tables(pos):
    row = (pos // GRID_W).astype(np.float32)
    col = (pos % GRID_W).astype(np.float32)
    nf = 16
    inv = (1.0 / (10000.0 ** (np.arange(nf, dtype=np.float32) / nf))).astype(np.float32)
    ang = np.concatenate([row[:, None] * inv, col[:, None] * inv], axis=-1).astype(np.float32)
    c, s = np.cos(ang).astype(np.float32), np.sin(ang).astype(np.float32)
    C = np.empty((128, len(pos)), np.float32)
    S = np.empty((128, len(pos)), np.float32)
    for r in range(128):
        C[r] = c[:, r % 32]
        S[r] = (-s[:, r % 32]) if (r % 64) < 32 else s[:, r % 32]
    return C, S


def rot_perm():
    R = np.zeros((128, 128), np.float32)
    for m in range(128):
        k = m + 32 if (m % 64) < 32 else m - 32
        R[k, m] = 1.0
    return R.astype(NPBF)


NA = OFF[9]
NV_LA = 5 * KC + 4 + 2


DBG = {}


def build_la(Tl):
    kb = KB()
    dv = Dev(kb)
    T = Tl + CTX
    sT = kb.dram("sT", [D, T], F32, kind="ExternalInput")
    pv_d = kb.dram("pv", [128, NV_LA], F32, kind="ExternalInput")
    win = kb.dram("win", [D, NA], BF16, kind="ExternalInput")
    wuq_d = kb.dram("wuq", [Q_LORA, 1536], BF16, kind="ExternalInput")
    wukv_d = kb.dram("wukv", [KV_LORA, 2048], BF16, kind="ExternalInput")
    rc_d = kb.dram("ropeC", [128, Tl], F32, kind="ExternalInput")
    rs_d = kb.dram("ropeS", [128, Tl], F32, kind="ExternalInput")
    rp_d = kb.dram("rperm", [128, 128], BF16, kind="ExternalInput")
    qkT = kb.dram("qkT", [2048, T], BF16, kind="ExternalOutput")
    v_o = kb.dram("v", [T, 1024], BF16, kind="ExternalOutput")
    miscT = kb.dram("miscT", [N_MISC, T], F32, kind="ExternalOutput")
    qmT = kb.dram("qmT", [1536, T], BF16, kind="ExternalOutput")
    kmT = kb.dram("kmT", [1024, T], BF16, kind="ExternalOutput")
    vm_o = kb.dram("vm", [T, 1024], BF16, kind="ExternalOutput")
    krT = kb.dram("krT", [64, T], BF16, kind="ExternalOutput")

    pv = kb.sbuf("pv_sb", [128, NV_LA], F32)
    PV = Buf("pv")
    kb.dma("sp", pv[:], pv_d[:], writes=[PV])
    gsl = kb.sbuf("gsl", [128, KC], F32)
    gsc = kb.sbuf("gsc", [128, KC], F32)
    GS = Buf("gs")
    g1, shl, scl, shc, scc = (pv[:, i * KC:(i + 1) * KC] for i in range(5))
    qg = pv[:, 5 * KC:5 * KC + 4]
    kvg = pv[:, 5 * KC + 4:5 * KC + 6]
    kb.op("dve", "scalar_tensor_tensor", out=gsl[:], in0=scl, scalar=1.0, in1=g1, op0=ALU.add, op1=ALU.mult,
          reads=[PV], writes=[GS])
    kb.op("dve", "scalar_tensor_tensor", out=gsc[:], in0=scc, scalar=1.0, in1=g1, op0=ALU.add, op1=ALU.mult,
          reads=[PV], writes=[GS])
    ropeC = kb.sbuf("ropeC_sb", [128, Tl], F32)
    ropeS = kb.sbuf("ropeS_sb", [128, Tl], F32)
    RTC = Buf("rtc")
    RTS = Buf("rts")
    kb.dma("sp", ropeC[:], rc_d[:], writes=[RTC])
    kb.dma("sp", ropeS[:], rs_d[:], writes=[RTS])
    RT2 = Buf("rt2b")
    rperm = kb.sbuf("rperm_sb", [128, 128], BF16)
    kb.dma("sp", rperm[:], rp_d[:], writes=[RT2])
    wuq = kb.sbuf("wuq_sb", [128, 4, 1536], BF16)
    wukv = kb.sbuf("wukv_sb", [128, 2, 2048], BF16)
    WU = Buf("wu")
    WU2 = Buf("wu2")
    kb.dma("sp", wuq[:], wuq_d.rearrange("(j p) n -> p j n", p=128), writes=[WU])
    kb.dma("sp", wukv[:], wukv_d.rearrange("(j p) n -> p j n", p=128), writes=[WU2])

    sTg = kb.sbuf("sTg", [128, KC, 512], F32)
    STG = Buf("sTg")
    sq = kb.sbuf("sq", [128, KC, 512], F32)
    SQ = Buf("sq")
    hT = kb.sbuf("hT", [128, KC, 512], BF16)
    HT = Buf("hT")
    wblk = kb.pool("wblk", [128, KC, 512], BF16, 2)
    mla_in = kb.sbuf("mla_in", [128, 7, 512], F32)
    MLI = Buf("mli")
    mla_n = kb.sbuf("mla_n", [128, 7, 512], BF16)
    MLN = Buf("mln")
    stf = kb.pool("stf", [128, 512], F32, 3)
    stb = kb.pool("stb", [128, 512], BF16, 4)
    rt1 = kb.pool("rt1", [128, 512], F32, 2)
    rt2 = kb.pool("rt2", [128, 512], F32, 2)
    qb = kb.pool("qb", [128, 512], BF16, 2)

    def rope_epi(p_t, p_b, nr, t0, tn, dst_ap):
        q_t, q_b = qb()
        kb.op("act", "activation", out=q_t[0:nr, 0:tn], in_=p_t[0:nr, 0:tn], func=AF.Copy,
              reads=[p_b], writes=[q_b])
        p2, p2b = dv.ps()
        kb.op("pe", "matmul", out=p2[0:nr, 0:tn], lhsT=rperm[0:nr, 0:nr], rhs=q_t[0:nr, 0:tn], start=True, stop=True,
              reads=[RT2, q_b], writes=[p2b])
        a_t, a_b = rt1()
        kb.op("dve", "tensor_tensor", out=a_t[0:nr, 0:tn], in0=p_t[0:nr, 0:tn], in1=ropeC[0:nr, t0:t0 + tn], op=ALU.mult,
              reads=[p_b, RTC], writes=[a_b])
        b_t, b_b = rt2()
        kb.op("dve", "tensor_tensor", out=b_t[0:nr, 0:tn], in0=p2[0:nr, 0:tn], in1=ropeS[0:nr, t0:t0 + tn], op=ALU.mult,
              reads=[p2b, RTS], writes=[b_b])
        o_t, o_b = stb()
        kb.op("pool", "tensor_tensor", out=o_t[0:nr, 0:tn], in0=a_t[0:nr, 0:tn], in1=b_t[0:nr, 0:tn], op=ALU.add,
              reads=[a_b, b_b], writes=[o_b])
        kb.store("sp", dst_ap, o_t[0:nr, 0:tn], o_b)

    def copy_epi(p_t, p_b, nr, nc_, dst_ap, bf, eng="act"):
        o_t, o_b = (stb if bf else stf)()
        if eng == "act":
            kb.op("act", "activation", out=o_t[0:nr, 0:nc_], in_=p_t[0:nr, 0:nc_], func=AF.Copy,
                  reads=[p_b], writes=[o_b])
        else:
            kb.op("dve", "tensor_copy", out=o_t[0:nr, 0:nc_], in_=p_t[0:nr, 0:nc_], reads=[p_b], writes=[o_b])
        kb.store("sp", dst_ap, o_t[0:nr, 0:nc_], o_b)

    for (t0, tn, is_lat) in groups_of(Tl)[:DBG.get('groups', 99)]:
        kb.dma("sp", sTg[:, :, 0:tn], sT[:, t0:t0 + tn].rearrange("(j p) t -> p j t", p=128), writes=[STG])
        dv.norm_fm(sTg, STG, KC, tn, D, gsl if is_lat else gsc, shl if is_lat else shc, GS, hT, HT, sq, SQ)
        for c0 in range(0, NA, 512)[DBG.get('b0', 0):DBG.get('b1', 99)]:
            cw = min(512, NA - c0)
            w_t, w_b = wblk()
            kb.dma("sp", w_t[:, :, 0:cw], win[:, c0:c0 + cw].rearrange("(j p) n -> p j n", p=128), writes=[w_b])
            if 2048 <= c0 < 3072:
                for tt in range(0, tn, 128):
                    p_t, p_b = dv.ps()
                    for j in range(KC):
                        kb.op("pe", "matmul", out=p_t[:, 0:cw], lhsT=hT[:, j, tt:tt + 128], rhs=w_t[:, j, 0:cw], start=(j == 0), stop=(j == KC - 1),
                            reads=[HT, w_b], writes=[p_b], signal=(j == KC - 1))
                    copy_epi(p_t, p_b, 128, cw, v_o[t0 + tt:t0 + tt + 128, c0 - 2048:c0 - 2048 + cw], True,
                             eng=("act", "dve")[(tt // 128) % 2])
                continue
            chunks = [(cc, min(128, cw - cc)) for cc in range(0, cw, 128)]
            if c0 + cw == NA:
                chunks = chunks[:-1] + [(C_KR - c0, 64), (C_DT - c0, 32)]
            for (cc, nr) in chunks:
                col = c0 + cc
                p_t, p_b = dv.ps()
                for j in range(KC):
                    kb.op("pe", "matmul", out=p_t[0:nr, 0:tn], lhsT=w_t[:, j, cc:cc + nr], rhs=hT[:, j, 0:tn], start=(j == 0), stop=(j == KC - 1),
                        reads=[HT, w_b], writes=[p_b], signal=(j == KC - 1))
                if col < 2048:
                    if is_lat:
                        rope_epi(p_t, p_b, 128, t0, tn, qkT[col:col + 128, t0:t0 + tn])
                    else:
                        copy_epi(p_t, p_b, 128, tn, qkT[col:col + 128, t0:t0 + tn], True)
                else:
                    mrow = col - 3072
                    if C_CQ <= col < C_DT:
                        ci = (col - C_CQ) // 128
                        kb.op("dve", "tensor_copy", out=mla_in[0:nr, ci, 0:tn], in_=p_t[0:nr, 0:tn],
                              reads=[p_b], writes=[MLI])
                    copy_epi(p_t, p_b, nr, tn, miscT[mrow:mrow + nr, t0:t0 + tn], False)
        if DBG.get('nomla'):
            continue
        dv.norm_fm(mla_in[:, 0:4, :], MLI, 4, tn, Q_LORA, qg, None, PV, mla_n[:, 0:4, :], MLN, sq, SQ)
        dv.norm_fm(mla_in[:, 4:6, :], MLI, 2, tn, KV_LORA, kvg, None, PV, mla_n[:, 4:6, :], MLN, sq, SQ)
        for h in range(H_C):
            for (c_lo, nr, is_rope) in ((h * 192, 128, False), (h * 192 + 128, 64, True)):
                p_t, p_b = dv.ps()
                for j in range(4):
                    kb.op("pe", "matmul", out=p_t[0:nr, 0:tn], lhsT=wuq[:, j, c_lo:c_lo + nr], rhs=mla_n[:, j, 0:tn], start=(j == 0), stop=(j == 3),
                        reads=[MLN, WU], writes=[p_b], signal=(j == 3))
                if is_rope and is_lat:
                    rope_epi(p_t, p_b, nr, t0, tn, qmT[c_lo:c_lo + nr, t0:t0 + tn])
                else:
                    copy_epi(p_t, p_b, nr, tn, qmT[c_lo:c_lo + nr, t0:t0 + tn], True)
            p_t, p_b = dv.ps()
            for j in range(2):
                kb.op("pe", "matmul", out=p_t[:, 0:tn], lhsT=wukv[:, j, h * 256:h * 256 + 128], rhs=mla_n[:, 4 + j, 0:tn], start=(j == 0), stop=(j == 1),
                    reads=[MLN, WU2], writes=[p_b], signal=(j == 1))
            copy_epi(p_t, p_b, 128, tn, kmT[h * 128:(h + 1) * 128, t0:t0 + tn], True, eng="dve")
        for tt in range(0, tn, 128):
            for half in range(2):
                p_t, p_b = dv.ps()
                for hh in range(4):
                    h = half * 4 + hh
                    for j in range(2):
                        kb.op("pe", "matmul", out=p_t[:, hh * 128:(hh + 1) * 128], lhsT=mla_n[:, 4 + j, tt:tt + 128],
                            rhs=wukv[:, j, h * 256 + 128:h * 256 + 256], start=(j == 0), stop=(j == 1),
                            reads=[MLN, WU2], writes=[p_b], signal=(hh == 3 and j == 1))
                copy_epi(p_t, p_b, 128, 512, vm_o[t0 + tt:t0 + tt + 128, half * 512:(half + 1) * 512], True)
        krp, krb = dv.ps()
        kb.op("dve", "tensor_copy", out=krp[0:64, 0:tn], in_=mla_in[0:64, 6, 0:tn], reads=[MLI], writes=[krb])
        if is_lat:
            rope_epi(krp, krb, 64, t0, tn, krT[:, t0:t0 + tn])
        else:
            copy_epi(krp, krb, 64, tn, krT[:, t0:t0 + tn], True)
    return kb.build()


def la_consts(Tl, core):
    pos = np.arange(core * Tl, (core + 1) * Tl)
    C, S = rope_tables(pos)
    return {"ropeC": C, "ropeS": S, "rperm": rot_perm(), "ones": np.ones((128, 128), np.float32)}


def run_la(sT_lat, sT_ctx, mods_l, inp, wbf, l):
    S = sT_lat.shape[1]
    Tl = S // NCORE
    nc = build_la(Tl)
    mx = mods_l[0].reshape(6, D)
    mc = mods_l[1].reshape(6, D)
    pv = np.concatenate([pvec(inp["norm1_g"][l]), pvec(mx[0]), pvec(mx[1]), pvec(mc[0]), pvec(mc[1]),
                         pvec(inp["q_norm_g"][l]), pvec(inp["kv_norm_g"][l])], axis=1)
    win = np.ascontiguousarray(wbf["w_in"][l][:, COLPERM])
    in_maps = []
    for i in range(NCORE):
        m = {"sT": np.ascontiguousarray(np.concatenate([sT_lat[:, i * Tl:(i + 1) * Tl], sT_ctx], axis=1)),
             "pv": pv, "win": win, "wuq": wbf["w_uq"][l], "wukv": wbf["w_ukv"][l]}
        m.update(la_consts(Tl, i))
        in_maps.append(m)
    return run(nc, in_maps)


TG_LD = 512
NV_LD = 15 * KC + 8
GATE0 = OFF[9]


def build_ld(Tl, last):
    kb = KB()
    dv = Dev(kb)
    T = Tl if last else Tl + CTX
    sT = kb.dram("sT", [D, T], F32, kind="ExternalInput")
    oaT = kb.dram("oaT", [1024, T], BF16, kind="ExternalInput")
    ygT = kb.dram("ygT", [1024, T], F32, kind="ExternalInput")
    ocT = kb.dram("ocT", [1024, T], BF16, kind="ExternalInput")
    pv_d = kb.dram("pv", [128, NV_LD], F32, kind="ExternalInput")
    wg = kb.dram("wg", [D, 3 * D], BF16, kind="ExternalInput")
    wod = kb.dram("wod", [1024, D], BF16, kind="ExternalInput")
    wos = kb.dram("wos", [1024, D], BF16, kind="ExternalInput")
    wom = kb.dram("wom", [1024, D], BF16, kind="ExternalInput")
    wout = kb.dram("wout", [D, D], BF16, kind="ExternalInput")
    w1 = kb.dram("w1", [D, D_FF], BF16, kind="ExternalInput")
    w2 = kb.dram("w2", [D_FF, D], BF16, kind="ExternalInput")
    sO = kb.dram("sO", [D, T], F32, kind="ExternalOutput")

    pv = kb.sbuf("pv_sb", [128, NV_LD], F32)
    PV = Buf("pv")
    kb.dma("sp", pv[:], pv_d[:], writes=[PV])
    col = lambda i: pv[:, i * KC:(i + 1) * KC]
    ssg = pv[:, 15 * KC:15 * KC + 8]
    gs = kb.sbuf("gs_sb", [128, 4, KC], F32)
    GS = Buf("gs")
    for k, (sc_i, g_i) in enumerate(((2, 0), (4, 0), (9, 7), (11, 7))):
        kb.op("dve", "scalar_tensor_tensor", out=gs[:, k, :], in0=col(sc_i), scalar=1.0, in1=col(g_i),
              op0=ALU.add, op1=ALU.mult, reads=[PV], writes=[GS])

    tg = min(TG_LD, Tl)
    sTg = kb.sbuf("sTg", [128, KC, tg], F32)
    STG = Buf("sTg")
    hT = kb.sbuf("hT", [128, KC, tg], BF16)
    HT = Buf("hT")
    oa = kb.sbuf("oa", [128, 8, tg], BF16)
    OA = Buf("oa")
    oc = kb.sbuf("oc", [128, 8, tg], BF16)
    OC = Buf("oc")
    yg = kb.sbuf("yg", [128, 8, tg], F32)
    YG = Buf("yg")
    ysn = kb.sbuf("ysn", [128, 8, tg], BF16)
    YSN = Buf("ysn")
    mg = kb.sbuf("mg", [128, KC, tg], BF16)
    MG = Buf("mg")
    uT = kb.pool("uT", [128, KC, tg], BF16, 1)
    wt = kb.pool("wt", [128, KC, 256], BF16, 6)
    sg = kb.pool("sg", [128, tg], F32, 2)
    tt_ = kb.pool("tt", [128, tg], F32, 2)
    t01 = kb.pool("t01", [128, tg], F32, 2)
    rl = kb.pool("rl", [128, tg], F32, 2)
    fo = kb.pool("fo", [128, tg], F32, 2)

    def wload(src, r0, nkc, c0, ncol=256):
        w_t, w_b = wt()
        kb.dma("sp", w_t[:, 0:nkc, 0:ncol], src[r0:r0 + nkc * 128, c0:c0 + ncol].rearrange("(j p) n -> p j n", p=128),
               writes=[w_b])
        return w_t, w_b

    def mmgroup(p_t, p_b, w_t, w_b, nkc, cc, x, XB, tn):
        for j in range(nkc):
            kb.op("pe", "matmul", out=p_t[:, 0:tn], lhsT=w_t[:, j, cc:cc + 128], rhs=x[:, j, 0:tn],
                  start=(j == 0), stop=(j == nkc - 1), reads=[w_b, XB], writes=[p_b], signal=(j == nkc - 1))

    groups = []
    for t0 in range(0, Tl, tg):
        groups.append((t0, tg, True))
    if not last:
        for t0 in range(Tl, Tl + CTX, tg):
            groups.append((t0, min(tg, CTX), False))
    for (t0, tn, is_lat) in groups:
        li = 0 if is_lat else 1
        kb.dma("sp", sTg[:, :, 0:tn], sT[:, t0:t0 + tn].rearrange("(j p) t -> p j t", p=128), writes=[STG])
        kb.dma("sp", oa[:, :, 0:tn], oaT[:, t0:t0 + tn].rearrange("(j p) t -> p j t", p=128), writes=[OA])
        kb.dma("sp", oc[:, :, 0:tn], ocT[:, t0:t0 + tn].rearrange("(j p) t -> p j t", p=128), writes=[OC])
        kb.dma("sp", yg[:, :, 0:tn], ygT[:, t0:t0 + tn].rearrange("(j p) t -> p j t", p=128), writes=[YG])
        dv.norm_fm(sTg, STG, KC, tn, D, gs[:, li, :], col(1 + 2 * li), GS, hT, HT, hT, HT)
        for g in range(G_M):
            dv.norm_fm(yg[:, 4 * g:4 * g + 4, :], YG, 4, tn, D_INNER // G_M, ssg[:, 4 * g:4 * g + 4], None, PV,
                       ysn[:, 4 * g:4 * g + 4, :], YSN, ysn[:, 4 * g:4 * g + 4, :], YSN)
        for c0 in range(0, D, 256):
            wts = [wload(wod, 0, 8, c0), wload(wos, 0, 8, c0), wload(wom, 0, 8, c0)]
            gts = [wload(wg, 0, KC, i * D + c0) for i in range(3)]
            for cc in (0, 128):
                m = (c0 + cc) // 128
                tacc = None
                for i, (x, XB) in enumerate(((oa, OA), (ysn, YSN), (oc, OC))):
                    py, pyb = dv.ps()
                    mmgroup(py, pyb, wts[i][0], wts[i][1], 8, cc, x, XB, tn)
                    pg, pgb = dv.ps()
                    mmgroup(pg, pgb, gts[i][0], gts[i][1], KC, cc, hT, HT, tn)
                    s_t, s_b = sg()
                    kb.op("act", "activation", out=s_t[:, 0:tn], in_=pg[:, 0:tn], func=AF.Sigmoid, reads=[pgb], writes=[s_b])
                    y_t, y_b = tt_()
                    kb.op("dve", "tensor_tensor", out=y_t[:, 0:tn], in0=py[:, 0:tn], in1=s_t[:, 0:tn], op=ALU.mult,
                          reads=[pyb, s_b], writes=[y_b])
                    if i == 0:
                        tacc = (y_t, y_b)
                    elif i == 1:
                        a_t, a_b = t01()
                        kb.op("pool", "tensor_tensor", out=a_t[:, 0:tn], in0=tacc[0][:, 0:tn], in1=y_t[:, 0:tn], op=ALU.add,
                              reads=[tacc[1], y_b], writes=[a_b])
                        tacc = (a_t, a_b)
                    else:
                        kb.op("pool", "tensor_tensor", out=mg[:, m, 0:tn], in0=tacc[0][:, 0:tn], in1=y_t[:, 0:tn], op=ALU.add,
                              reads=[tacc[1], y_b], writes=[MG])
        for c0 in range(0, D, 256):
            w_t, w_b = wload(wout, 0, KC, c0)
            for cc in (0, 128):
                m = (c0 + cc) // 128
                p_t, p_b = dv.ps()
                mmgroup(p_t, p_b, w_t, w_b, KC, cc, mg, MG, tn)
                kb.op("dve", "scalar_tensor_tensor", out=sTg[:, m, 0:tn], in0=p_t[:, 0:tn], scalar=pv[:, (5 + li) * KC + m:(5 + li) * KC + m + 1],
                      in1=sTg[:, m, 0:tn], op0=ALU.mult, op1=ALU.add, reads=[p_b, PV, STG], writes=[STG])
        dv.norm_fm(sTg, STG, KC, tn, D, gs[:, 2 + li, :], col(8 + 2 * li), GS, hT, HT, hT, HT)
        for hb in range(D_FF // 2048):
            u_t, u_b = uT()
            for c0 in range(0, 2048, 256):
                w_t, w_b = wload(w1, 0, KC, hb * 2048 + c0)
                for cc in (0, 128):
                    c = (c0 + cc) // 128
                    p_t, p_b = dv.ps()
                    mmgroup(p_t, p_b, w_t, w_b, KC, cc, hT, HT, tn)
                    r_t, r_b = rl()
                    kb.op("act", "activation", out=r_t[:, 0:tn], in_=p_t[:, 0:tn], func=AF.Relu, reads=[p_b], writes=[r_b])
                    kb.op("pool", "tensor_tensor", out=u_t[:, c, 0:tn], in0=r_t[:, 0:tn], in1=r_t[:, 0:tn], op=ALU.mult,
                          reads=[r_b], writes=[u_b])
            for c0 in range(0, D, 256):
                w_t, w_b = wload(w2, hb * 2048, KC, c0)
                for cc in (0, 128):
                    m = (c0 + cc) // 128
                    p_t, p_b = dv.ps()
                    mmgroup(p_t, p_b, w_t, w_b, KC, cc, u_t, u_b, tn)
                    kb.op("dve", "scalar_tensor_tensor", out=sTg[:, m, 0:tn], in0=p_t[:, 0:tn],
                          scalar=pv[:, (12 + li) * KC + m:(12 + li) * KC + m + 1], in1=sTg[:, m, 0:tn],
                          op0=ALU.mult, op1=ALU.add, reads=[p_b, PV, STG], writes=[STG])
        if not last:
            kb.store("sp", sO[:, t0:t0 + tn].rearrange("(j p) t -> p j t", p=128), sTg[:, :, 0:tn], STG)
        else:
            r_t, r_b = dv.rstd_fm(sTg, STG, KC, tn, D, hT, HT)
            for j in range(KC):
                o_t, o_b = fo()
                kb.op("dve", "scalar_tensor_tensor", out=o_t[:, 0:tn], in0=sTg[:, j, 0:tn], scalar=pv[:, 14 * KC + j:14 * KC + j + 1],
                      in1=r_t[:, 0:tn], op0=ALU.mult, op1=ALU.mult, reads=[STG, PV, r_b], writes=[o_b])
                kb.store("sp", sO[j * 128:(j + 1) * 128, t0:t0 + tn], o_t[:, 0:tn], o_b)
    return kb.build()


def run_ld(sT_lat, sT_ctx, oaT, ygT, ocT, mods_l, inp, wbf, l, last):
    S = sT_lat.shape[1]
    Tl = S // NCORE
    nc = build_ld(Tl, last)
    mx = mods_l[0].reshape(6, D)
    mc = mods_l[1].reshape(6, D)
    cols = [inp["norm1_g"][l], mx[0], mx[1], mc[0], mc[1], mx[2], mc[2], inp["norm2_g"][l], mx[3], mx[4], mc[3], mc[4],
            mx[5], mc[5], inp["final_norm_g"]]
    pv = np.concatenate([pvec(c) for c in cols] + [pvec(inp["ssm_norm_g"][l])], axis=1)
    wg = np.ascontiguousarray(wbf["w_in"][l][:, GATE0:])
    in_maps = []
    for i in range(NCORE):
        def sl(a_lat, a_ctx):
            parts = [a_lat[:, i * Tl:(i + 1) * Tl]] + ([] if last else [a_ctx])
            return np.ascontiguousarray(np.concatenate(parts, axis=1))
        m = {"sT": sl(sT_lat, sT_ctx), "oaT": sl(oaT[:, :S], oaT[:, S:]), "ygT": sl(ygT[:, :S], ygT[:, S:]),
             "ocT": sl(ocT[:, :S], ocT[:, S:]), "pv": pv, "wg": wg, "wod": wbf["w_o_diff"][l], "wos": wbf["w_o_ssm"][l],
             "wom": wbf["w_o_mla"][l], "wout": wbf["w_out"][l], "w1": wbf["w_mlp1"][l], "w2": wbf["w_mlp2"][l],
             "ones": np.ones((128, 128), np.float32)}
        in_maps.append(m)
    res = run(nc, in_maps)
    new_lat = np.concatenate([r["sO"][:, :Tl] for r in res], axis=1)
    new_ctx = None if last else res[0]["sO"][:, Tl:]
    return new_lat, new_ctx


LB_PE_MOD = 3


def build_lb(Tl, S, with_ctx, lam_init):
    kb = KB()
    dv = Dev(kb, nps=4)
    Tq = Tl + (CTX if with_ctx else 0)
    TK = S + CTX
    NKT = TK // 128
    qd = kb.dram("qd", [1024, Tq], BF16, kind="ExternalInput")
    kd = kb.dram("kd", [1024, TK], BF16, kind="ExternalInput")
    vd = kb.dram("vd", [TK, 1024], BF16, kind="ExternalInput")
    qm = kb.dram("qm", [1536, Tq], BF16, kind="ExternalInput")
    km = kb.dram("km", [1024, TK], BF16, kind="ExternalInput")
    kr = kb.dram("kr", [64, TK], BF16, kind="ExternalInput")
    vmd = kb.dram("vmd", [TK, 1024], BF16, kind="ExternalInput")
    lq_d = kb.dram("lq", [128, 256], F32, kind="ExternalInput")
    sg_d = kb.dram("subg", [128, 1], F32, kind="ExternalInput")
    oaT = kb.dram("oaT", [1024, Tq], BF16, kind="ExternalOutput")
    ocT = kb.dram("ocT", [1024, Tq], BF16, kind="ExternalOutput")

    lq = kb.sbuf("lq_sb", [128, 256], F32)
    LQ = Buf("lq")
    kb.dma("sp", lq[:], lq_d[:], writes=[LQ])
    subg = kb.sbuf("subg_sb", [128, 1], F32)
    SG = Buf("subg")
    kb.dma("sp", subg[:], sg_d[:], writes=[SG])
    lt = kb.sbuf("lt", [128, 128], F32)
    LT = Buf("lt")
    ls = kb.sbuf("ls", [128, 4], F32)
    LS = Buf("ls")
    kb.op("dve", "tensor_tensor", out=lt[:, 0:64], in0=lq[:, 0:64], in1=lq[:, 64:128], op=ALU.mult, reads=[LQ], writes=[LT])
    kb.op("dve", "tensor_tensor", out=lt[:, 64:128], in0=lq[:, 128:192], in1=lq[:, 192:256], op=ALU.mult, reads=[LQ], writes=[LT])
    kb.op("dve", "tensor_reduce", out=ls[:, 0:2], in_=lt[:].rearrange("p (a b) -> p a b", a=2), axis=AX.X, op=ALU.add,
          reads=[LT], writes=[LS])
    kb.op("act", "activation", out=ls[:, 0:2], in_=ls[:, 0:2], func=AF.Exp, reads=[LS], writes=[LS])
    kb.op("dve", "tensor_tensor", out=ls[:, 2:3], in0=ls[:, 1:2], in1=ls[:, 0:1], op=ALU.subtract, reads=[LS], writes=[LS])
    kb.op("dve", "tensor_scalar", out=ls[:, 2:3], in0=ls[:, 2:3], scalar1=-float(lam_init), scalar2=None, op0=ALU.add,
          reads=[LS], writes=[LS])
    kb.op("dve", "tensor_scalar", out=ls[:, 3:4], in0=subg[:, 0:1], scalar1=float(1.0 - lam_init), scalar2=None, op0=ALU.mult,
          reads=[SG, LS], writes=[LS])
    nlam = ls[:, 2:3]
    gsub = ls[:, 3:4]
    ones_bf = kb.sbuf("ones_bf", [128, 128], BF16)
    OB = Buf("onesbf")
    kb.op("dve", "tensor_copy", out=ones_bf[:], in_=dv.ones[:], reads=[dv.ONES], writes=[OB])

    tg = min(512, Tl)
    kA = kb.sbuf("kA", [128, TK], BF16)
    kB_ = kb.sbuf("kB", [128, TK], BF16)
    KBb = Buf("kB")
    kb.op("pool", "memset", _args=(kB_[64:128, :], 0.0), writes=[KBb])
    vt = kb.sbuf("vt", [128, NKT, 128], BF16)
    PK = 13 if NKT % 13 == 0 else (10 if NKT % 10 == 0 else NKT)
    VTs = [Buf("vt%d" % i) for i in range(NKT // PK)]

    def load_v(src, h):
        for i, b in enumerate(VTs):
            kb.dma("sp", vt[:, i * PK:(i + 1) * PK, :],
                   src[i * PK * 128:(i + 1) * PK * 128, h * 128:(h + 1) * 128].rearrange("(k p) e -> p k e", p=128), writes=[b])
    KAs = [Buf("kA%d" % i) for i in range(NKT // PK)]
    KA2 = [Buf("kA2_%d" % i) for i in range(NKT // PK)]

    def load_k(srcs):
        for i in range(NKT // PK):
            cs = slice(i * PK * 128, (i + 1) * PK * 128)
            for si, (r0, nr, src) in enumerate(srcs):
                kb.dma("sp", kA[r0:r0 + nr, cs], src[:, cs], writes=[KAs[i] if si == 0 else KA2[i]])
    qAp = kb.pool("qA", [128, Tq], BF16, 2)
    qBp = kb.pool("qB", [128, Tq], BF16, 2)
    q1p = kb.pool("q1z", [128, Tq], BF16, 2)
    q2p = kb.pool("q2z", [128, Tq], BF16, 2)
    for _ in range(2):
        t_, b_ = qBp()
        kb.op("pool", "memset", _args=(t_[64:128, :], 0.0), writes=[b_])
        t_, b_ = q1p()
        kb.op("pool", "memset", _args=(t_[64:128, :], 0.0), writes=[b_])
        t_, b_ = q2p()
        kb.op("pool", "memset", _args=(t_[0:64, :], 0.0), writes=[b_])
    stp = dv.ps
    pop = kb.pool("po", [128, 512], F32, 2, space="psum")
    pssp = kb.pool("pss", [128, 512], F32, 2, space="psum")
    ptp = kb.pool("pt", [128, 512], BF16, 10)
    accs = [(kb.pool("accD%d" % i, [128, 512], F32, 1), kb.pool("accP%d" % i, [128, 512], F32, 1)) for i in range(2)]
    rsp = kb.pool("rs", [128, 512], F32, 2)
    o1p = kb.pool("o1", [128, 1, 512], F32, 2)
    o2p = kb.pool("o2", [128, 512], F32, 2)
    sqp = kb.pool("sqs", [128, 1, 512], F32, 2)
    obp = kb.pool("ob", [128, 512], BF16, 3)

    qgroups = [(t0, tg, True) for t0 in range(0, Tl, tg)] + ([(Tl, CTX, False)] if with_ctx else [])

    def kreads(kbuf, kt):
        if not isinstance(kbuf, list):
            return [kbuf]
        if isinstance(kbuf[0], list):
            return [kk[kt // PK] for kk in kbuf]
        return [kbuf[kt // PK]]

    def flash(streams, scale, t0, tn, kts):
        ns = len(streams)
        look = 4 // ns - 1
        n = len(kts)
        pos = [pop() for _ in range(ns)]
        acc = [(accs[i][0](), accs[i][1]()) for i in range(ns)]
        psss = [pssp() for _ in range(ns)]
        pe_used = [False] * ns
        pts = {}
        for i in range(n + look):
            if i < n:
                kt = kts[i]
                for si, pairs in enumerate(streams):
                    st, stb = stp()
                    for pi, (kt_, kbuf, r0, nr, qt_, qbuf) in enumerate(pairs):
                        kb.op("pe", "matmul", out=st[:, 0:tn], lhsT=kt_[r0:r0 + nr, kt * 128:(kt + 1) * 128],
                              rhs=qt_[r0:r0 + nr, t0:t0 + tn], start=(pi == 0), stop=(pi == len(pairs) - 1),
                              reads=kreads(kbuf, kt) + [qbuf], writes=[stb],
                              signal=(pi == len(pairs) - 1))
                    p_t, p_b = ptp()
                    kb.op("act", "activation", out=p_t[:, 0:tn], in_=st[:, 0:tn], func=AF.Exp, scale=float(scale),
                          reads=[stb], writes=[p_b])
                    pts[(i, si)] = (p_t, p_b)
            j = i - look
            if j >= 0:
                kt = kts[j]
                for si in range(ns):
                    p_t, p_b = pts.pop((j, si))
                    po, pob = pos[si]
                    kb.op("pe", "matmul", out=po[:, 0:tn], lhsT=vt[:, kt, :], rhs=p_t[:, 0:tn], start=(j == 0), stop=(j == n - 1),
                          reads=[VTs[kt // PK], p_b], writes=[pob], signal=(j == n - 1))
                    if j % LB_PE_MOD == LB_PE_MOD - 1:
                        pss, pssb = psss[si]
                        kb.op("pe", "matmul", out=pss[:, 0:tn], lhsT=ones_bf[:], rhs=p_t[:, 0:tn], start=(j == LB_PE_MOD - 1), stop=False,
                              reads=[OB, p_b], writes=[pssb], signal=False)
                        pe_used[si] = True
                    else:
                        (a_t, a_b) = acc[si][0]
                        if j == 0:
                            kb.op("dve", "tensor_copy", out=a_t[:, 0:tn], in_=p_t[:, 0:tn], reads=[p_b], writes=[a_b])
                        else:
                            kb.op("dve", "tensor_tensor", out=a_t[:, 0:tn], in0=a_t[:, 0:tn], in1=p_t[:, 0:tn], op=ALU.add,
                                  reads=[a_b, p_b], writes=[a_b])
        outs = []
        for si in range(ns):
            (d_t, d_b), _unused = acc[si]
            pss, pssb = psss[si]
            kb.op("pe", "matmul", out=pss[:, 0:tn], lhsT=dv.ones[:], rhs=d_t[:, 0:tn], start=(not pe_used[si]), stop=True,
                  reads=[dv.ONES, d_b], writes=[pssb])
            r_t, r_b = rsp()
            kb.op("dve", "reciprocal", out=r_t[:, 0:tn], in_=pss[:, 0:tn], reads=[pssb], writes=[r_b])
            outs.append((pos[si][0], pos[si][1], r_t, r_b))
        return outs

    def qslices(is_lat):
        return list(range(NKT)) if is_lat else list(range(CTX // 128))

    for h in range(H_A):
        load_k([(0, 64, kd[h * 64:(h + 1) * 64, :]), (64, 64, kd[512 + h * 64:512 + (h + 1) * 64, :])])
        load_v(vd, h)
        qA, QA = q1p()
        qB, QB = q2p()
        kb.dma("sp", qA[0:64, :], qd[h * 64:(h + 1) * 64, :], writes=[QA])
        kb.dma("sp", qB[64:128, :], qd[512 + h * 64:512 + (h + 1) * 64, :], writes=[QB])
        for (t0, tn, is_lat) in qgroups:
            kts = qslices(is_lat)
            (po1, po1b, r1, r1b), (po2, po2b, r2, r2b) = flash(
                [[(kA, [KAs, KA2], 0, 128, qA, QA)], [(kA, [KAs, KA2], 0, 128, qB, QB)]], DA ** -0.5, t0, tn, kts)
            o1, o1b = o1p()
            kb.op("dve", "tensor_tensor", out=o1[:, 0, 0:tn], in0=po1[:, 0:tn], in1=r1[:, 0:tn], op=ALU.mult,
                  reads=[po1b, r1b], writes=[o1b])
            o2, o2b = o2p()
            kb.op("dve", "tensor_tensor", out=o2[:, 0:tn], in0=po2[:, 0:tn], in1=r2[:, 0:tn], op=ALU.mult,
                  reads=[po2b, r2b], writes=[o2b])
            kb.op("dve", "scalar_tensor_tensor", out=o1[:, 0, 0:tn], in0=o2[:, 0:tn], scalar=nlam, in1=o1[:, 0, 0:tn],
                  op0=ALU.mult, op1=ALU.add, reads=[o2b, LS, o1b], writes=[o1b])
            sq_t, sq_b = sqp()
            rr, rrb = dv.rstd_fm(o1, o1b, 1, tn, DV_A, sq_t, sq_b)
            ob, obb = obp()
            kb.op("dve", "scalar_tensor_tensor", out=ob[:, 0:tn], in0=o1[:, 0, 0:tn], scalar=gsub, in1=rr[:, 0:tn],
                  op0=ALU.mult, op1=ALU.mult, reads=[o1b, LS, rrb], writes=[obb])
            kb.store("sp", oaT[h * 128:(h + 1) * 128, t0:t0 + tn], ob[:, 0:tn], obb)
    kb.dma("sp", kB_[0:64, :], kr[:, :], writes=[KBb])
    for h in range(H_C):
        for i_ in range(NKT // PK):
            cs_ = slice(i_ * PK * 128, (i_ + 1) * PK * 128)
            kb.dma("sp", kA[:, cs_], km[h * 128:(h + 1) * 128, cs_], writes=[KAs[i_], KA2[i_]])
        load_v(vmd, h)
        qA, QA = qAp()
        qB, QB = qBp()
        kb.dma("sp", qA[:, :], qm[h * 192:h * 192 + 128, :], writes=[QA])
        kb.dma("sp", qB[0:64, :], qm[h * 192 + 128:(h + 1) * 192, :], writes=[QB])
        for (t0, tn, is_lat) in qgroups:
            kts = qslices(is_lat)
            ((po, pob, r_t, r_b),) = flash([[(kA, KAs, 0, 128, qA, QA), (kB_, KBb, 0, 128, qB, QB)]], MLA_SCALE, t0, tn, kts)
            ob, obb = obp()
            kb.op("dve", "tensor_tensor", out=ob[:, 0:tn], in0=po[:, 0:tn], in1=r_t[:, 0:tn], op=ALU.mult,
                  reads=[pob, r_b], writes=[obb])
            kb.store("sp", ocT[h * 128:(h + 1) * 128, t0:t0 + tn], ob[:, 0:tn], obb)
    return kb.build()


def lam_init_of(l):
    return 0.8 - 0.6 * math.exp(-0.3 * l)


def gather_fm(res, name, Tl):
    return np.ascontiguousarray(np.concatenate([res[0][name][:, Tl:]] + [r[name][:, :Tl] for r in res], axis=1))


def gather_tm(res, name, Tl):
    return np.ascontiguousarray(np.concatenate([res[0][name][Tl:]] + [r[name][:Tl] for r in res], axis=0))


def run_lb(la_res, S, inp, l, with_ctx):
    Tl = S // NCORE
    nc = build_lb(Tl, S, with_ctx, lam_init_of(l))
    kd = gather_fm(la_res, "qkT", Tl)[1024:]
    vd = gather_tm(la_res, "v", Tl)
    km = gather_fm(la_res, "kmT", Tl)
    kr = gather_fm(la_res, "krT", Tl)
    vmd = gather_tm(la_res, "vm", Tl)
    lq = np.ascontiguousarray(np.broadcast_to(inp["lam_qk"][l].reshape(1, 256), (128, 256))).astype(np.float32)
    subg = np.ascontiguousarray(inp["subln_g"][l].reshape(128, 1)).astype(np.float32)
    Tq = Tl + (CTX if with_ctx else 0)
    in_maps = []
    for i in range(NCORE):
        in_maps.append({"qd": np.ascontiguousarray(la_res[i]["qkT"][:1024, :Tq]), "kd": kd, "vd": vd,
                        "qm": np.ascontiguousarray(la_res[i]["qmT"][:, :Tq]), "km": km, "kr": kr, "vmd": vmd,
                        "lq": lq, "subg": subg, "ones": np.ones((128, 128), np.float32)})
    res = run(nc, in_maps)
    oaT = np.concatenate([r["oaT"][:, :Tl] for r in res] + ([res[0]["oaT"][:, Tl:]] if with_ctx else []), axis=1)
    ocT = np.concatenate([r["ocT"][:, :Tl] for r in res] + ([res[0]["ocT"][:, Tl:]] if with_ctx else []), axis=1)
    return oaT, ocT


def build_lc(S):
    kb = KB()
    TK = S + CTX
    NCH = TK // 128
    NQ = NCH * 4
    xr = kb.dram("xr", [3, 128, TK], F32, kind="ExternalInput")
    zr = kb.dram("zr", [128, TK], F32, kind="ExternalInput")
    dtr = kb.dram("dtr", [128, NQ], F32, kind="ExternalInput")
    dtb = kb.dram("dtb", [128, NQ], F32, kind="ExternalInput")
    alg = kb.dram("alg", [128, NQ], F32, kind="ExternalInput")
    cw_d = kb.dram("cw", [128, 3, 5], F32, kind="ExternalInput")
    cb_d = kb.dram("cb", [128, 3], F32, kind="ExternalInput")
    dsk_d = kb.dram("dsk", [128, 1], F32, kind="ExternalInput")
    tri_d = kb.dram("tri", [4, 128, 128], F32, kind="ExternalInput")
    id_d = kb.dram("ident", [128, 128], BF16, kind="ExternalInput")
    ones_d = kb.dram("ones", [128, 128], F32, kind="ExternalInput")
    ygT = kb.dram("ygT", [128, TK], F32, kind="ExternalOutput")
    yfs = kb.dram("yfs", [128, TK], F32)

    def cload(name, shape, dt, src):
        t = kb.sbuf(name, shape, dt)
        b = Buf(name)
        kb.dma("sp", t[:], src, writes=[b])
        return t, b
    tri, TRI = cload("tri_sb", [128, 4, 128], F32, tri_d.rearrange("k p n -> p k n"))
    ident, IDB = cload("id_sb", [128, 128], BF16, id_d[:])
    ones, ONES = cload("ones_sb", [128, 128], F32, ones_d[:])
    cw, CW = cload("cw_sb", [128, 3, 5], F32, cw_d[:])
    cb, CBB = cload("cb_sb", [128, 3], F32, cb_d[:])
    dsk, DSK = cload("dsk_sb", [128, 1], F32, dsk_d[:])
    dt_all, DT = cload("dt_all", [128, NQ], F32, dtr[:])
    dtb_sb, DTBB = cload("dtb_sb", [128, NQ], F32, dtb[:])
    a_all, AA = cload("a_all", [128, NQ], F32, alg[:])
    TU, TL, XF, XB = (tri[:, k, :] for k in range(4))

    ps = kb.pool("ps", [128, 512], F32, 6, space="psum")
    pst = kb.pool("pst", [128, 256], BF16, 2, space="psum")

    kb.op("dve", "tensor_tensor", out=dt_all[:], in0=dt_all[:], in1=dtb_sb[:], op=ALU.add, reads=[DT, DTBB], writes=[DT])
    kb.op("act", "activation", out=dt_all[:], in_=dt_all[:], func=AF.Exp, reads=[DT], writes=[DT])
    kb.op("act", "activation", out=dt_all[:], in_=dt_all[:], func=AF.Ln, bias=1.0, reads=[DT], writes=[DT])
    kb.op("act", "activation", out=a_all[:], in_=a_all[:], func=AF.Exp, reads=[AA], writes=[AA])
    kb.op("dve", "scalar_tensor_tensor", out=a_all[:], in0=a_all[:], scalar=-1.0, in1=dt_all[:], op0=ALU.mult, op1=ALU.mult,
          reads=[AA, DT], writes=[AA])
    cumF = kb.sbuf("cumF", [128, NQ], F32)
    cumB = kb.sbuf("cumB", [128, NQ], F32)
    dec = kb.sbuf("dec", [128, NQ], F32)
    w_all = kb.sbuf("w_all", [128, NQ], F32)
    TB = Buf("tables")
    for q0 in range(0, NQ, 512):
        qn = min(512, NQ - q0)
        pF, pFb = ps()
        kb.op("pe", "matmul", out=pF[:, 0:qn], lhsT=TU, rhs=a_all[:, q0:q0 + qn], start=True, stop=True, reads=[TRI, AA], writes=[pFb])
        pB, pBb = ps()
        kb.op("pe", "matmul", out=pB[:, 0:qn], lhsT=TL, rhs=a_all[:, q0:q0 + qn], start=True, stop=True, reads=[TRI, AA], writes=[pBb])
        pT, pTb = ps()
        kb.op("pe", "matmul", out=pT[:, 0:qn], lhsT=ones[:], rhs=a_all[:, q0:q0 + qn], start=True, stop=True, reads=[ONES, AA], writes=[pTb])
        kb.op("act", "activation", out=dec[:, q0:q0 + qn], in_=pT[:, 0:qn], func=AF.Copy, reads=[pTb], writes=[TB])
        kb.op("dve", "tensor_tensor", out=cumF[:, q0:q0 + qn], in0=dec[:, q0:q0 + qn], in1=pF[:, 0:qn], op=ALU.subtract,
              reads=[TB, pFb], writes=[TB])
        kb.op("dve", "tensor_tensor", out=cumB[:, q0:q0 + qn], in0=dec[:, q0:q0 + qn], in1=pB[:, 0:qn], op=ALU.subtract,
              reads=[TB, pBb], writes=[TB])
    kb.op("act", "activation", out=cumF[:], in_=cumF[:], func=AF.Exp, reads=[TB], writes=[TB])
    kb.op("act", "activation", out=cumB[:], in_=cumB[:], func=AF.Exp, reads=[TB], writes=[TB])
    kb.op("act", "activation", out=dec[:], in_=dec[:], func=AF.Exp, reads=[TB], writes=[TB])
    v4 = lambda t: t[:].rearrange("p (c k) -> p c k", k=4)
    kb.op("dve", "tensor_tensor", out=v4(w_all)[:, :, 0:2], in0=v4(dt_all)[:, :, 0:2], in1=v4(cumF)[:, :, 0:2], op=ALU.mult,
          reads=[DT, TB], writes=[TB])
    kb.op("dve", "tensor_tensor", out=v4(w_all)[:, :, 2:4], in0=v4(dt_all)[:, :, 2:4], in1=v4(cumB)[:, :, 2:4], op=ALU.mult,
          reads=[DT, TB], writes=[TB])

    xbc = kb.sbuf("xbc", [128, 3, TK], BF16)
    XBC = Buf("xbc")
    SEG = min(2048, S)
    raw = kb.pool("raw", [128, SEG + 4], F32, 2)
    acc = kb.pool("cacc", [128, SEG], F32, 2)
    segs = [(0, CTX, 0, CTX)] + [(CTX + a, SEG, CTX, TK) for a in range(0, S, SEG)]
    ei = 0
    for (a0, n, s0, s1) in segs:
        for k3 in range(3):
            r_t, r_b = raw()
            lo = max(a0 - 2, s0)
            hi = min(a0 + n + 2, s1)
            eng = "dve"
            kb.op("pool", "memset", _args=(r_t[:, 0:n + 4], 0.0), writes=[r_b])
            kb.dma("sp", r_t[:, lo - (a0 - 2):hi - (a0 - 2)], xr[k3, :, lo:hi], writes=[r_b])
            c_t, c_b = acc()
            kb.op(eng, "tensor_scalar", out=c_t[:, 0:n], in0=r_t[:, 2:2 + n], scalar1=cw[:, k3, 2:3], scalar2=cb[:, k3:k3 + 1],
                  op0=ALU.mult, op1=ALU.add, reads=[r_b, CW, CBB], writes=[c_b])
            for k in (0, 1, 3, 4):
                kb.op(eng, "scalar_tensor_tensor", out=c_t[:, 0:n], in0=r_t[:, k:k + n], scalar=cw[:, k3, k:k + 1], in1=c_t[:, 0:n],
                      op0=ALU.mult, op1=ALU.add, reads=[r_b, CW, c_b], writes=[c_b])
            kb.op("act", "activation", out=xbc[:, k3, a0:a0 + n], in_=c_t[:, 0:n], func=AF.Silu, reads=[c_b], writes=[XBC])

    H = kb.sbuf("H", [128, 128], F32)
    HB = Buf("H")
    Hbf = kb.sbuf("Hbf", [128, 128], BF16)
    HBF = Buf("Hbf")
    btm = kb.pool("btm", [128, 128], BF16, 2)
    cbm = kb.pool("cbm", [128, 128], F32, 2)
    xdp = kb.pool("xd", [128, 64], BF16, 4)
    xddp = kb.pool("xdd", [128, 64], BF16, 4)
    yp = kb.pool("Y", [128, 128], F32, 3)
    ep = kb.pool("E", [128, 128], F32, 3)
    erp = kb.pool("Er", [128, 128], F32, 3)
    mtp = kb.pool("MT", [128, 128], BF16, 3)
    csp = kb.pool("Cs", [128, 128], BF16, 3)
    yfo = kb.pool("yfo", [128, 128], F32, 3)
    yfi = kb.pool("yfi", [128, 128], F32, 3)
    zin = kb.pool("zin", [128, 128], F32, 3)
    ygo = kb.pool("ygo", [128, 128], F32, 3)
    YF = [Buf("yf%d" % c) for c in range(NCH)]

    def step(c, d, H, HB, Hbf, HBF):
        Td, Xd = (TU, XF) if d == 0 else (TL, XB)
        cs = slice(c * 128, (c + 1) * 128)
        px, pxb = pst()
        kb.op("pe", "transpose", out=px[:, 0:128], in_=xbc[:, 0, cs], identity=ident[:], reads=[XBC, IDB], writes=[pxb], signal=False)
        kb.op("pe", "transpose", out=px[:, 128:256], in_=xbc[:, 1, cs], identity=ident[:], reads=[XBC, IDB], writes=[pxb])
        b_t, b_b = btm()
        kb.op("act", "activation", out=b_t[:], in_=px[:, 128:256], func=AF.Copy, reads=[pxb], writes=[b_b])
        pc, pcb = ps()
        kb.op("pe", "matmul", out=pc[:, 0:128], lhsT=xbc[:, 1, cs], rhs=xbc[:, 2, cs], start=True, stop=True, reads=[XBC], writes=[pcb])
        m_t, m_b = cbm()
        kb.op("dve", "tensor_tensor", out=m_t[:], in0=pc[:, 0:128], in1=Td, op=ALU.mult, reads=[pcb, TRI], writes=[m_b])
        py, pyb = ps()
        pss, pssb = ps()
        cols = [c * 4 + d * 2 + h for h in range(2)]
        hss = [slice(h * 64, (h + 1) * 64) for h in range(2)]
        Ys, PGs, PRs, XD, XDD, Es, ERs, MTs, CSs = [], [], [], [], [], [], [], [], []
        for h in range(2):
            y_t, y_b = yp()
            kb.op("dve", "tensor_scalar", out=y_t[:], in0=Td, scalar1=a_all[:, cols[h]:cols[h] + 1], scalar2=None, op0=ALU.mult,
                  reads=[TRI, AA], writes=[y_b])
            Ys.append((y_t, y_b))
        for h in range(2):
            y_t, y_b = Ys[h]
            pg, pgb = ps()
            kb.op("pe", "matmul", out=pg[:, 0:128], lhsT=Xd, rhs=y_t[:], start=True, stop=True, reads=[TRI, y_b], writes=[pgb])
            pr, prb = ps()
            kb.op("pe", "matmul", out=pr[:, 0:128], lhsT=ones[:], rhs=y_t[:], start=True, stop=True, reads=[ONES, y_b], writes=[prb])
            PGs.append((pg, pgb))
            PRs.append((pr, prb))
        for h in range(2):
            xd, xdb = xdp()
            kb.op("dve", "tensor_scalar", out=xd[:], in0=px[:, hss[h]], scalar1=dt_all[:, cols[h]:cols[h] + 1], scalar2=None, op0=ALU.mult,
                  reads=[pxb, DT], writes=[xdb])
            xdd, xddb = xddp()
            kb.op("dve", "tensor_scalar", out=xdd[:], in0=px[:, hss[h]], scalar1=w_all[:, cols[h]:cols[h] + 1], scalar2=None, op0=ALU.mult,
                  reads=[pxb, TB], writes=[xddb])
            XD.append((xd, xdb))
            XDD.append((xdd, xddb))
        for h in range(2):
            e_t, e_b = ep()
            kb.op("act", "activation", out=e_t[:], in_=PGs[h][0][:, 0:128], func=AF.Exp, reads=[PGs[h][1]], writes=[e_b])
            er_t, er_b = erp()
            kb.op("act", "activation", out=er_t[:], in_=PRs[h][0][:, 0:128], func=AF.Exp, reads=[PRs[h][1]], writes=[er_b])
            Es.append((e_t, e_b))
            ERs.append((er_t, er_b))
        for h in range(2):
            mt, mtb = mtp()
            kb.op("dve", "tensor_tensor", out=mt[:], in0=Es[h][0][:], in1=m_t[:], op=ALU.mult, reads=[Es[h][1], m_b], writes=[mtb])
            cs_t, cs_b = csp()
            kb.op("pool", "tensor_tensor", out=cs_t[:], in0=xbc[:, 2, cs], in1=ERs[h][0][:], op=ALU.mult, reads=[XBC, ERs[h][1]], writes=[cs_b])
            MTs.append((mt, mtb))
            CSs.append((cs_t, cs_b))
        for h in range(2):
            hs = hss[h]
            kb.op("pe", "matmul", out=pss[:, hs], lhsT=b_t[:], rhs=XDD[h][0][:], start=True, stop=True, reads=[b_b, XDD[h][1]], writes=[pssb],
                  signal=(h == 1))
        for h in range(2):
            hs = hss[h]
            kb.op("pe", "matmul", out=py[hs, 0:128], lhsT=XD[h][0][:], rhs=MTs[h][0][:], start=True, stop=False, reads=[XD[h][1], MTs[h][1]],
                  writes=[pyb], signal=False)
            kb.op("pe", "matmul", out=py[hs, 0:128], lhsT=Hbf[:, hs], rhs=CSs[h][0][:], start=False, stop=True, reads=[HBF, CSs[h][1]], writes=[pyb])
        for h in range(2):
            hs = hss[h]
            kb.op("dve", "scalar_tensor_tensor", out=H[:, hs], in0=H[:, hs], scalar=dec[:, cols[h]:cols[h] + 1], in1=pss[:, hs],
                  op0=ALU.mult, op1=ALU.add, reads=[HB, TB, pssb], writes=[HB])
        kb.op("act", "activation", out=Hbf[:], in_=H[:], func=AF.Copy, reads=[HB], writes=[HBF])
        return py, pyb

    H2 = kb.sbuf("H2", [128, 128], F32)
    HB2 = Buf("H2")
    Hbf2 = kb.sbuf("Hbf2", [128, 128], BF16)
    HBF2 = Buf("Hbf2")
    ybs = kb.dram("ybs", [128, TK], F32)
    YB = [Buf("yb%d" % c) for c in range(NCH)]
    for (t_, b_) in ((H, HB), (Hbf, HBF), (H2, HB2), (Hbf2, HBF2)):
        kb.op("dve", "memset", _args=(t_[:], 0.0), writes=[b_])
    order_b = list(range(CTX // 128 - 1, -1, -1)) + list(range(NCH - 1, CTX // 128 - 1, -1))
    for i in range(NCH):
        c = i
        py, pyb = step(c, 0, H, HB, Hbf, HBF)
        o_t, o_b = yfo()
        kb.op("act", "activation", out=o_t[:], in_=py[:, 0:128], func=AF.Copy, reads=[pyb], writes=[o_b])
        kb.dma("sp", yfs[:, c * 128:(c + 1) * 128], o_t[:], reads=[o_b], writes=[YF[c]], sembuf=o_b)
        c = order_b[i]
        py, pyb = step(c, 1, H2, HB2, Hbf2, HBF2)
        o_t, o_b = yfi()
        kb.op("act", "activation", out=o_t[:], in_=py[:, 0:128], func=AF.Copy, reads=[pyb], writes=[o_b])
        kb.dma("sp", ybs[:, c * 128:(c + 1) * 128], o_t[:], reads=[o_b], writes=[YB[c]], sembuf=o_b)
    ea = kb.pool("ea", [128, 512], F32, 2)
    eb = kb.pool("eb", [128, 512], F32, 2)
    ez = kb.pool("ez", [128, 512], F32, 2)
    eo = kb.pool("eo", [128, 512], F32, 2)
    for c0 in range(0, NCH, 4):
        nb = min(4, NCH - c0)
        w = nb * 128
        cs = slice(c0 * 128, c0 * 128 + w)
        a_t, a_b = ea()
        kb.dma("sp", a_t[:, 0:w], yfs[:, cs], reads=[YF[c] for c in range(c0, c0 + nb)], writes=[a_b], sembuf=a_b)
        b_t, b_b = eb()
        kb.dma("sp", b_t[:, 0:w], ybs[:, cs], reads=[YB[c] for c in range(c0, c0 + nb)], writes=[b_b], sembuf=b_b)
        z_t, z_b = ez()
        kb.dma("sp", z_t[:, 0:w], zr[:, cs], writes=[z_b])
        kb.op("dve", "tensor_tensor", out=a_t[:, 0:w], in0=a_t[:, 0:w], in1=b_t[:, 0:w], op=ALU.add, reads=[a_b, b_b], writes=[a_b])
        kb.op("dve", "scalar_tensor_tensor", out=a_t[:, 0:w], in0=xbc[:, 0, cs], scalar=dsk[:, 0:1], in1=a_t[:, 0:w], op0=ALU.mult, op1=ALU.add,
              reads=[XBC, DSK, a_b], writes=[a_b])
        kb.op("act", "activation", out=z_t[:, 0:w], in_=z_t[:, 0:w], func=AF.Silu, reads=[z_b], writes=[z_b])
        g_t, g_b = eo()
        kb.op("pool", "tensor_tensor", out=g_t[:, 0:w], in0=a_t[:, 0:w], in1=z_t[:, 0:w], op=ALU.mult, reads=[a_b, z_b], writes=[g_b])
        kb.store("sp", ygT[:, cs], g_t[:, 0:w], g_b)
    return kb.build()


def lc_consts():
    i = np.arange(128)
    TU = (i[:, None] <= i[None, :]).astype(np.float32)
    TL = (i[:, None] >= i[None, :]).astype(np.float32)
    XF = (i[None, :] < i[:, None]).astype(np.float32)
    XB = (i[:, None] < i[None, :]).astype(np.float32)
    return {"tri": np.stack([TU, TL, XF, XB]), "ident": np.eye(128, dtype=np.float32).astype(NPBF),
            "ones": np.ones((128, 128), np.float32)}


def run_lc(la_res, S, inp, l):
    Tl = S // NCORE
    TK = S + CTX
    NCH = TK // 128
    nc = build_lc(S)
    misc = gather_fm(la_res, "miscT", Tl)
    consts = lc_consts()
    in_maps = []
    for i in range(NCORE):
        g = i // 4
        ch = slice(i * 128, (i + 1) * 128)
        xrow = misc[M_XBC + i * 128:M_XBC + (i + 1) * 128]
        brow = misc[M_XBC + 1024 + g * 128:M_XBC + 1024 + (g + 1) * 128]
        crow = misc[M_XBC + 1280 + g * 128:M_XBC + 1280 + (g + 1) * 128]
        hsel = [2 * i, 2 * i + 1, 16 + 2 * i, 16 + 2 * i + 1]
        dt4 = misc[M_DT:M_DT + 32][hsel]
        dtr = np.ascontiguousarray(dt4.reshape(4, NCH, 128).transpose(2, 1, 0).reshape(128, NCH * 4))
        v4 = lambda a: np.ascontiguousarray(np.broadcast_to(np.tile(a.reshape(-1)[hsel], NCH)[None, :], (128, NCH * 4))).astype(np.float32)
        cwl = inp["conv_w"][l]
        cwp = np.stack([cwl[:, i * 128:(i + 1) * 128].T, cwl[:, 1024 + g * 128:1024 + (g + 1) * 128].T,
                        cwl[:, 1280 + g * 128:1280 + (g + 1) * 128].T], axis=1)
        cbl = inp["conv_b"][l]
        cbp = np.stack([cbl[i * 128:(i + 1) * 128], cbl[1024 + g * 128:1024 + (g + 1) * 128],
                        cbl[1280 + g * 128:1280 + (g + 1) * 128]], axis=1)
        m = {"xr": np.ascontiguousarray(np.stack([xrow, brow, crow])), "zr": np.ascontiguousarray(misc[M_Z + i * 128:M_Z + (i + 1) * 128]),
             "dtr": dtr, "dtb": v4(inp["dt_bias"][l]), "alg": v4(inp["a_log"][l]),
             "cw": np.ascontiguousarray(cwp).astype(np.float32), "cb": np.ascontiguousarray(cbp).astype(np.float32),
             "dsk": np.ascontiguousarray(np.repeat(inp["d_skip"][l][2 * i:2 * i + 2], 64).reshape(128, 1)).astype(np.float32)}
        m.update(consts)
        in_maps.append(m)
    res = run(nc, in_maps)
    yg = np.concatenate([r["ygT"] for r in res], axis=0)
    return np.ascontiguousarray(np.concatenate([yg[:, CTX:], yg[:, :CTX]], axis=1))


def kernel(**inp):
    import sys
    import time
    t00 = time.time()

    def log(msg):
        print("[kernel] %7.1fs %s" % (time.time() - t00, msg), file=sys.stderr, flush=True)
    inp = {k: np.asarray(v) for k, v in inp.items()}
    x = inp["x"]
    S = x.shape[1]
    L = inp["w_in"].shape[0]
    mods, wbf = run_l0(inp, L)
    log("L0 done")
    sT_lat = np.ascontiguousarray(x[0].T)
    sT_ctx = np.ascontiguousarray(inp["ctx"][0].T)
    for l in range(L):
        last = (l == L - 1)
        la = run_la(sT_lat, sT_ctx, mods[l], inp, wbf, l)
        log("LA%d done" % l)
        oaT, ocT = run_lb(la, S, inp, l, with_ctx=not last)
        log("LB%d done" % l)
        ygT = run_lc(la, S, inp, l)
        log("LC%d done" % l)
        del la
        sT_lat, sT_ctx = run_ld(sT_lat, sT_ctx, oaT, ygT, ocT, mods[l], inp, wbf, l, last)
        log("LD%d done" % l)
    return np.ascontiguousarray(sT_lat.T)[None].astype(np.float32)
```
